# Optimizing a Trainium2 kernel written in Bass

```python
import math
import jax, jax.numpy as jnp
from jax import lax
import numpy as np

D_MODEL = 1024
BATCH = 4
SEQ = 4096
DEPTH = 1

CHUNK = 64
N_MEM = 256
D_CONV = D_MODEL
CONV_K = 31
D_SSM = D_MODEL // 2
SSM_GROUP = 16
SSM_GROUPS = D_SSM // SSM_GROUP
SSM_STATE = 64
XA_HEADS = 4
XA_HEAD_DIM = D_MODEL // XA_HEADS
D_FF = 4 * D_MODEL
D_IN = 2 * D_CONV + D_SSM + 2 * D_MODEL
LN_EPS = 1e-5
DEEPNORM_ALPHA = (2.0 * DEPTH) ** 0.25
DEEPNORM_BETA = (8.0 * DEPTH) ** -0.25

kernel_name = "gated_conformer_s5_memxattn_deepnorm"


def layer_norm(x, g, b):
    xf = x.astype(jnp.float32)
    mu = jnp.mean(xf, axis=-1, keepdims=True)
    xc = xf - mu
    var = jnp.mean(xc * xc, axis=-1, keepdims=True)
    y = xc * lax.rsqrt(var + LN_EPS) * g.astype(jnp.float32) + b.astype(jnp.float32)
    return y.astype(x.dtype)


def conformer_conv_branch(val, gate, dw, db, ng, nb, w_out):
    u = val * jax.nn.sigmoid(gate)
    c = lax.conv_general_dilated(
        u, dw[:, None, :].astype(u.dtype),
        window_strides=(1,), padding=[(CONV_K - 1, 0)],
        dimension_numbers=("NWC", "WIO", "NWC"),
        feature_group_count=D_CONV) + db
    c = layer_norm(c, ng, nb)
    c = jax.nn.silu(c)
    return c @ w_out


def _ssm_combine(left, right):
    a1r, a1i, b1r, b1i = left
    a2r, a2i, b2r, b2i = right
    ar = a2r * a1r - a2i * a1i
    ai = a2r * a1i + a2i * a1r
    br = a2r * b1r - a2i * b1i + b2r
    bi = a2r * b1i + a2i * b1r + b2i
    return (ar, ai, br, bi)


def s5_branch(u, log_step, lam_re, lam_im, b_re, b_im, c_re, c_im, d, w_glu):
    bsz, s, _ = u.shape
    uf = u.astype(jnp.float32).reshape(bsz, s, SSM_GROUPS, SSM_GROUP)
    step = jnp.exp(log_step.astype(jnp.float32))[:, None]
    lr = lam_re.astype(jnp.float32)
    li = lam_im.astype(jnp.float32)
    mag = jnp.exp(lr * step)
    ar = mag * jnp.cos(li * step)
    ai = mag * jnp.sin(li * step)
    den = lr * lr + li * li
    nr = ar - 1.0
    cr = (nr * lr + ai * li) / den
    ci = (ai * lr - nr * li) / den
    br = b_re.astype(jnp.float32)
    bi = b_im.astype(jnp.float32)
    bbr = cr[..., None] * br - ci[..., None] * bi
    bbi = cr[..., None] * bi + ci[..., None] * br
    bu_r = jnp.einsum('bsgh,gph->bsgp', uf, bbr)
    bu_i = jnp.einsum('bsgh,gph->bsgp', uf, bbi)
    a_r = jnp.broadcast_to(ar, bu_r.shape)
    a_i = jnp.broadcast_to(ai, bu_i.shape)
    _, _, xr, xi = lax.associative_scan(_ssm_combine, (a_r, a_i, bu_r, bu_i), axis=1)
    y = (jnp.einsum('bsgp,ghp->bsgh', xr, c_re.astype(jnp.float32))
         - jnp.einsum('bsgp,ghp->bsgh', xi, c_im.astype(jnp.float32))
         + d.astype(jnp.float32).reshape(SSM_GROUPS, SSM_GROUP) * uf)
    y = y.reshape(bsz, s, D_SSM).astype(u.dtype)
    z = y @ w_glu
    return z[..., :D_MODEL] * jax.nn.sigmoid(z[..., D_MODEL:])


def hybrid_mixer(h, w_in, conv_dw, conv_db, conv_norm_g, conv_norm_b, w_conv_out,
                 log_step, lam_re, lam_im, b_re, b_im, c_re, c_im, d, w_ssm_glu, w_mix_out):
    p = h @ w_in
    o0 = D_CONV
    o1 = 2 * D_CONV
    o2 = o1 + D_SSM
    o3 = o2 + D_MODEL
    conv_val, conv_gate = p[..., :o0], p[..., o0:o1]
    ssm_in = p[..., o1:o2]
    gate_a, gate_b = p[..., o2:o3], p[..., o3:]
    y_a = conformer_conv_branch(conv_val, conv_gate, conv_dw, conv_db,
                                conv_norm_g, conv_norm_b, w_conv_out)
    y_b = s5_branch(ssm_in, log_step, lam_re, lam_im, b_re, b_im, c_re, c_im, d, w_ssm_glu)
    merged = jax.nn.sigmoid(gate_a) * y_a + jax.nn.sigmoid(gate_b) * y_b
    return merged @ w_mix_out


def memory_cross_attention(h, mem, wq, wkv, wo):
    bsz, s, _ = h.shape
    q = (h @ wq).reshape(bsz, s, XA_HEADS, XA_HEAD_DIM)
    kv = mem @ wkv
    k = kv[..., :D_MODEL].reshape(bsz, N_MEM, XA_HEADS, XA_HEAD_DIM)
    v = kv[..., D_MODEL:].reshape(bsz, N_MEM, XA_HEADS, XA_HEAD_DIM)
    scores = jnp.einsum('bshd,bmhd->bhsm', q.astype(jnp.float32), k.astype(jnp.float32))
    probs = jax.nn.softmax(scores * (XA_HEAD_DIM ** -0.5), axis=-1).astype(h.dtype)
    o = jnp.einsum('bhsm,bmhd->bshd', probs, v).reshape(bsz, s, D_MODEL)
    return o @ wo


def sq_relu_mlp(h, w_up, w_down):
    z = jax.nn.relu(h @ w_up)
    return (z * z) @ w_down


def setup_inputs(seed: int = 0) -> dict:
    key = jax.random.key(seed)
    ks = jax.random.split(key, 32)
    f32 = jnp.float32
    L = DEPTH
    nrm = lambda k, shape, scale: jax.random.normal(k, shape, f32) * scale
    gain = lambda k, shape: 1.0 + 0.02 * jax.random.normal(k, shape, f32)
    bias = lambda k, shape: 0.02 * jax.random.normal(k, shape, f32)
    n_idx = jnp.arange(SSM_STATE, dtype=f32)
    lam_re = -0.5 + 0.01 * jax.random.normal(ks[11], (L, SSM_GROUPS, SSM_STATE), f32)
    lam_im = math.pi * n_idx + 0.01 * jax.random.normal(ks[12], (L, SSM_GROUPS, SSM_STATE), f32)
    log_step = jax.random.uniform(ks[10], (L, SSM_GROUPS), f32,
                                  math.log(1e-3), math.log(1e-1))
    return {
        "x": jax.random.normal(ks[0], (BATCH, SEQ, D_MODEL), f32),
        "mem": jax.random.normal(ks[1], (BATCH, N_MEM, D_MODEL), f32),
        "in_norm_g": gain(ks[2], (D_MODEL,)),
        "in_norm_b": bias(ks[3], (D_MODEL,)),
        "w_in": nrm(ks[4], (L, D_MODEL, D_IN), D_MODEL ** -0.5),
        "conv_dw": nrm(ks[5], (L, CONV_K, D_CONV), CONV_K ** -0.5),
        "conv_db": bias(ks[6], (L, D_CONV)),
        "conv_norm_g": gain(ks[7], (L, D_CONV)),
        "conv_norm_b": bias(ks[8], (L, D_CONV)),
        "w_conv_out": nrm(ks[9], (L, D_CONV, D_MODEL), D_CONV ** -0.5),
        "ssm_log_step": log_step,
        "ssm_lambda_re": lam_re,
        "ssm_lambda_im": lam_im,
        "ssm_b_re": nrm(ks[13], (L, SSM_GROUPS, SSM_STATE, SSM_GROUP), (2.0 * SSM_GROUP) ** -0.5),
        "ssm_b_im": nrm(ks[14], (L, SSM_GROUPS, SSM_STATE, SSM_GROUP), (2.0 * SSM_GROUP) ** -0.5),
        "ssm_c_re": nrm(ks[15], (L, SSM_GROUPS, SSM_GROUP, SSM_STATE), (2.0 * SSM_STATE) ** -0.5),
        "ssm_c_im": nrm(ks[16], (L, SSM_GROUPS, SSM_GROUP, SSM_STATE), (2.0 * SSM_STATE) ** -0.5),
        "ssm_d": nrm(ks[17], (L, D_SSM), 1.0),
        "w_ssm_glu": nrm(ks[18], (L, D_SSM, 2 * D_MODEL), D_SSM ** -0.5),
        "w_mix_out": nrm(ks[19], (L, D_MODEL, D_MODEL), DEEPNORM_BETA * D_MODEL ** -0.5),
        "ln1_g": gain(ks[20], (L, D_MODEL)),
        "ln1_b": bias(ks[21], (L, D_MODEL)),
        "xa_wq": nrm(ks[22], (L, D_MODEL, D_MODEL), D_MODEL ** -0.5),
        "xa_wkv": nrm(ks[23], (L, D_MODEL, 2 * D_MODEL), D_MODEL ** -0.5),
        "xa_wo": nrm(ks[24], (L, D_MODEL, D_MODEL), DEEPNORM_BETA * D_MODEL ** -0.5),
        "ln2_g": gain(ks[25], (L, D_MODEL)),
        "ln2_b": bias(ks[26], (L, D_MODEL)),
        "mlp_w_up": nrm(ks[27], (L, D_MODEL, D_FF), D_MODEL ** -0.5),
        "mlp_w_down": nrm(ks[28], (L, D_FF, D_MODEL), DEEPNORM_BETA * D_FF ** -0.5),
        "ln3_g": gain(ks[29], (L, D_MODEL)),
        "ln3_b": bias(ks[30], (L, D_MODEL)),
    }


def reference(x, mem, in_norm_g, in_norm_b, w_in, conv_dw, conv_db, conv_norm_g, conv_norm_b,
              w_conv_out, ssm_log_step, ssm_lambda_re, ssm_lambda_im, ssm_b_re, ssm_b_im,
              ssm_c_re, ssm_c_im, ssm_d, w_ssm_glu, w_mix_out, ln1_g, ln1_b,
              xa_wq, xa_wkv, xa_wo, ln2_g, ln2_b, mlp_w_up, mlp_w_down, ln3_g, ln3_b):
    h = layer_norm(x, in_norm_g, in_norm_b)
    for l in range(DEPTH):
        mix = hybrid_mixer(h, w_in[l], conv_dw[l], conv_db[l], conv_norm_g[l], conv_norm_b[l],
                           w_conv_out[l], ssm_log_step[l], ssm_lambda_re[l], ssm_lambda_im[l],
                           ssm_b_re[l], ssm_b_im[l], ssm_c_re[l], ssm_c_im[l], ssm_d[l],
                           w_ssm_glu[l], w_mix_out[l])
        h = layer_norm(DEEPNORM_ALPHA * h + mix, ln1_g[l], ln1_b[l])
        xa = memory_cross_attention(h, mem, xa_wq[l], xa_wkv[l], xa_wo[l])
        h = layer_norm(DEEPNORM_ALPHA * h + xa, ln2_g[l], ln2_b[l])
        ff = sq_relu_mlp(h, mlp_w_up[l], mlp_w_down[l])
        h = layer_norm(DEEPNORM_ALPHA * h + ff, ln3_g[l], ln3_b[l])
    return h
```

```python
import math
from contextlib import ExitStack
import numpy as np
import ml_dtypes
import concourse.bass as bass
import concourse.mybir as mybir
from concourse.bass_utils import run_bass_kernel_spmd

F32 = mybir.dt.float32
BF16 = mybir.dt.bfloat16
I32 = mybir.dt.int32
ALU = mybir.AluOpType
AF = mybir.ActivationFunctionType
AX = mybir.AxisListType

ENGS = ("pe", "act", "dve", "pool", "sp")
D = 1024
NT = 2048
NA = 4096
TT = 512
G = 32
ALPHA = 2.0 ** 0.25
EPS = 1e-5
DBG_BAR = False
SCAN_SINGLE = True


class Prog:
    def __init__(self, nc, stack, n_lanes=8):
        self.nc = nc
        self.ops = {e: [] for e in ENGS}
        self.cnt = {}
        self.sem = {}
        for e in ENGS:
            self.sem[e] = stack.enter_context(nc.semaphore("clk_" + e))
            self.cnt[e] = 0
        self.lanes = {}
        for q in ("sp", "pool", "act"):
            self.lanes[q] = []
            for i in range(n_lanes):
                k = "dma_%s_%d" % (q, i)
                self.sem[k] = stack.enter_context(nc.semaphore(k))
                self.cnt[k] = 0
                self.lanes[q].append(k)
        self.lane_rr = {q: 0 for q in self.lanes}
        self.waited = {e: {} for e in ENGS}
        self.last_w = {}
        self.readers = {}
        self.out_events = []

    def _deps(self, reads, writes):
        deps = {}

        def add(ev):
            if ev is None:
                return
            s, v = ev
            if deps.get(s, 0) < v:
                deps[s] = v

        for k in reads:
            add(self.last_w.get(k))
            if isinstance(k, tuple) and k and k[0] == "ps":
                for s_, v_ in self.readers.get(k, {}).items():
                    add((s_, v_))
        for k in writes:
            add(self.last_w.get(k))
            for s, v in self.readers.get(k, {}).items():
                add((s, v))
        return deps

    def _emit_waits(self, eng, deps):
        for s, v in deps.items():
            if s == eng and eng == "pe":
                continue
            if self.waited[eng].get(s, 0) >= v:
                continue
            self.waited[eng][s] = v
            sem = self.sem[s]
            self.ops[eng].append(lambda E, sem=sem, v=v: E.wait_ge(sem, v))

    def _record(self, ev, reads, writes):
        for k in reads:
            d = self.readers.setdefault(k, {})
            if d.get(ev[0], 0) < ev[1]:
                d[ev[0]] = ev[1]
        for k in writes:
            self.last_w[k] = ev
            self.readers[k] = {}

    def op(self, eng, fn, reads=(), writes=(), inc=True):
        deps = self._deps(reads, writes)
        self._emit_waits(eng, deps)
        if inc:
            self.cnt[eng] += 1
            ev = (eng, self.cnt[eng])
            sem = self.sem[eng]
            self.ops[eng].append(lambda E, fn=fn, sem=sem: fn(E).then_inc(sem, 1))
        else:
            ev = (eng, self.cnt[eng] + 1)
            self.ops[eng].append(lambda E, fn=fn: fn(E))
        self._record(ev, reads, writes)
        return ev

    def dma(self, q, out, in_, reads=(), writes=(), is_output=False, **kw):
        deps = self._deps(reads, writes)
        self._emit_waits(q, deps)
        lanes = self.lanes[q]
        lane = lanes[self.lane_rr[q] % len(lanes)]
        self.lane_rr[q] += 1
        if self.cnt[lane] > 0:
            self._emit_waits(q, {lane: self.cnt[lane]})
        self.cnt[lane] += 16
        ev = (lane, self.cnt[lane])
        sem = self.sem[lane]
        self.ops[q].append(
            lambda E, out=out, in_=in_, sem=sem, kw=kw: E.dma_start(out=out, in_=in_, **kw).then_inc(sem, 16)
        )
        self._record(ev, reads, writes)
        if is_output:
            self.out_events.append(ev)
        return ev

    def barrier(self):
        for e in ENGS:
            deps = {s: c for s, c in self.cnt.items() if c > 0 and s != e}
            self._emit_waits(e, deps)

    def finish(self):
        self.barrier()

    def run(self, block):
        ops = self.ops

        @block.tensor
        def _(E):
            for f in ops["pe"]:
                f(E)

        @block.scalar
        def _(E):
            for f in ops["act"]:
                f(E)

        @block.vector
        def _(E):
            for f in ops["dve"]:
                f(E)

        @block.gpsimd
        def _(E):
            for f in ops["pool"]:
                f(E)

        @block.sync
        def _(E):
            for f in ops["sp"]:
                f(E)


def V(t, off, dims):
    a = t if (hasattr(t, "ap") and hasattr(t, "offset")) else t[:]
    return bass.AP(a.tensor, a.offset + off, [list(a.ap[0])] + [list(d) for d in dims])


class Arena:
    def __init__(self, tile, nbytes):
        self.t, self.n, self.off = tile, nbytes, 0

    def reset(self):
        self.off = 0

    def alloc(self, shape, dt):
        esz = 2 if dt == BF16 else 4
        n = 1
        for d in shape[1:]:
            n *= d
        self.off = (self.off + 3) // 4 * 4
        base = self.t[:].bitcast(dt)
        e0 = self.off // esz
        a = base[:, e0:e0 + n]
        self.off += n * esz
        assert self.off <= self.n, (self.off, self.n)
        if len(shape) == 3:
            a = a.rearrange("p (a b) -> p a b", a=shape[1])
        elif len(shape) == 4:
            a = a.rearrange("p (a b c) -> p a b c", a=shape[1], b=shape[2])
        return a


def build():
    nc = bass.Bass("TRN2", target_bir_lowering=False)
    dr = {}

    def din(name, shape, dt=F32):
        dr[name] = nc.dram_tensor(name, list(shape), dt, kind="ExternalInput").ap()
        return dr[name]

    xT = din("xT", [D, NA])
    maskv = din("maskv", [128, 1])
    memT = din("memT", [D, 256])
    w_in = din("w_in", [D, 4608])
    w_conv_out = din("w_conv_out", [D, D])
    w_glu = din("w_glu", [512, 2048])
    w_mix = din("w_mix", [D, D])
    wq = din("wq", [D, D])
    wkv = din("wkv", [D, 2048])
    wo = din("wo", [D, D])
    w_up = din("w_up", [D, 4096])
    w_down = din("w_down", [4096, D])
    vecs = din("vecs", [128, 12, 8])
    dwT = din("dwT", [128, 8, 31])
    lam = din("lam", [128, 3, G])
    bc_in = din("bc_in", [128, 4, G, 16])
    dvec = din("dvec", [128, 4])
    consts = din("consts", [128, 3, 128])
    cmask = din("cmask", [128, 128])
    selP_d = din("selP", [128, 64, 128])
    selU_d = din("selU", [128, 64, 128])
    rowc_d = din("rowc", [128, 64])
    jmat_d = din("jmat", [128, 128])
    outT = nc.dram_tensor("outT", [D, NT], F32, kind="ExternalOutput").ap()
    h0s = nc.dram_tensor("h0s", [D, NT], F32, kind="Internal").ap()

    with ExitStack() as st:
        P = Prog(nc, st)
        cnt = [0]

        def sb(shape, dt, stack=st, name=None):
            cnt[0] += 1
            return stack.enter_context(nc.sbuf_tensor(name or ("t%d" % cnt[0]), list(shape), dt))

        banks = [st.enter_context(nc.psum_tensor("ps%d" % i, [128, 512], F32)) for i in range(8)]
        bank_i = {"g": 0, "ln": 0}
        bank_pool = {"g": list(range(8)), "ln": None}

        def bank(pool="g"):
            if pool == "ln" and bank_pool["ln"] is None:
                pool = "g"
            lst = bank_pool[pool]
            i = lst[bank_i[pool] % len(lst)]
            bank_i[pool] += 1
            return banks[i], ("ps", i)

        def split_banks(on):
            bank_pool["g"] = [0, 1, 2, 3] if on else list(range(8))
            bank_pool["ln"] = [4, 5, 6, 7] if on else None

        cst = sb([128, 3, 128], BF16)
        P.dma("pool", cst[:], consts, writes=["cst"])
        onesM, ones1, ident = cst[:, 0, :], cst[:, 1, :], cst[:, 2, :]
        vec = sb([128, 12, 8], F32)
        P.dma("sp", vec[:], vecs, writes=["vec"])
        mk = sb([128, 1], F32)
        P.dma("sp", mk[:], maskv, writes=["mk"])
        dv = sb([128, 4], F32)
        P.dma("sp", dv[:], dvec, writes=["dv"])
        epst = sb([128, 1], F32)
        vecA = sb([128, 2, 8], F32)
        P.op("dve", lambda E: E.tensor_scalar(out=vecA[:], in0=vec[:, 6:8, :], scalar1=ALPHA, scalar2=None, op0=ALU.mult), reads=["vec"], writes=["vecA"])
        P.op("dve", lambda E: E.memset(epst[:], EPS), writes=["eps"])
        hh = sb([128, 8, 32], BF16)

        NW = 6
        wringT = sb([128, NW * 1024], BF16)
        wring = [V(wringT, i * 1024, [[1, 1024]]) for i in range(NW)]
        wr_i = [0]

        def wload(w_ap, r0, KC, c0, width=128):
            i = wr_i[0] % NW
            wr_i[0] += 1
            t = wring[i]
            key = ("w", i)
            src = w_ap[r0:r0 + KC * 128, c0:c0 + width].rearrange("(kc p) n -> p kc n", p=128)
            dst = V(t, 0, [[width, KC], [1, width]])
            P.dma("pool", dst, src, writes=[key])
            return (lambda kc, a, b: V(t, kc * width + a, [[1, b - a]])), key

        def mm_group(ps_ap, pskey, parts):
            n = len(parts)
            for i, (l, r, rk) in enumerate(parts):
                P.op("pe", lambda E, l=l, r=r, i=i: E.matmul(ps_ap, l, r, start=(i == 0), stop=(i == n - 1)),
                     reads=list(rk), writes=[pskey], inc=(i == n - 1))

        rR = sb([128, 8 * NT], F32)
        mgR = sb([128, 8 * NT], BF16)
        RA = Arena(rR, 8 * NT * 4)
        MA = Arena(mgR, 8 * NT * 2)
        r = rR[:].rearrange("p (c t) -> p c t", c=8)
        mg = mgR[:].rearrange("p (c t) -> p c t", c=8)
        xb = MA.alloc([128, 8, 512], BF16)
        sq = MA.alloc([128, 8, 512], BF16)
        m2 = MA.alloc([128, 512], F32)
        rs = MA.alloc([128, 512], F32)
        tmps = [MA.alloc([128, 512], F32) for _ in range(2)]

        LNS = dict(xb=xb, sq=sq, m2=m2, rs=rs, tmps=tmps, tag="A")

        def ln_front_a(src, srckey, n, xb_dma_src, S=None):
            S = S or LNS
            xb_, sq_, tg = S["xb"], S["sq"], S["tag"]
            xbk = S.get("xbkey", "lnxb" + tg)
            if xb_dma_src is not None:
                P.dma("pool", xb_[:, :, 0:n], xb_dma_src, reads=[srckey(c) for c in range(8)], writes=[xbk])
            for c in range(8):
                P.op("act", lambda E, c=c, sq_=sq_: E.activation(out=sq_[:, c, 0:n], in_=src(c), func=AF.Square),
                     reads=[srckey(c)], writes=[("lnsq" + tg, c)])
            b1, k1 = bank("ln")
            b2, k2 = bank("ln")
            mm_group(b1[:, 0:n], k1, [(onesM, xb_[:, c, 0:n], ["cst", xbk]) for c in range(8)])
            mm_group(b2[:, 0:n], k2, [(onesM, sq_[:, c, 0:n], ["cst", ("lnsq" + tg, c)]) for c in range(8)])
            return (b1, k1, b2, k2)

        def ln_front_b_steps(st_, n, S=None):
            S = S or LNS
            m2_, rs_, tg = S["m2"], S["rs"], S["tag"]
            km, kr = "lnm2" + tg, "lnrs" + tg
            b1, k1, b2, k2 = st_
            return [
                lambda: P.op("act", lambda E: E.activation(out=m2_[:, 0:n], in_=b1[:, 0:n], func=AF.Square), reads=[k1], writes=[km]),
                lambda: P.op("dve", lambda E: E.tensor_tensor(out=rs_[:, 0:n], in0=b2[:, 0:n], in1=m2_[:, 0:n], op=ALU.subtract),
                             reads=[k2, km], writes=[kr]),
                lambda: P.op("act", lambda E: E.activation(out=rs_[:, 0:n], in_=rs_[:, 0:n], func=AF.Ln, bias=epst[:, 0:1], scale=1.0),
                             reads=[kr, "eps"], writes=[kr]),
                lambda: P.op("act", lambda E: E.activation(out=rs_[:, 0:n], in_=rs_[:, 0:n], func=AF.Exp, scale=-0.5),
                             reads=[kr], writes=[kr]),
                lambda: P.op("dve", lambda E: E.tensor_copy(out=b2[:, 0:n], in_=rs_[:, 0:n]), reads=[kr], writes=[k2]),
                lambda: P.op("dve", lambda E: E.tensor_tensor(out=b1[:, 0:n], in0=b1[:, 0:n], in1=rs_[:, 0:n], op=ALU.mult),
                             reads=[k1, kr], writes=[k1]),
            ]

        def ln_front_b(st_, n, S=None):
            for f_ in ln_front_b_steps(st_, n, S):
                f_()

        def ln_back(st_, src, srckey, n, gi, out_f32=None, out_bf=None, okeys=(), bfkeys=None, inter=None, S=None, f32_affine=None):
            S = S or LNS
            fsc = (lambda c: vec[:, gi, c:c + 1]) if f32_affine is None else (lambda c: f32_affine[:, 0, c:c + 1])
            fbi = (lambda c: vec[:, gi + 1, c:c + 1]) if f32_affine is None else (lambda c: f32_affine[:, 1, c:c + 1])
            tmps_, tg = S["tmps"], S["tag"]
            b1, k1, b2, k2 = st_
            bfkeys = list(bfkeys)
            inter = list(inter) if inter else []
            for c in range(8):
                if c >= 1 and inter:
                    inter.pop(0)()
                tmp = tmps_[c % 2]
                tk = ("lntmp" + tg, c % 2)
                P.op("dve", lambda E, c=c, tmp=tmp: E.tensor_tensor(out=tmp[:, 0:n], in0=src(c), in1=b2[:, 0:n], op=ALU.mult),
                     reads=[srckey(c), k2], writes=[tk])
                P.op("dve", lambda E, c=c, tmp=tmp: E.tensor_tensor(out=tmp[:, 0:n], in0=tmp[:, 0:n], in1=b1[:, 0:n], op=ALU.subtract),
                     reads=[tk, k1], writes=[tk])
                if out_f32 is not None:
                    P.op("act", lambda E, c=c, tmp=tmp: E.activation(out=out_f32(c), in_=tmp[:, 0:n], func=AF.Identity,
                                                                     bias=fbi(c), scale=fsc(c)),
                         reads=[tk, "vec", "vecA"], writes=[okeys(c)])
                    if out_bf is not None:
                        if c % 2 == 0:
                            P.op("dve", lambda E, c=c, tmp=tmp: E.tensor_scalar(out=out_bf(c), in0=tmp[:, 0:n], scalar1=vec[:, gi, c:c + 1], scalar2=vec[:, gi + 1, c:c + 1],
                                                                                op0=ALU.mult, op1=ALU.add),
                                 reads=[tk, "vec"], writes=bfkeys)
                        else:
                            P.op("act", lambda E, c=c, tmp=tmp: E.activation(out=out_bf(c), in_=tmp[:, 0:n], func=AF.Identity,
                                                                             bias=vec[:, gi + 1, c:c + 1], scale=vec[:, gi, c:c + 1]),
                                 reads=[tk, "vec"], writes=bfkeys)
                else:
                    P.op("act", lambda E, c=c, tmp=tmp: E.activation(out=out_bf(c), in_=tmp[:, 0:n], func=AF.Identity,
                                                                     bias=vec[:, gi + 1, c:c + 1], scale=vec[:, gi, c:c + 1]),
                         reads=[tk, "vec"], writes=bfkeys)
            for f_ in inter:
                f_()

        with ExitStack() as sM:
            yT = sb([128, 4, NT], BF16, sM)
            with ExitStack() as s1:
                E8a = sb([128, G, 128], BF16, s1)
                E8b = sb([128, G, 128], BF16, s1)
                M0 = sb([128, G, 128], BF16, s1)
                FC = sb([128, G, 128], BF16, s1)
                R8 = sb([128, G], F32, s1)
                q8 = sb([128, G], F32, s1)
                FC2 = V(wringT, 0, [[128, G], [1, 128]])
                RA.reset()
                ra = RA.alloc
                lm = ra([128, 3, G], F32)
                P.dma("sp", lm, lam, writes=["lm"])
                bcin = ra([128, 4, G, 16], F32)
                P.dma("sp", bcin, bc_in, writes=["bcin"])
                cm = ra([128, 128], F32)
                P.dma("sp", cm, cmask, writes=["cm"])
                step = ra([128, G], F32); xx = ra([128, G], F32); th = ra([128, G], F32); q0 = ra([128, G], F32)
                NE = 16
                pw = ra([128, 2, NE, G], F32)
                ang = ra([128, 2, NE, G], F32)
                mag = ra([128, NE, G], F32)
                ti = ra([128, 2, NE, G], I32)
                tf = ra([128, 2, NE, G], F32)
                K = ["tbl"]
                KR = K + ["lm", "bcin"]
                P.op("act", lambda E: E.activation(out=step, in_=lm[:, 2, :], func=AF.Exp), reads=KR, writes=K)
                P.op("dve", lambda E: E.tensor_tensor(out=xx, in0=lm[:, 0, :], in1=step, op=ALU.mult), reads=KR, writes=K)
                P.op("dve", lambda E: E.tensor_tensor(out=th, in0=lm[:, 1, :], in1=step, op=ALU.mult), reads=KR, writes=K)
                INV2PI = 1.0 / (2.0 * math.pi)

                def reduce_(dst, src_ap, tiv, tfv):
                    P.op("dve", lambda E: E.tensor_copy(out=tiv, in_=src_ap), reads=K, writes=K)
                    P.op("dve", lambda E: E.tensor_copy(out=tfv, in_=tiv), reads=K, writes=K)
                    P.op("dve", lambda E: E.tensor_tensor(out=dst, in0=src_ap, in1=tfv, op=ALU.subtract), reads=K, writes=K)

                P.op("dve", lambda E: E.tensor_scalar(out=q0, in0=th, scalar1=INV2PI, scalar2=None, op0=ALU.mult), reads=K, writes=K)
                reduce_(q0, q0, ti[:, 0, 0, :], tf[:, 0, 0, :])
                for e in range(NE):
                    ex = float(e - 7)
                    P.op("dve", lambda E, e=e, ex=ex: E.tensor_scalar(out=ang[:, 1, e, :], in0=q0, scalar1=ex, scalar2=None, op0=ALU.mult), reads=K, writes=K)
                    P.op("dve", lambda E, e=e, ex=ex: E.tensor_scalar(out=ang[:, 0, e, :], in0=q0, scalar1=ex, scalar2=0.25, op0=ALU.mult, op1=ALU.add), reads=K, writes=K)
                    P.op("act", lambda E, e=e, ex=ex: E.activation(out=mag[:, e, :], in_=xx, func=AF.Exp, scale=ex), reads=K, writes=K)
                reduce_(ang, ang, ti, tf)
                P.op("act", lambda E: E.activation(out=pw, in_=ang, func=AF.Sin, scale=6.28318), reads=K, writes=K)
                for ri in range(2):
                    P.op("dve", lambda E, ri=ri: E.tensor_tensor(out=pw[:, ri], in0=pw[:, ri], in1=mag, op=ALU.mult), reads=K, writes=K)
                P.op("dve", lambda E: E.tensor_copy(out=R8[:], in_=mag[:, 15, :]), reads=K, writes=["A"])
                P.op("dve", lambda E: E.tensor_scalar(out=q8[:], in0=q0, scalar1=8.0, scalar2=None, op0=ALU.mult), reads=K, writes=["A"])
                P.op("dve", lambda E: E.tensor_copy(out=ti[:, 0, 0, :], in_=q8[:]), reads=K + ["A"], writes=K)
                P.op("dve", lambda E: E.tensor_copy(out=tf[:, 0, 0, :], in_=ti[:, 0, 0, :]), reads=K, writes=K)
                P.op("dve", lambda E: E.tensor_tensor(out=q8[:], in0=q8[:], in1=tf[:, 0, 0, :], op=ALU.subtract), reads=K + ["A"], writes=["A"])
                ar, ai = pw[:, 0, 8, :], pw[:, 1, 8, :]
                t1 = ra([128, G], F32); t2 = ra([128, G], F32); den = ra([128, G], F32)
                cr = ra([128, G], F32); ci = ra([128, G], F32); nr = ra([128, G], F32)
                lr, li = lm[:, 0, :], lm[:, 1, :]
                TT_ = lambda o, a, b, op: P.op("dve", lambda E: E.tensor_tensor(out=o, in0=a, in1=b, op=op), reads=KR, writes=K)
                P.op("dve", lambda E: E.tensor_scalar(out=nr, in0=ar, scalar1=-1.0, scalar2=None, op0=ALU.add), reads=K, writes=K)
                TT_(t1, lr, lr, ALU.mult); TT_(t2, li, li, ALU.mult); TT_(den, t1, t2, ALU.add)
                P.op("dve", lambda E: E.reciprocal(out=den, in_=den), reads=K, writes=K)
                TT_(t1, nr, lr, ALU.mult); TT_(t2, ai, li, ALU.mult); TT_(cr, t1, t2, ALU.add); TT_(cr, cr, den, ALU.mult)
                TT_(t1, ai, lr, ALU.mult); TT_(t2, nr, li, ALU.mult); TT_(ci, t1, t2, ALU.subtract); TT_(ci, ci, den, ALU.mult)
                bbr = ra([128, G, 16], F32); bbi = ra([128, G, 16], F32); tb = ra([128, G, 16], F32)
                bcg = lambda t: V(t, 0, [[1, G], [0, 16]])
                Br, Bi, Cr, Ci = (bcin[:, k] for k in range(4))
                TT_(bbr, Br, bcg(cr), ALU.mult); TT_(tb, Bi, bcg(ci), ALU.mult); TT_(bbr, bbr, tb, ALU.subtract)
                TT_(bbi, Bi, bcg(cr), ALU.mult); TT_(tb, Br, bcg(ci), ALU.mult); TT_(bbi, bbi, tb, ALU.add)
                X1 = ra([128, G, 16], F32); X2 = ra([128, G, 16], F32)
                Y1 = ra([128, G, 16], F32); Y2 = ra([128, G, 16], F32)
                Y3 = ra([128, G, 16], F32); Y4 = ra([128, G, 16], F32)
                cp = lambda o, a, s_: P.op("dve", lambda E: E.tensor_scalar(out=o, in0=a, scalar1=s_, scalar2=None, op0=ALU.mult), reads=KR, writes=K)
                cp(X1[0:64], bbr[0:64], 1.0); cp(X1[64:128], bbi[64:128], 1.0)
                cp(X2[0:64], bbi[0:64], -1.0); cp(X2[64:128], bbr[64:128], 1.0)
                cp(Y1[0:64], Cr[0:64], 1.0); cp(Y1[64:128], Ci[64:128], -1.0)
                cp(Y2[0:64], Ci[0:64], -1.0); cp(Y2[64:128], Cr[64:128], -1.0)
                cp(Y3[0:64], Ci[0:64], -1.0); cp(Y3[64:128], Cr[64:128], -1.0)
                cp(Y4[0:64], Cr[0:64], -1.0); cp(Y4[64:128], Ci[64:128], 1.0)
                EB = ra([128, G, 8, 16], BF16)
                FCp = ra([128, G, 8, 16], BF16)
                FCv = V(FC, 0, [[128, G], [16, 8], [1, 16]])
                FC2v = V(wringT, 0, [[128, G], [16, 8], [1, 16]])
                MT = Arena(mgR, 8 * NT * 2)
                ta = MT.alloc([128, G, 16], F32); tc = MT.alloc([128, G, 16], F32)

                def ctab(dst, e_idx, Xa, Xb):
                    prr = V(pw, (0 * NE + e_idx) * G, [[1, G], [0, 16]])
                    pii = V(pw, (1 * NE + e_idx) * G, [[1, G], [0, 16]])
                    TT_(ta, Xa, prr, ALU.mult); TT_(tc, Xb, pii, ALU.mult); TT_(dst, ta, tc, ALU.add)

                for j in range(8):
                    ctab(EB[:, :, j, :], (7 - j) + 7, X1, X2)
                    ctab(FCv[:, :, j, :], (j + 1) + 7, Y1, Y2)
                    ctab(FCp[:, :, j, :], (j - 7) + 7, Y1, Y2)
                    ctab(FC2v[:, :, j, :], (j + 1) + 7, Y3, Y4)
                for g4 in range(G // 4):
                    b, k = bank()
                    b2, k2 = bank()
                    for gi_ in range(4):
                        g = g4 * 4 + gi_
                        ebg = V(EB, g * 128, [[1, 128]])
                        P.op("pe", lambda E, b=b, gi_=gi_, ebg=ebg: E.matmul(b[:, gi_ * 128:(gi_ + 1) * 128], ebg, ident, start=True, stop=True),
                             reads=K + ["cst"], writes=[k], inc=(gi_ == 3))
                    for gi_ in range(4):
                        g = g4 * 4 + gi_
                        ebg = V(EB, g * 128, [[1, 128]])
                        fpg = V(FCp, g * 128, [[1, 128]])
                        P.op("pe", lambda E, b2=b2, gi_=gi_, ebg=ebg, fpg=fpg: E.matmul(b2[:, gi_ * 128:(gi_ + 1) * 128], ebg, fpg, start=True, stop=True),
                             reads=K, writes=[k2], inc=(gi_ == 3))
                    bv = b[:, :].rearrange("p (a c) -> p a c", a=4)
                    gs = slice(g4 * 4, g4 * 4 + 4)
                    P.op("act", lambda E, bv=bv, gs=gs: E.activation(out=E8a[:, gs, :], in_=bv, func=AF.Copy), reads=[k], writes=["E8a"])
                    P.op("act", lambda E, bv=bv, gs=gs: E.activation(out=E8b[:, gs, 0:64], in_=bv[:, :, 64:128], func=AF.Copy, scale=-1.0), reads=[k], writes=["E8b"])
                    P.op("act", lambda E, bv=bv, gs=gs: E.activation(out=E8b[:, gs, 64:128], in_=bv[:, :, 0:64], func=AF.Copy), reads=[k], writes=["E8b"])
                    P.op("dve", lambda E, b2=b2, gs=gs: E.tensor_tensor(out=M0[:, gs, :], in0=b2[:, :].rearrange("p (a c) -> p a c", a=4),
                                                                       in1=V(cm, 0, [[0, 4], [1, 128]]), op=ALU.mult), reads=[k2, "cm"], writes=["M0"])
                P.barrier()
                usT = sb([128, 4, NA], BF16, s1)
                rowc = sb([128, 64], F32, s1)
                carry = sb([128, G], F32, s1)
                P.dma("sp", rowc[:], rowc_d, writes=["rot"])
                sP = ExitStack()
                selP = sb([128, 64, 128], BF16, sP)
                P.dma("pool", selP[:], selP_d, writes=["selP"])
                RA.reset()
                MI = Arena(mgR, 8 * NT * 2)
                MI.off = 8 * 1024 * 3
                hbt = xb
                MI.off = 24 * 1024
                Wssm = MI.alloc([128, 8, 512], BF16)
                P.dma("pool", Wssm, w_in[:, 2048:2560].rearrange("(kc p) n -> p kc n", p=128), writes=["Wssm"])
                NXB = 3
                xt = [ra([128, 8, TT], F32) for _ in range(NXB)]
                def xload(it):
                    P.dma("sp", xt[it % NXB], xT[:, it * TT:(it + 1) * TT].rearrange("(c p) t -> p c t", p=128), writes=[("xt", it % NXB, c) for c in range(8)])

                def xsrc(it):
                    x_t = xt[it % NXB]
                    return (lambda c: x_t[:, c, :])

                NTI = NA // TT
                xdram = lambda it: xT[:, it * TT:(it + 1) * TT].rearrange("(c p) t -> p c t", p=128)
                xload(0)
                split_banks(True)
                xkf = lambda it: (lambda c: ("xt", it % NXB, c))
                xbs = [ra([128, 8, TT], BF16) for _ in range(2)]
                SIN_ = [dict(LNS, xb=xbs[i], xbkey="lnxbI%d" % i) for i in range(2)]

                def xbdma(it):
                    P.dma("pool", xbs[it % 2], xdram(it), writes=["lnxbI%d" % (it % 2)])

                xbdma(0)
                xbdma(1)
                xload(1)
                fr = {0: ln_front_a(xsrc(0), xkf(0), TT, None, SIN_[0])}
                ln_front_b(fr[0], TT)

                def ssm_in(it):
                    own_ = it >= 4
                    for m in range(4):
                        b, k = bank()
                        mm_group(b[:, :], k, [(Wssm[:, kc, m * 128:(m + 1) * 128], hbt[:, kc, :], ["Wssm", "hbt"]) for kc in range(8)])
                        dst = V(usT, m * NA + it * 64, [[512, 8], [1, 64]])
                        srcv = V(b, 0, [[1, 8], [8, 64]])
                        if own_:
                            P.op("dve", lambda E, dst=dst, srcv=srcv: E.tensor_copy(out=dst, in_=srcv),
                                 reads=[k], writes=["usT"])
                        else:
                            P.op("dve", lambda E, dst=dst, srcv=srcv: E.tensor_scalar(out=dst, in0=srcv, scalar1=mk[:, 0:1], scalar2=None, op0=ALU.mult),
                                 reads=[k, "mk"], writes=["usT"])

                for it in range(NTI):
                    x_t = xt[it % NXB]
                    if it + 2 < NTI:
                        xload(it + 2)
                    if it + 1 < NTI:
                        fr[it + 1] = ln_front_a(xsrc(it + 1), xkf(it + 1), TT, None, SIN_[(it + 1) % 2])
                        if it + 2 < NTI:
                            xbdma(it + 2)
                    own = it >= 4
                    o0 = (it - 4) * TT
                    if own:
                        ln_back(fr[it], xsrc(it), xkf(it), TT, 0, out_f32=xsrc(it), out_bf=lambda c: hbt[:, c, :], okeys=xkf(it), bfkeys=["hbt"],
                                inter=(ln_front_b_steps(fr[it + 1], TT) if it + 1 < NTI else None))
                        P.dma("act", h0s[:, o0:o0 + TT].rearrange("(c p) t -> p c t", p=128), x_t, reads=[xkf(it)(c) for c in range(8)], writes=["h0s"])
                    else:
                        ln_back(fr[it], xsrc(it), xkf(it), TT, 0, out_bf=lambda c: hbt[:, c, :], okeys=None, bfkeys=["hbt"],
                                inter=(ln_front_b_steps(fr[it + 1], TT) if it + 1 < NTI else None))
                        if it == 3:
                            P.op("act", lambda E: E.activation(out=hh[:], in_=hbt[:, :, TT - 32:TT], func=AF.Copy), reads=["hbt"], writes=["hh"])
                    ssm_in(it)
                split_banks(False)
                P.barrier()
                RA.reset()
                Up = ra([128, G, 512], BF16)
                Yp = ra([128, G, 256], BF16)
                NB = 64
                SAq = ra([128, G, NB], F32)
                SBq = ra([128, G, NB], F32)
                MB = Arena(mgR, 8 * NT * 2)
                Wq = MB.alloc([128, G, NB], F32)
                M8 = MB.alloc([128, G, NB], F32)
                Wsc = Wq
                cosT = MB.alloc([128, G, NB], F32)
                sinT = MB.alloc([128, G, NB], F32)
                tiq = SAq.bitcast(I32)
                KT_ = ["rot"]
                P.op("dve", lambda E: E.tensor_tensor(out=sinT, in0=V(q8, 0, [[1, G], [0, NB]]), in1=V(rowc, 0, [[0, G], [1, NB]]), op=ALU.mult),
                     reads=KT_ + ["A"], writes=KT_)
                P.op("dve", lambda E: E.tensor_scalar(out=cosT, in0=sinT, scalar1=0.25, scalar2=None, op0=ALU.add), reads=KT_, writes=KT_)
                for T_ in (sinT, cosT):
                    P.op("dve", lambda E, T_=T_: E.tensor_copy(out=tiq, in_=T_), reads=KT_, writes=KT_)
                    P.op("dve", lambda E, T_=T_: E.tensor_copy(out=Wq, in_=tiq), reads=KT_, writes=KT_)
                    P.op("dve", lambda E, T_=T_: E.tensor_tensor(out=T_, in0=T_, in1=Wq, op=ALU.subtract), reads=KT_, writes=KT_)
                    P.op("act", lambda E, T_=T_: E.activation(out=T_, in_=T_, func=AF.Sin, scale=6.28318), reads=KT_, writes=KT_)
                P.op("dve", lambda E: E.memset(carry[:], 0.0), writes=["carry"])
                P.op("dve", lambda E: E.tensor_copy(out=M8, in_=V(R8, 0, [[1, G], [0, NB]])), reads=["A"], writes=["M8"])
                P.op("dve", lambda E: E.memset(V(M8, 0, [[NB, G]]), 0.0), reads=["M8"], writes=["M8"])
                if True:
                    for g in range(G):
                        q, gl = g // 8, g % 8
                        b, k = bank()
                        mm_group(b[:, :], k, [(selP[:, gl * 8 + j, :], V(usT, q * NA + j * 512, [[1, 512]]), ["selP", "usT"]) for j in range(8)])
                        P.op("act", lambda E, b=b, g=g: E.activation(out=Up[:, g, :], in_=b[:, :], func=AF.Copy), reads=[k], writes=["Up"])
                    P.barrier()
                    sP.close()
                with ExitStack() as s2:
                    SINc = sb([128, G, NB], BF16, s2)
                    SINs = sb([128, G, NB], BF16, s2)
                    jm = sb([128, 128], F32, s2)
                    P.dma("sp", jm[:], jmat_d, writes=["jm"])
                    wl = sb([128, G], F32, s2)
                    c1 = sb([128, G], F32, s2)
                    c2_ = sb([128, G], F32, s2)
                    KT_ = ["rot"]
                    selUa = SAq.bitcast(BF16).rearrange("p g (a b) -> p (g a) b", a=1) if False else V(SAq.bitcast(BF16), 0, [[128, 32], [1, 128]])
                    selUb = V(SBq.bitcast(BF16), 0, [[128, 32], [1, 128]])
                    P.dma("pool", selUa, selU_d[:, 0:32, :], reads=KT_, writes=["selUa"])
                    TTd = lambda o, x, y, op, rk, wk: P.op("dve", lambda E: E.tensor_tensor(out=o, in0=x, in1=y, op=op), reads=rk, writes=wk)
                    NE8 = 512 // NB

                    def s8_rotin(e8):
                        for (Et, dst, tab, nm) in ((E8a, Wq, cosT, "Wq"), (E8b, SBq, sinT, "SB")):
                            for j4 in range(4):
                                b, k = bank()
                                for gl in range(8):
                                    g = j4 * 8 + gl
                                    P.op("pe", lambda E, b=b, g=g, gl=gl, Et=Et, e8=e8: E.matmul(b[:, gl * NB:(gl + 1) * NB], Et[:, g, :], Up[:, g, e8 * NB:(e8 + 1) * NB], start=True, stop=True),
                                         reads=["E8a", "E8b", "Up"], writes=[k], inc=(gl == 7))
                                P.op("dve", lambda E, b=b, j4=j4, dst=dst, tab=tab: E.tensor_tensor(out=dst[:, j4 * 8:(j4 + 1) * 8, :], in0=b[:, :].rearrange("p (a b) -> p a b", a=8),
                                                                                                   in1=tab[:, j4 * 8:(j4 + 1) * 8, :], op=ALU.mult),
                                     reads=[k] + KT_, writes=[nm])
                        TTd(Wq, Wq, SBq, ALU.subtract, ["Wq", "SB"], ["Wq"])

                    s8_rotin(0)
                    for e8 in range(NE8):
                        own = e8 >= 256 // NB
                        if own:
                            P.op("act", lambda E: E.activation(out=V(SINc, 0, [[NB, G]]), in_=carry[:], func=AF.Copy), reads=["carry"], writes=["SINc"])
                            P.op("dve", lambda E: E.memset(V(SINs, 0, [[NB, G]]), 0.0), writes=["SINs"])
                        TTd(c1[:], R8[:], carry[:], ALU.mult, ["A", "carry"], ["c1"])
                        TTd(V(Wq, 0, [[NB, G]]), V(Wq, 0, [[NB, G]]), c1[:], ALU.add, ["Wq", "c1"], ["Wq"])
                        P.op("dve", lambda E: E.tensor_tensor_scan(out=V(Wq, 0, [[1, G * NB]]), data0=V(M8, 0, [[1, G * NB]]), data1=V(Wq, 0, [[1, G * NB]]), initial=0.0, op0=ALU.mult, op1=ALU.add),
                             reads=["Wq", "M8"], writes=["Wq"])
                        if own:
                            TTd(SINc[:, :, 1:NB], cosT[:, :, 0:NB - 1], Wsc[:, :, 0:NB - 1], ALU.mult, KT_ + ["Wq"], ["SINc"])
                            TTd(SINs[:, :, 1:NB], sinT[:, :, 0:NB - 1], Wsc[:, :, 0:NB - 1], ALU.mult, KT_ + ["Wq"], ["SINs"])
                        P.op("dve", lambda E: E.tensor_copy(out=wl[:], in_=V(Wsc, NB - 1, [[NB, G]])), reads=["Wq"], writes=["wl"])
                        bj, kj = bank()
                        mm_group(bj[:, 0:G], kj, [(jm[:], wl[:], ["jm", "wl"])])
                        TTd(c1[:], V(cosT, NB - 1, [[NB, G]]), wl[:], ALU.mult, KT_ + ["wl"], ["c1"])
                        TTd(c2_[:], V(sinT, NB - 1, [[NB, G]]), bj[:, 0:G], ALU.mult, KT_ + [kj], ["c2_"])
                        TTd(carry[:], c1[:], c2_[:], ALU.add, ["c1", "c2_"], ["carry"])
                        if e8 + 1 < NE8:
                            s8_rotin(e8 + 1)
                        if own:
                            o0 = (e8 - 256 // NB) * NB
                            for j4 in range(4):
                                b, k = bank()
                                for gl in range(8):
                                    g = j4 * 8 + gl
                                    pso = b[:, gl * NB:(gl + 1) * NB]
                                    parts = [(M0[:, g, :], Up[:, g, e8 * NB:(e8 + 1) * NB], ["M0", "Up"]),
                                             (FC[:, g, :], SINc[:, g, :], ["tbl", "SINc"]),
                                             (FC2[:, g, :], SINs[:, g, :], ["tbl", "SINs"])]
                                    for i_, (l_, r_, rk_) in enumerate(parts):
                                        P.op("pe", lambda E, pso=pso, l_=l_, r_=r_, i_=i_: E.matmul(pso, l_, r_, start=(i_ == 0), stop=(i_ == 2)),
                                             reads=rk_, writes=[k], inc=(gl == 7 and i_ == 2))
                                P.op("act", lambda E, b=b, j4=j4, o0=o0: E.activation(out=Yp[:, j4 * 8:(j4 + 1) * 8, o0:o0 + NB], in_=b[:, :].rearrange("p (a b) -> p a b", a=8), func=AF.Copy),
                                     reads=[k], writes=["Yp"])
                    P.dma("pool", selUb, selU_d[:, 32:64, :], reads=["Wq"], writes=["SB", "selUb"])
                    for i in range(8):
                        sel_, sk_ = (selUa, "selUa") if i < 4 else (selUb, "selUb")
                        for q in range(4):
                            b, k = bank()
                            mm_group(b[:, 0:256], k, [(sel_[:, (i % 4) * 8 + gl, :], Yp[:, q * 8 + gl, :], [sk_, "Yp"]) for gl in range(8)])
                            P.op("dve", lambda E, b=b, q=q, i=i: E.scalar_tensor_tensor(
                                out=V(yT, q * NT + i, [[8, 256]]), in0=V(usT, q * NA + i * 512 + 256, [[1, 256]]), scalar=dv[:, q:q + 1],
                                in1=b[:, 0:256], op0=ALU.mult, op1=ALU.add), reads=[k, "usT", "dv"], writes=["yT"])
                    P.barrier()

            h0b = sb([128, 8, NT], BF16, sM)
            for n_ in range(4):
                P.dma("pool", h0b[:, :, n_ * TT:(n_ + 1) * TT], h0s[:, n_ * TT:(n_ + 1) * TT].rearrange("(c p) t -> p c t", p=128),
                      reads=["h0s"], writes=[("h0b", n_)])
            c2 = sb([128, 8, NT], BF16, sM)
            RA.reset()
            u = ra([128, 8, 32 + NT], BF16)
            dw = ra([128, 8, 31], F32)
            P.dma("sp", dw, dwT, writes=["dw"])
            idf = ra([128, 128], F32)
            P.op("act", lambda E: E.activation(out=idf, in_=ident, func=AF.Copy), reads=["cst"], writes=["idf"])
            acc1 = ra([128, 4, 512], F32)
            acc2 = ra([128, 4, 512], F32)
            cf = ra([128, 512], F32)
            cfs = [cf, ra([128, 512], F32)]
            MC = Arena(mgR, 8 * NT * 2)
            dgs = [MC.alloc([128, 31, 128], BF16) for _ in range(2)]
            sgs = [MC.alloc([128, 512], F32) for _ in range(2)]
            csqs = [MC.alloc([128, 512], BF16) for _ in range(2)]
            MCK = ["dg0", "dg1", "sg0", "sg1", "csq0", "csq1"]
            vg_i = [0]

            def VG(q):
                wv, kv_ = wload(w_in, 0, 8, q * 128)
                wg, kg_ = wload(w_in, 0, 8, 1024 + q * 128)
                tiles = [(None, 32)] + [(n * TT, TT) for n in range(4)]
                for (t0, n) in tiles:
                    rhs = (lambda kc: hh[:, kc, :]) if t0 is None else (lambda kc, t0=t0: h0b[:, kc, t0:t0 + TT])
                    rk = "hh" if t0 is None else ("h0b", t0 // TT)
                    sg = sgs[vg_i[0] % 2]
                    sk = "sg%d" % (vg_i[0] % 2)
                    vg_i[0] += 1
                    bv, kbv = bank()
                    bg, kbg = bank()
                    mm_group(bv[:, 0:n], kbv, [(wv(kc, 0, 128), rhs(kc), [kv_, rk]) for kc in range(8)])
                    mm_group(bg[:, 0:n], kbg, [(wg(kc, 0, 128), rhs(kc), [kg_, rk]) for kc in range(8)])
                    P.op("act", lambda E, bg=bg, n=n, sg=sg: E.activation(out=sg[:, 0:n], in_=bg[:, 0:n], func=AF.Sigmoid), reads=[kbg], writes=[sk])
                    if t0 is None:
                        P.op("dve", lambda E, bv=bv, q=q, sg=sg: E.scalar_tensor_tensor(out=u[:, q, 0:32], in0=sg[:, 0:32], scalar=mk[:, 0:1], in1=bv[:, 0:32],
                                                                                    op0=ALU.mult, op1=ALU.mult), reads=[kbv, sk, "mk"], writes=[("u", q)])
                    else:
                        P.op("dve", lambda E, bv=bv, t0=t0, q=q, sg=sg: E.tensor_tensor(out=u[:, q, 32 + t0:32 + t0 + TT], in0=bv[:, :], in1=sg, op=ALU.mult),
                             reads=[kbv, sk], writes=[("u", q)])
                dg = dgs[q % 2]
                for k in range(31):
                    P.op("dve", lambda E, k=k, q=q, dg=dg: E.tensor_scalar(out=dg[:, k, :], in0=idf, scalar1=dw[:, q, k:k + 1], scalar2=None, op0=ALU.mult),
                         reads=["idf", "dw"], writes=["dg%d" % (q % 2)])

            def CONV(q, after_n=None):
                dg = dgs[q % 2]
                dk = "dg%d" % (q % 2)
                for n in range(4):
                    if after_n is not None and n >= 1:
                        after_n(n - 1)
                    csq = csqs[n % 2]
                    ck = "csq%d" % (n % 2)
                    b, kb = bank()
                    mm_group(b[:, :], kb, [(dg[:, k, :], u[:, q, 2 + n * TT + k: 2 + n * TT + k + TT], [dk, ("u", q)]) for k in range(31)])
                    P.op("act", lambda E, b=b, q=q, n=n: E.activation(out=c2[:, q, n * TT:(n + 1) * TT], in_=b[:, :], func=AF.Identity, bias=vec[:, 10, q:q + 1], scale=1.0),
                         reads=[kb, "vec"], writes=[("c2", n, q)])
                    P.op("act", lambda E, b=b, q=q, csq=csq: E.activation(out=csq, in_=b[:, :], func=AF.Square, bias=vec[:, 10, q:q + 1], scale=1.0),
                         reads=[kb, "vec"], writes=[ck])
                    b1, k1 = bank()
                    b2, k2 = bank()
                    mm_group(b1[:, :], k1, [(onesM, c2[:, q, n * TT:(n + 1) * TT], ["cst", ("c2", n, q)])])
                    mm_group(b2[:, :], k2, [(onesM, csq, ["cst", ck])])
                    if q == 0:
                        P.op("dve", lambda E, b1=b1, n=n: E.tensor_copy(out=acc1[:, n, :], in_=b1[:, :]), reads=[k1], writes=[("acc1", n)])
                        P.op("dve", lambda E, b2=b2, n=n: E.tensor_copy(out=acc2[:, n, :], in_=b2[:, :]), reads=[k2], writes=[("acc2", n)])
                    else:
                        P.op("dve", lambda E, b1=b1, n=n: E.tensor_tensor(out=acc1[:, n, :], in0=b1[:, :], in1=acc1[:, n, :], op=ALU.add), reads=[k1, ("acc1", n)], writes=[("acc1", n)])
                        P.op("dve", lambda E, b2=b2, n=n: E.tensor_tensor(out=acc2[:, n, :], in0=b2[:, :], in1=acc2[:, n, :], op=ALU.add), reads=[k2, ("acc2", n)], writes=[("acc2", n)])
                if after_n is not None:
                    after_n(3)

            def apply_n(n):
                a1, a2 = acc1[:, n, :], acc2[:, n, :]
                Kc = [("acc1", n), ("acc2", n)]
                pr, kpr = bank()
                pm, kpm = bank()
                P.op("act", lambda E, a1=a1: E.activation(out=cf, in_=a1, func=AF.Square), reads=Kc, writes=["cf0"])
                P.op("dve", lambda E, a2=a2: E.tensor_tensor(out=a2, in0=a2, in1=cf, op=ALU.subtract), reads=Kc + ["cf0"], writes=Kc)
                P.op("act", lambda E, a2=a2: E.activation(out=a2, in_=a2, func=AF.Ln, bias=epst[:, 0:1], scale=1.0), reads=Kc + ["eps"], writes=Kc)
                P.op("act", lambda E, a2=a2: E.activation(out=a2, in_=a2, func=AF.Exp, scale=-0.5), reads=Kc, writes=Kc)
                P.op("dve", lambda E, a2=a2, pr=pr: E.tensor_copy(out=pr[:, :], in_=a2), reads=Kc, writes=[kpr])
                P.op("dve", lambda E, a1=a1, a2=a2, pm=pm: E.tensor_tensor(out=pm[:, :], in0=a1, in1=a2, op=ALU.mult), reads=Kc, writes=[kpm])
                for q in range(8):
                    cs = c2[:, q, n * TT:(n + 1) * TT]
                    cfq = cfs[q % 2]
                    ck = "cf%d" % (q % 2)
                    P.op("dve", lambda E, cs=cs, pr=pr, cfq=cfq: E.tensor_tensor(out=cfq, in0=cs, in1=pr[:, :], op=ALU.mult), reads=[kpr, ("c2", n, q)], writes=[ck])
                    P.op("dve", lambda E, pm=pm, cfq=cfq: E.tensor_tensor(out=cfq, in0=cfq, in1=pm[:, :], op=ALU.subtract), reads=[kpm, ck], writes=[ck])
                    P.op("act", lambda E, cs=cs, cfq=cfq, q=q: E.activation(out=cs, in_=cfq, func=AF.Silu, bias=vec[:, 3, q:q + 1], scale=vec[:, 2, q:q + 1]),
                         reads=[ck, "vec"], writes=[("c2", n, q)])

            VG(0)
            for q in range(8):
                if q + 1 < 8:
                    VG(q + 1)
                CONV(q, after_n=(apply_n if q == 7 else None))
            if DBG_BAR:
                P.barrier()
            sA = ra([128, 512], F32); sBt = ra([128, 512], F32); sC = ra([128, 512], F32)
            tA = ra([128, 512], F32); tB = ra([128, 512], F32)
            for m in range(8):
                wga, kga = wload(w_in, 0, 8, 2560 + m * 128)
                wgb, kgb = wload(w_in, 0, 8, 3584 + m * 128)
                wco, kco = wload(w_conv_out, 0, 8, m * 128)
                wzv, kzv = wload(w_glu, 0, 4, m * 128)
                wzg, kzg = wload(w_glu, 0, 4, 1024 + m * 128)
                for n in range(4):
                    ts = slice(n * TT, (n + 1) * TT)
                    b1, k1 = bank(); b2, k2 = bank(); b3, k3 = bank(); b4, k4 = bank(); b5, k5 = bank()
                    mm_group(b1[:, :], k1, [(wga(kc, 0, 128), h0b[:, kc, ts], [kga, ("h0b", n)]) for kc in range(8)])
                    mm_group(b2[:, :], k2, [(wco(kc, 0, 128), c2[:, kc, ts], [kco, ("c2", n, kc)]) for kc in range(8)])
                    mm_group(b3[:, :], k3, [(wgb(kc, 0, 128), h0b[:, kc, ts], [kgb, ("h0b", n)]) for kc in range(8)])
                    mm_group(b4[:, :], k4, [(wzv(kc, 0, 128), yT[:, kc, ts], [kzv, "yT"]) for kc in range(4)])
                    mm_group(b5[:, :], k5, [(wzg(kc, 0, 128), yT[:, kc, ts], [kzg, "yT"]) for kc in range(4)])
                    P.op("act", lambda E, b1=b1: E.activation(out=sA, in_=b1[:, :], func=AF.Sigmoid), reads=[k1], writes=["sA"])
                    P.op("act", lambda E, b3=b3: E.activation(out=sBt, in_=b3[:, :], func=AF.Sigmoid), reads=[k3], writes=["sB"])
                    P.op("act", lambda E, b5=b5: E.activation(out=sC, in_=b5[:, :], func=AF.Sigmoid), reads=[k5], writes=["sC"])
                    P.op("dve", lambda E, b2=b2: E.tensor_tensor(out=tA, in0=b2[:, :], in1=sA, op=ALU.mult), reads=[k2, "sA"], writes=["tA"])
                    P.op("dve", lambda E, b4=b4: E.tensor_tensor(out=tB, in0=b4[:, :], in1=sC, op=ALU.mult), reads=[k4, "sC"], writes=["tB"])
                    P.op("dve", lambda E: E.tensor_tensor(out=tB, in0=tB, in1=sBt, op=ALU.mult), reads=["tB", "sB"], writes=["tB"])
                    P.op("dve", lambda E, m=m, ts=ts: E.tensor_tensor(out=mg[:, m, ts], in0=tA, in1=tB, op=ALU.add), reads=["tA", "tB"], writes=["mg"] + MCK)
            P.barrier()
        hb = sb([128, 8, NT], BF16)
        wres = sb([128, 8, 1024], BF16)

        def wres_load(w_ap, r0):
            P.dma("pool", wres[:], w_ap[r0:r0 + 1024, :].rearrange("(kc p) n -> p kc n", p=128), writes=["wres", "wres7"])

        def proj_ln(emit_mm, gi, write_out=False, S=None, f32_affine=None):
            rsrc = lambda n: (lambda c: r[:, c, n * TT:(n + 1) * TT])
            rtile = lambda n: r[:, :, n * TT:(n + 1) * TT]
            rk = lambda n: (lambda c: ("r", n, c))
            A_ = lambda n: ln_front_a(rsrc(n), rk(n), TT, rtile(n), S)
            fr = {}
            split_banks(True)
            if emit_mm:
                emit_mm(0)
                emit_mm(1)
            fr[0] = A_(0)
            ln_front_b(fr[0], TT, S)
            for n in range(4):
                ts = slice(n * TT, (n + 1) * TT)
                if emit_mm and n + 2 < 4:
                    emit_mm(n + 2)
                if n + 1 < 4:
                    fr[n + 1] = A_(n + 1)
                ln_back(fr[n], rsrc(n), rk(n), TT, gi, out_f32=rsrc(n),
                        out_bf=(None if write_out else (lambda c, ts=ts: hb[:, c, ts])), okeys=rk(n), bfkeys=[("hb", n)],
                        inter=(ln_front_b_steps(fr[n + 1], TT, S) if n + 1 < 4 else None), S=S, f32_affine=f32_affine)
                if write_out:
                    P.dma("sp", outT[:, n * TT:(n + 1) * TT].rearrange("(c p) t -> p c t", p=128), r[:, :, ts], reads=[("r", n, c) for c in range(8)], writes=["outT"], is_output=True)
            split_banks(False)

        wres_load(w_mix, 0)
        hx_i = [0]

        def mix_mm(n):
            ts = slice(n * TT, (n + 1) * TT)
            for m in range(8):
                hxi = hx[hx_i[0] % 2]
                hk = ("hx", hx_i[0] % 2)
                hx_i[0] += 1
                P.dma("sp", hxi[:], h0s[m * 128:(m + 1) * 128, n * TT:(n + 1) * TT], reads=["h0s"], writes=[hk])
                b, k = bank()
                mm_group(b[:, :], k, [(wres[:, kc, m * 128:(m + 1) * 128], mg[:, kc, ts], ["wres", "wres7", "mg"]) for kc in range(8)])
                P.op("dve", lambda E, b=b, m=m, ts=ts, hxi=hxi: E.scalar_tensor_tensor(out=r[:, m, ts], in0=hxi[:], scalar=ALPHA, in1=b[:, :],
                                                                                    op0=ALU.mult, op1=ALU.add), reads=[k, hk], writes=[("r", n, m)])

        mT = sb([128, 8, 256], BF16)
        P.dma("pool", mT[:], memT.rearrange("(c p) t -> p c t", p=128), writes=["mT"])
        KT = sb([128, 8, 256], BF16)
        Vt = sb([128, 2, D], BF16)
        wvt = V(wringT, 0, [[512, 8], [1, 512]])
        wvk = [("w", i) for i in range(4)]

        def kv_proj():
            for m in range(8):
                wk_, kk_ = wload(wkv, 0, 8, m * 128)
                b, k = bank()
                mm_group(b[:, 0:256], k, [(wk_(kc, 0, 128), mT[:, kc, :], [kk_, "mT"]) for kc in range(8)])
                P.op("act", lambda E, b=b, m=m: E.activation(out=KT[:, m, :], in_=b[:, 0:256], func=AF.Copy), reads=[k], writes=["KT"])
            for db_ in range(2):
                P.dma("pool", wvt, wkv[:, 1024 + db_ * 512:1024 + (db_ + 1) * 512].rearrange("(kc p) n -> p kc n", p=128), writes=wvk)
                for mc in range(2):
                    b, k = bank()
                    mm_group(b[:, :], k, [(mT[:, kc, mc * 128:(mc + 1) * 128], wvt[:, kc, :], wvk + ["mT"]) for kc in range(8)])
                    P.op("act", lambda E, b=b, mc=mc, db_=db_: E.activation(out=Vt[:, mc, db_ * 512:(db_ + 1) * 512], in_=b[:, :], func=AF.Copy), reads=[k], writes=["Vt"])

        with ExitStack() as sL:
            hx = [sb([128, 512], F32, sL) for _ in range(2)]
            S1 = dict(xb=sb([128, 8, 512], BF16, sL), sq=sb([128, 8, 512], BF16, sL), m2=sb([128, 512], F32, sL), rs=sb([128, 512], F32, sL),
                      tmps=[sb([128, 512], F32, sL) for _ in range(2)], tag="B")
            kv_done = [False]

            def mix_mm2(n):
                mix_mm(n)
                if n == 3 and not kv_done[0]:
                    kv_done[0] = True
                    kv_proj()

            proj_ln(mix_mm2, 4, S=S1)
            P.dma("pool", wres[:, 0:7, :], wo[0:896, :].rearrange("(kc p) n -> p kc n", p=128), writes=["wres"])
            P.barrier()

        with ExitStack() as s1:
            oT = sb([128, 8, NT], BF16, s1)
            qh = sb([128, 1, 2, TT], BF16, s1)
            PT = sb([128, 2, 512], BF16, s1)
            rden = sb([128, 512], F32, s1)
            qh2 = V(wres, 7 * 1024, [[512, 2], [1, 512]])
            qbuf = [lambda dc: qh[:, 0, dc, :], lambda dc: qh2[:, dc, :]]
            items = [(h, n) for h in range(4) for n in range(4)]
            wqs_h = {}

            def stage_q(i):
                h, n = items[i]
                if h not in wqs_h:
                    wqs_h[h] = [wload(wq, 0, 8, (2 * h + dc) * 128) for dc in range(2)]
                ts = slice(n * TT, (n + 1) * TT)
                qb = i % 2
                for dc in range(2):
                    wq_, kq_ = wqs_h[h][dc]
                    b, k = bank()
                    mm_group(b[:, :], k, [(wq_(kc, 0, 128), hb[:, kc, ts], [kq_, ("hb", n)]) for kc in range(8)])
                    dst = qbuf[qb](dc)
                    P.op("act", lambda E, b=b, dst=dst: E.activation(out=dst, in_=b[:, :], func=AF.Copy), reads=[k], writes=[("qh", qb, dc)])

            def stage_att(i):
                h, n = items[i]
                ts = slice(n * TT, (n + 1) * TT)
                qb = i % 2
                for mc in range(2):
                    b, k = bank()
                    mm_group(b[:, :], k, [(KT[:, 2 * h + dc, mc * 128:(mc + 1) * 128], qbuf[qb](dc), ["KT", ("qh", qb, dc)]) for dc in range(2)])
                    P.op("act", lambda E, b=b, mc=mc: E.activation(out=PT[:, mc, :], in_=b[:, :], func=AF.Exp, scale=1.0 / 16.0), reads=[k], writes=[("PT", mc)])
                bd, kd = bank()
                mm_group(bd[:, :], kd, [(ones1, PT[:, mc, :], ["cst", ("PT", mc)]) for mc in range(2)])
                P.op("act", lambda E, bd=bd: E.activation(out=rden[:], in_=bd[:, :], func=AF.Ln), reads=[kd], writes=["rden"])
                P.op("act", lambda E: E.activation(out=rden[:], in_=rden[:], func=AF.Exp, scale=-1.0), reads=["rden"], writes=["rden"])
                for dc in range(2):
                    b, k = bank()
                    mm_group(b[:, :], k, [(Vt[:, mc, (2 * h + dc) * 128:(2 * h + dc + 1) * 128], PT[:, mc, :], ["Vt", ("PT", mc)]) for mc in range(2)])
                    P.op("dve", lambda E, b=b, h=h, dc=dc, ts=ts: E.tensor_tensor(out=oT[:, 2 * h + dc, ts], in0=b[:, :], in1=rden[:], op=ALU.mult),
                         reads=[k, "rden"], writes=["oT"])

            stage_q(0)
            for i in range(len(items)):
                if i + 1 < len(items):
                    stage_q(i + 1)
                stage_att(i)
            P.dma("pool", wres[:, 7:8, :], wo[896:1024, :].rearrange("(kc p) n -> p kc n", p=128), writes=["wres7", ("qh", 1, 0), ("qh", 1, 1)])

            def wo_mm(n):
                ts = slice(n * TT, (n + 1) * TT)
                for m in range(8):
                    b, k = bank()
                    mm_group(b[:, :], k, [(wres[:, kc, m * 128:(m + 1) * 128], oT[:, kc, ts], ["wres", "wres7", "oT"]) for kc in range(8)])
                    P.op("dve", lambda E, b=b, m=m, ts=ts: E.scalar_tensor_tensor(out=r[:, m, ts], in0=r[:, m, ts], scalar=ALPHA, in1=b[:, :],
                                                                               op0=ALU.mult, op1=ALU.add), reads=[k, ("r", n, m)], writes=[("r", n, m)])

            proj_ln(wo_mm, 6, f32_affine=vecA)
            P.barrier()

        with ExitStack() as s1:
            z = sb([128, 8, NT], BF16, s1)
            rl = sb([128, 512], F32, s1)
            wres_load(w_down, 3 * 1024)
            for hbk in range(4):
                for mm in range(8):
                    wu, ku = wload(w_up, 0, 8, hbk * 1024 + mm * 128)
                    for n in range(4):
                        ts = slice(n * TT, (n + 1) * TT)
                        b, k = bank()
                        mm_group(b[:, :], k, [(wu(kc, 0, 128), hb[:, kc, ts], [ku, ("hb", n)]) for kc in range(8)])
                        P.op("act", lambda E, b=b: E.activation(out=rl[:], in_=b[:, :], func=AF.Relu), reads=[k], writes=["rl"])
                        P.op("dve", lambda E, b=b, mm=mm, ts=ts: E.tensor_tensor(out=z[:, mm, ts], in0=b[:, :], in1=rl[:], op=ALU.mult), reads=[k, "rl"], writes=["z"])
                if hbk == 3:
                    break
                for m in range(8):
                    wd, kd = wload(w_down, hbk * 1024, 8, m * 128)
                    for n in range(4):
                        ts = slice(n * TT, (n + 1) * TT)
                        b, k = bank()
                        mm_group(b[:, :], k, [(wd(kc, 0, 128), z[:, kc, ts], [kd, "z"]) for kc in range(8)])
                        P.op("dve", lambda E, b=b, m=m, ts=ts: E.tensor_tensor(out=r[:, m, ts], in0=b[:, :], in1=r[:, m, ts], op=ALU.add), reads=[k, ("r", n, m)], writes=[("r", n, m)])

            def down_mm(n):
                ts = slice(n * TT, (n + 1) * TT)
                for m in range(8):
                    b, k = bank()
                    mm_group(b[:, :], k, [(wres[:, kc, m * 128:(m + 1) * 128], z[:, kc, ts], ["wres", "wres7", "z"]) for kc in range(8)])
                    P.op("dve", lambda E, b=b, m=m, ts=ts: E.tensor_tensor(out=r[:, m, ts], in0=b[:, :], in1=r[:, m, ts], op=ALU.add), reads=[k, ("r", n, m)], writes=[("r", n, m)])

            proj_ln(down_mm, 8, write_out=True)
            P.barrier()
        P.finish()
        with nc.Block() as block:
            P.run(block)
    return nc


_NC = [None]


def kernel(**inp):
    f = lambda a: np.ascontiguousarray(np.asarray(a, dtype=np.float32))
    x = f(inp["x"]); mem = f(inp["mem"])
    fm = lambda v: f(v).reshape(8, 128).T
    vecs = np.zeros((128, 12, 8), np.float32)
    for i, (gk, bk) in enumerate([("in_norm_g", "in_norm_b"), ("conv_norm_g", "conv_norm_b"), ("ln1_g", "ln1_b"), ("ln2_g", "ln2_b"), ("ln3_g", "ln3_b")]):
        vecs[:, 2 * i] = fm(np.asarray(inp[gk]).reshape(-1))
        vecs[:, 2 * i + 1] = fm(np.asarray(inp[bk]).reshape(-1))
    vecs[:, 10] = fm(np.asarray(inp["conv_db"]).reshape(-1))
    dwT = f(inp["conv_dw"])[0].T.reshape(8, 128, 31).transpose(1, 0, 2)
    dup = lambda a: np.concatenate([a, a], axis=0)
    lam = np.stack([dup(f(inp["ssm_lambda_re"])[0].T), dup(f(inp["ssm_lambda_im"])[0].T),
                    np.broadcast_to(f(inp["ssm_log_step"])[0][None, :], (128, G))], axis=1)
    Br = dup(f(inp["ssm_b_re"])[0].transpose(1, 0, 2)); Bi = dup(f(inp["ssm_b_im"])[0].transpose(1, 0, 2))
    Cr = dup(f(inp["ssm_c_re"])[0].transpose(2, 0, 1)); Ci = dup(f(inp["ssm_c_im"])[0].transpose(2, 0, 1))
    bc_in = np.stack([Br, Bi, Cr, Ci], axis=1)
    dvec = f(inp["ssm_d"])[0].reshape(4, 128).T
    consts = np.stack([np.full((128, 128), 1.0 / 1024, np.float32), np.ones((128, 128), np.float32), np.eye(128, dtype=np.float32)], axis=1)
    jl = np.arange(128) // 16
    cmask = (jl[None, :] >= jl[:, None]).astype(np.float32)
    selP = np.zeros((128, 64, 128), np.float32)
    selU = np.zeros((128, 64, 128), np.float32)
    for gl in range(8):
        for j in range(8):
            for h in range(16):
                selP[gl * 16 + h, gl * 8 + j, j * 16 + h] = 1.0
                selU[j * 16 + h, j * 8 + gl, gl * 16 + h] = 1.0
    rowc = np.broadcast_to(np.arange(1, 65, dtype=np.float32)[None, :], (128, 64)).copy()
    jmat = np.zeros((128, 128), np.float32)
    for m_ in range(64):
        jmat[m_ + 64, m_] = -1.0
        jmat[m_, m_ + 64] = 1.0
    common = dict(rowc=rowc, jmat=jmat,
        w_in=f(inp["w_in"])[0], w_conv_out=f(inp["w_conv_out"])[0], w_glu=f(inp["w_ssm_glu"])[0], w_mix=f(inp["w_mix_out"])[0],
        wq=f(inp["xa_wq"])[0], wkv=f(inp["xa_wkv"])[0], wo=f(inp["xa_wo"])[0], w_up=f(inp["mlp_w_up"])[0], w_down=f(inp["mlp_w_down"])[0],
        vecs=vecs, dwT=f(dwT), lam=f(lam), bc_in=f(bc_in), dvec=f(dvec), consts=f(consts), cmask=cmask, selP=selP, selU=selU)
    in_maps = []
    for core in range(8):
        b, half = core // 2, core % 2
        xT = np.zeros((D, NA), np.float32)
        xT[:, NT:] = x[b, half * NT:(half + 1) * NT].T
        if half == 1:
            xT[:, :NT] = x[b, 0:NT].T
        m = dict(common)
        m["xT"] = xT
        m["maskv"] = np.full((128, 1), float(half), np.float32)
        m["memT"] = f(mem[b].T)
        in_maps.append(m)
    if _NC[0] is None:
        _NC[0] = build()
    res = run_bass_kernel_spmd(_NC[0], in_maps, core_ids=list(range(8)))
    out = np.zeros((4, 4096, D), np.float32)
    for core in range(8):
        b, half = core // 2, core % 2
        out[b, half * NT:(half + 1) * NT] = res.results[core]["outT"].T
    return out
```

```python
import math
from contextlib import ExitStack
import numpy as np
import ml_dtypes
import concourse.bass as bass
import concourse.mybir as mybir
from concourse.bass_utils import run_bass_kernel_spmd

F32 = mybir.dt.float32
BF16 = mybir.dt.bfloat16
I32 = mybir.dt.int32
ALU = mybir.AluOpType
AF = mybir.ActivationFunctionType
AX = mybir.AxisListType

ENGS = ("pe", "act", "dve", "pool", "sp")
D = 1024
NT = 2048
NA = 4096
TT = 512
G = 32
ALPHA = 2.0 ** 0.25
EPS = 1e-5
DBG_BAR = False
SCAN_SINGLE = True


class Prog:
    def __init__(self, nc, stack, n_lanes=8):
        self.nc = nc
        self.ops = {e: [] for e in ENGS}
        self.cnt = {}
        self.sem = {}
        for e in ENGS:
            self.sem[e] = stack.enter_context(nc.semaphore("clk_" + e))
            self.cnt[e] = 0
        self.lanes = {}
        for q in ("sp", "pool", "act"):
            self.lanes[q] = []
            for i in range(n_lanes):
                k = "dma_%s_%d" % (q, i)
                self.sem[k] = stack.enter_context(nc.semaphore(k))
                self.cnt[k] = 0
                self.lanes[q].append(k)
        self.lane_rr = {q: 0 for q in self.lanes}
        self.waited = {e: {} for e in ENGS}
        self.last_w = {}
        self.readers = {}
        self.out_events = []

    def _deps(self, reads, writes):
        deps = {}

        def add(ev):
            if ev is None:
                return
            s, v = ev
            if deps.get(s, 0) < v:
                deps[s] = v

        for k in reads:
            add(self.last_w.get(k))
            if isinstance(k, tuple) and k and k[0] == "ps":
                for s_, v_ in self.readers.get(k, {}).items():
                    add((s_, v_))
        for k in writes:
            add(self.last_w.get(k))
            for s, v in self.readers.get(k, {}).items():
                add((s, v))
        return deps

    def _emit_waits(self, eng, deps):
        for s, v in deps.items():
            if s == eng and eng == "pe":
                continue
            if self.waited[eng].get(s, 0) >= v:
                continue
            self.waited[eng][s] = v
            sem = self.sem[s]
            self.ops[eng].append(lambda E, sem=sem, v=v: E.wait_ge(sem, v))

    def _record(self, ev, reads, writes):
        for k in reads:
            d = self.readers.setdefault(k, {})
            if d.get(ev[0], 0) < ev[1]:
                d[ev[0]] = ev[1]
        for k in writes:
            self.last_w[k] = ev
            self.readers[k] = {}

    def op(self, eng, fn, reads=(), writes=(), inc=True):
        deps = self._deps(reads, writes)
        self._emit_waits(eng, deps)
        if inc:
            self.cnt[eng] += 1
            ev = (eng, self.cnt[eng])
            sem = self.sem[eng]
            self.ops[eng].append(lambda E, fn=fn, sem=sem: fn(E).then_inc(sem, 1))
        else:
            ev = (eng, self.cnt[eng] + 1)
            self.ops[eng].append(lambda E, fn=fn: fn(E))
        self._record(ev, reads, writes)
        return ev

    def dma(self, q, out, in_, reads=(), writes=(), is_output=False, **kw):
        deps = self._deps(reads, writes)
        self._emit_waits(q, deps)
        lanes = self.lanes[q]
        lane = lanes[self.lane_rr[q] % len(lanes)]
        self.lane_rr[q] += 1
        if self.cnt[lane] > 0:
            self._emit_waits(q, {lane: self.cnt[lane]})
        self.cnt[lane] += 16
        ev = (lane, self.cnt[lane])
        sem = self.sem[lane]
        self.ops[q].append(
            lambda E, out=out, in_=in_, sem=sem, kw=kw: E.dma_start(out=out, in_=in_, **kw).then_inc(sem, 16)
        )
        self._record(ev, reads, writes)
        if is_output:
            self.out_events.append(ev)
        return ev

    def barrier(self):
        for e in ENGS:
            deps = {s: c for s, c in self.cnt.items() if c > 0 and s != e}
            self._emit_waits(e, deps)

    def finish(self):
        self.barrier()

    def run(self, block):
        ops = self.ops

        @block.tensor
        def _(E):
            for f in ops["pe"]:
                f(E)

        @block.scalar
        def _(E):
            for f in ops["act"]:
                f(E)

        @block.vector
        def _(E):
            for f in ops["dve"]:
                f(E)

        @block.gpsimd
        def _(E):
            for f in ops["pool"]:
                f(E)

        @block.sync
        def _(E):
            for f in ops["sp"]:
                f(E)


def V(t, off, dims):
    a = t if (hasattr(t, "ap") and hasattr(t, "offset")) else t[:]
    return bass.AP(a.tensor, a.offset + off, [list(a.ap[0])] + [list(d) for d in dims])


class Arena:
    def __init__(self, tile, nbytes):
        self.t, self.n, self.off = tile, nbytes, 0

    def reset(self):
        self.off = 0

    def alloc(self, shape, dt):
        esz = 2 if dt == BF16 else 4
        n = 1
        for d in shape[1:]:
            n *= d
        self.off = (self.off + 3) // 4 * 4
        base = self.t[:].bitcast(dt)
        e0 = self.off // esz
        a = base[:, e0:e0 + n]
        self.off += n * esz
        assert self.off <= self.n, (self.off, self.n)
        if len(shape) == 3:
            a = a.rearrange("p (a b) -> p a b", a=shape[1])
        elif len(shape) == 4:
            a = a.rearrange("p (a b c) -> p a b c", a=shape[1], b=shape[2])
        return a


def build():
    nc = bass.Bass("TRN2", target_bir_lowering=False)
    dr = {}

    def din(name, shape, dt=F32):
        dr[name] = nc.dram_tensor(name, list(shape), dt, kind="ExternalInput").ap()
        return dr[name]

    xT = din("xT", [D, NA])
    maskv = din("maskv", [128, 1])
    memT = din("memT", [D, 256])
    w_in = din("w_in", [D, 4608])
    w_conv_out = din("w_conv_out", [D, D])
    w_glu = din("w_glu", [512, 2048])
    w_mix = din("w_mix", [D, D])
    wq = din("wq", [D, D])
    wkv = din("wkv", [D, 2048])
    wo = din("wo", [D, D])
    w_up = din("w_up", [D, 4096])
    w_down = din("w_down", [4096, D])
    vecs = din("vecs", [128, 12, 8])
    dwT = din("dwT", [128, 8, 31])
    lam = din("lam", [128, 3, G])
    bc_in = din("bc_in", [128, 4, G, 16])
    dvec = din("dvec", [128, 4])
    consts = din("consts", [128, 3, 128])
    cmask = din("cmask", [128, 128])
    selP_d = din("selP", [128, 64, 128])
    selU_d = din("selU", [128, 64, 128])
    rowc_d = din("rowc", [128, 64])
    jmat_d = din("jmat", [128, 128])
    outT = nc.dram_tensor("outT", [D, NT], F32, kind="ExternalOutput").ap()
    h0s = nc.dram_tensor("h0s", [D, NT], F32, kind="Internal").ap()

    with ExitStack() as st:
        P = Prog(nc, st)
        cnt = [0]

        def sb(shape, dt, stack=st, name=None):
            cnt[0] += 1
            return stack.enter_context(nc.sbuf_tensor(name or ("t%d" % cnt[0]), list(shape), dt))

        banks = [st.enter_context(nc.psum_tensor("ps%d" % i, [128, 512], F32)) for i in range(8)]
        bank_i = {"g": 0, "ln": 0}
        bank_pool = {"g": list(range(8)), "ln": None}

        def bank(pool="g"):
            if pool == "ln" and bank_pool["ln"] is None:
                pool = "g"
            lst = bank_pool[pool]
            i = lst[bank_i[pool] % len(lst)]
            bank_i[pool] += 1
            return banks[i], ("ps", i)

        def split_banks(on):
            bank_pool["g"] = [0, 1, 2, 3] if on else list(range(8))
            bank_pool["ln"] = [4, 5, 6, 7] if on else None

        cst = sb([128, 3, 128], BF16)
        P.dma("pool", cst[:], consts, writes=["cst"])
        onesM, ones1, ident = cst[:, 0, :], cst[:, 1, :], cst[:, 2, :]
        vec = sb([128, 12, 8], F32)
        P.dma("sp", vec[:], vecs, writes=["vec"])
        mk = sb([128, 1], F32)
        P.dma("sp", mk[:], maskv, writes=["mk"])
        dv = sb([128, 4], F32)
        P.dma("sp", dv[:], dvec, writes=["dv"])
        epst = sb([128, 1], F32)
        vecA = sb([128, 2, 8], F32)
        P.op("dve", lambda E: E.tensor_scalar(out=vecA[:], in0=vec[:, 6:8, :], scalar1=ALPHA, scalar2=None, op0=ALU.mult), reads=["vec"], writes=["vecA"])
        P.op("dve", lambda E: E.memset(epst[:], EPS), writes=["eps"])
        hh = sb([128, 8, 32], BF16)

        NW = 6
        wringT = sb([128, NW * 1024], BF16)
        wring = [V(wringT, i * 1024, [[1, 1024]]) for i in range(NW)]
        wr_i = [0]

        ring_bufs = list(wring)

        def wload(w_ap, r0, KC, c0, width=128):
            i = wr_i[0] % len(ring_bufs)
            wr_i[0] += 1
            t = ring_bufs[i]
            key = ("w", i)
            src = w_ap[r0:r0 + KC * 128, c0:c0 + width].rearrange("(kc p) n -> p kc n", p=128)
            dst = V(t, 0, [[width, KC], [1, width]])
            P.dma("pool", dst, src, writes=[key])
            return (lambda kc, a, b: V(t, kc * width + a, [[1, b - a]])), key

        def mm_group(ps_ap, pskey, parts):
            n = len(parts)
            for i, (l, r, rk) in enumerate(parts):
                P.op("pe", lambda E, l=l, r=r, i=i: E.matmul(ps_ap, l, r, start=(i == 0), stop=(i == n - 1)),
                     reads=list(rk), writes=[pskey], inc=(i == n - 1))

        rR = sb([128, 8 * NT], F32)
        mgR = sb([128, 8 * NT], BF16)
        RA = Arena(rR, 8 * NT * 4)
        MA = Arena(mgR, 8 * NT * 2)
        r = rR[:].rearrange("p (c t) -> p c t", c=8)
        mg = mgR[:].rearrange("p (c t) -> p c t", c=8)
        xb = MA.alloc([128, 8, 512], BF16)
        sq = MA.alloc([128, 8, 512], BF16)
        m2 = MA.alloc([128, 512], F32)
        rs = MA.alloc([128, 512], F32)
        tmps = [MA.alloc([128, 512], F32) for _ in range(2)]

        LNS = dict(xb=xb, sq=sq, m2=m2, rs=rs, tmps=tmps, tag="A")

        def ln_front_a(src, srckey, n, xb_dma_src, S=None):
            S = S or LNS
            xb_, sq_, tg = S["xb"], S["sq"], S["tag"]
            xbk = S.get("xbkey", "lnxb" + tg)
            if xb_dma_src is not None:
                P.dma("pool", xb_[:, :, 0:n], xb_dma_src, reads=[srckey(c) for c in range(8)], writes=[xbk])
            for c in range(8):
                P.op("act", lambda E, c=c, sq_=sq_: E.activation(out=sq_[:, c, 0:n], in_=src(c), func=AF.Square),
                     reads=[srckey(c)], writes=[("lnsq" + tg, c)])
            b1, k1 = bank("ln")
            b2, k2 = bank("ln")
            mm_group(b1[:, 0:n], k1, [(onesM, xb_[:, c, 0:n], ["cst", xbk]) for c in range(8)])
            mm_group(b2[:, 0:n], k2, [(onesM, sq_[:, c, 0:n], ["cst", ("lnsq" + tg, c)]) for c in range(8)])
            return (b1, k1, b2, k2)

        def ln_front_b_steps(st_, n, S=None):
            S = S or LNS
            m2_, rs_, tg = S["m2"], S["rs"], S["tag"]
            km, kr = "lnm2" + tg, "lnrs" + tg
            b1, k1, b2, k2 = st_
            return [
                lambda: P.op("act", lambda E: E.activation(out=m2_[:, 0:n], in_=b1[:, 0:n], func=AF.Square), reads=[k1], writes=[km]),
                lambda: P.op("dve", lambda E: E.tensor_tensor(out=rs_[:, 0:n], in0=b2[:, 0:n], in1=m2_[:, 0:n], op=ALU.subtract),
                             reads=[k2, km], writes=[kr]),
                lambda: P.op("act", lambda E: E.activation(out=rs_[:, 0:n], in_=rs_[:, 0:n], func=AF.Ln, bias=epst[:, 0:1], scale=1.0),
                             reads=[kr, "eps"], writes=[kr]),
                lambda: P.op("act", lambda E: E.activation(out=rs_[:, 0:n], in_=rs_[:, 0:n], func=AF.Exp, scale=-0.5),
                             reads=[kr], writes=[kr]),
                lambda: P.op("dve", lambda E: E.tensor_copy(out=b2[:, 0:n], in_=rs_[:, 0:n]), reads=[kr], writes=[k2]),
                lambda: P.op("dve", lambda E: E.tensor_tensor(out=b1[:, 0:n], in0=b1[:, 0:n], in1=rs_[:, 0:n], op=ALU.mult),
                             reads=[k1, kr], writes=[k1]),
            ]

        def ln_front_b(st_, n, S=None):
            for f_ in ln_front_b_steps(st_, n, S):
                f_()

        def ln_back(st_, src, srckey, n, gi, out_f32=None, out_bf=None, okeys=(), bfkeys=None, inter=None, S=None, f32_affine=None):
            S = S or LNS
            fsc = (lambda c: vec[:, gi, c:c + 1]) if f32_affine is None else (lambda c: f32_affine[:, 0, c:c + 1])
            fbi = (lambda c: vec[:, gi + 1, c:c + 1]) if f32_affine is None else (lambda c: f32_affine[:, 1, c:c + 1])
            tmps_, tg = S["tmps"], S["tag"]
            b1, k1, b2, k2 = st_
            bfkeys = list(bfkeys)
            inter = list(inter) if inter else []
            for c in range(8):
                if c >= 1 and inter:
                    inter.pop(0)()
                tmp = tmps_[c % 2]
                tk = ("lntmp" + tg, c % 2)
                P.op("dve", lambda E, c=c, tmp=tmp: E.tensor_tensor(out=tmp[:, 0:n], in0=src(c), in1=b2[:, 0:n], op=ALU.mult),
                     reads=[srckey(c), k2], writes=[tk])
                P.op("dve", lambda E, c=c, tmp=tmp: E.tensor_tensor(out=tmp[:, 0:n], in0=tmp[:, 0:n], in1=b1[:, 0:n], op=ALU.subtract),
                     reads=[tk, k1], writes=[tk])
                if out_f32 is not None:
                    P.op("act", lambda E, c=c, tmp=tmp: E.activation(out=out_f32(c), in_=tmp[:, 0:n], func=AF.Identity,
                                                                     bias=fbi(c), scale=fsc(c)),
                         reads=[tk, "vec", "vecA"], writes=[okeys(c)])
                    if out_bf is not None:
                        if c % 2 == 0:
                            P.op("dve", lambda E, c=c, tmp=tmp: E.tensor_scalar(out=out_bf(c), in0=tmp[:, 0:n], scalar1=vec[:, gi, c:c + 1], scalar2=vec[:, gi + 1, c:c + 1],
                                                                                op0=ALU.mult, op1=ALU.add),
                                 reads=[tk, "vec"], writes=bfkeys)
                        else:
                            P.op("act", lambda E, c=c, tmp=tmp: E.activation(out=out_bf(c), in_=tmp[:, 0:n], func=AF.Identity,
                                                                             bias=vec[:, gi + 1, c:c + 1], scale=vec[:, gi, c:c + 1]),
                                 reads=[tk, "vec"], writes=bfkeys)
                else:
                    P.op("act", lambda E, c=c, tmp=tmp: E.activation(out=out_bf(c), in_=tmp[:, 0:n], func=AF.Identity,
                                                                     bias=vec[:, gi + 1, c:c + 1], scale=vec[:, gi, c:c + 1]),
                         reads=[tk, "vec"], writes=bfkeys)
            for f_ in inter:
                f_()

        with ExitStack() as sM:
            yT = sb([128, 4, NT], BF16, sM)
            with ExitStack() as s1:
                E8a = sb([128, G, 128], BF16, s1)
                E8b = sb([128, G, 128], BF16, s1)
                M0 = sb([128, G, 128], BF16, s1)
                FC = sb([128, G, 128], BF16, s1)
                R8 = sb([128, G], F32, s1)
                q8 = sb([128, G], F32, s1)
                FC2 = V(wringT, 0, [[128, G], [1, 128]])
                RA.reset()
                ra = RA.alloc
                lm = ra([128, 3, G], F32)
                P.dma("sp", lm, lam, writes=["lm"])
                bcin = ra([128, 4, G, 16], F32)
                P.dma("sp", bcin, bc_in, writes=["bcin"])
                cm = ra([128, 128], F32)
                P.dma("sp", cm, cmask, writes=["cm"])
                step = ra([128, G], F32); xx = ra([128, G], F32); th = ra([128, G], F32); q0 = ra([128, G], F32)
                NE = 16
                pw = ra([128, 2, NE, G], F32)
                ang = ra([128, 2, NE, G], F32)
                mag = ra([128, NE, G], F32)
                ti = ra([128, 2, NE, G], I32)
                tf = ra([128, 2, NE, G], F32)
                K = ["tbl"]
                KR = K + ["lm", "bcin"]
                P.op("act", lambda E: E.activation(out=step, in_=lm[:, 2, :], func=AF.Exp), reads=KR, writes=K)
                P.op("dve", lambda E: E.tensor_tensor(out=xx, in0=lm[:, 0, :], in1=step, op=ALU.mult), reads=KR, writes=K)
                P.op("dve", lambda E: E.tensor_tensor(out=th, in0=lm[:, 1, :], in1=step, op=ALU.mult), reads=KR, writes=K)
                INV2PI = 1.0 / (2.0 * math.pi)

                def reduce_(dst, src_ap, tiv, tfv):
                    P.op("dve", lambda E: E.tensor_copy(out=tiv, in_=src_ap), reads=K, writes=K)
                    P.op("dve", lambda E: E.tensor_copy(out=tfv, in_=tiv), reads=K, writes=K)
                    P.op("dve", lambda E: E.tensor_tensor(out=dst, in0=src_ap, in1=tfv, op=ALU.subtract), reads=K, writes=K)

                P.op("dve", lambda E: E.tensor_scalar(out=q0, in0=th, scalar1=INV2PI, scalar2=None, op0=ALU.mult), reads=K, writes=K)
                reduce_(q0, q0, ti[:, 0, 0, :], tf[:, 0, 0, :])
                for e in range(NE):
                    ex = float(e - 7)
                    P.op("dve", lambda E, e=e, ex=ex: E.tensor_scalar(out=ang[:, 1, e, :], in0=q0, scalar1=ex, scalar2=None, op0=ALU.mult), reads=K, writes=K)
                    P.op("dve", lambda E, e=e, ex=ex: E.tensor_scalar(out=ang[:, 0, e, :], in0=q0, scalar1=ex, scalar2=0.25, op0=ALU.mult, op1=ALU.add), reads=K, writes=K)
                    P.op("act", lambda E, e=e, ex=ex: E.activation(out=mag[:, e, :], in_=xx, func=AF.Exp, scale=ex), reads=K, writes=K)
                reduce_(ang, ang, ti, tf)
                P.op("act", lambda E: E.activation(out=pw, in_=ang, func=AF.Sin, scale=6.28318), reads=K, writes=K)
                for ri in range(2):
                    P.op("dve", lambda E, ri=ri: E.tensor_tensor(out=pw[:, ri], in0=pw[:, ri], in1=mag, op=ALU.mult), reads=K, writes=K)
                P.op("dve", lambda E: E.tensor_copy(out=R8[:], in_=mag[:, 15, :]), reads=K, writes=["A"])
                P.op("dve", lambda E: E.tensor_scalar(out=q8[:], in0=q0, scalar1=8.0, scalar2=None, op0=ALU.mult), reads=K, writes=["A"])
                P.op("dve", lambda E: E.tensor_copy(out=ti[:, 0, 0, :], in_=q8[:]), reads=K + ["A"], writes=K)
                P.op("dve", lambda E: E.tensor_copy(out=tf[:, 0, 0, :], in_=ti[:, 0, 0, :]), reads=K, writes=K)
                P.op("dve", lambda E: E.tensor_tensor(out=q8[:], in0=q8[:], in1=tf[:, 0, 0, :], op=ALU.subtract), reads=K + ["A"], writes=["A"])
                ar, ai = pw[:, 0, 8, :], pw[:, 1, 8, :]
                t1 = ra([128, G], F32); t2 = ra([128, G], F32); den = ra([128, G], F32)
                cr = ra([128, G], F32); ci = ra([128, G], F32); nr = ra([128, G], F32)
                lr, li = lm[:, 0, :], lm[:, 1, :]
                TT_ = lambda o, a, b, op: P.op("dve", lambda E: E.tensor_tensor(out=o, in0=a, in1=b, op=op), reads=KR, writes=K)
                P.op("dve", lambda E: E.tensor_scalar(out=nr, in0=ar, scalar1=-1.0, scalar2=None, op0=ALU.add), reads=K, writes=K)
                TT_(t1, lr, lr, ALU.mult); TT_(t2, li, li, ALU.mult); TT_(den, t1, t2, ALU.add)
                P.op("dve", lambda E: E.reciprocal(out=den, in_=den), reads=K, writes=K)
                TT_(t1, nr, lr, ALU.mult); TT_(t2, ai, li, ALU.mult); TT_(cr, t1, t2, ALU.add); TT_(cr, cr, den, ALU.mult)
                TT_(t1, ai, lr, ALU.mult); TT_(t2, nr, li, ALU.mult); TT_(ci, t1, t2, ALU.subtract); TT_(ci, ci, den, ALU.mult)
                bbr = ra([128, G, 16], F32); bbi = ra([128, G, 16], F32); tb = ra([128, G, 16], F32)
                bcg = lambda t: V(t, 0, [[1, G], [0, 16]])
                Br, Bi, Cr, Ci = (bcin[:, k] for k in range(4))
                TT_(bbr, Br, bcg(cr), ALU.mult); TT_(tb, Bi, bcg(ci), ALU.mult); TT_(bbr, bbr, tb, ALU.subtract)
                TT_(bbi, Bi, bcg(cr), ALU.mult); TT_(tb, Br, bcg(ci), ALU.mult); TT_(bbi, bbi, tb, ALU.add)
                X1 = ra([128, G, 16], F32); X2 = ra([128, G, 16], F32)
                Y1 = ra([128, G, 16], F32); Y2 = ra([128, G, 16], F32)
                Y3 = ra([128, G, 16], F32); Y4 = ra([128, G, 16], F32)
                cp = lambda o, a, s_: P.op("dve", lambda E: E.tensor_scalar(out=o, in0=a, scalar1=s_, scalar2=None, op0=ALU.mult), reads=KR, writes=K)
                cp(X1[0:64], bbr[0:64], 1.0); cp(X1[64:128], bbi[64:128], 1.0)
                cp(X2[0:64], bbi[0:64], -1.0); cp(X2[64:128], bbr[64:128], 1.0)
                cp(Y1[0:64], Cr[0:64], 1.0); cp(Y1[64:128], Ci[64:128], -1.0)
                cp(Y2[0:64], Ci[0:64], -1.0); cp(Y2[64:128], Cr[64:128], -1.0)
                cp(Y3[0:64], Ci[0:64], -1.0); cp(Y3[64:128], Cr[64:128], -1.0)
                cp(Y4[0:64], Cr[0:64], -1.0); cp(Y4[64:128], Ci[64:128], 1.0)
                EB = ra([128, G, 8, 16], BF16)
                FCp = ra([128, G, 8, 16], BF16)
                FCv = V(FC, 0, [[128, G], [16, 8], [1, 16]])
                FC2v = V(wringT, 0, [[128, G], [16, 8], [1, 16]])
                MT = Arena(mgR, 8 * NT * 2)
                ta = MT.alloc([128, G, 16], F32); tc = MT.alloc([128, G, 16], F32)

                def ctab(dst, e_idx, Xa, Xb):
                    prr = V(pw, (0 * NE + e_idx) * G, [[1, G], [0, 16]])
                    pii = V(pw, (1 * NE + e_idx) * G, [[1, G], [0, 16]])
                    TT_(ta, Xa, prr, ALU.mult); TT_(tc, Xb, pii, ALU.mult); TT_(dst, ta, tc, ALU.add)

                for j in range(8):
                    ctab(EB[:, :, j, :], (7 - j) + 7, X1, X2)
                    ctab(FCv[:, :, j, :], (j + 1) + 7, Y1, Y2)
                    ctab(FCp[:, :, j, :], (j - 7) + 7, Y1, Y2)
                    ctab(FC2v[:, :, j, :], (j + 1) + 7, Y3, Y4)
                for g4 in range(G // 4):
                    b, k = bank()
                    b2, k2 = bank()
                    for gi_ in range(4):
                        g = g4 * 4 + gi_
                        ebg = V(EB, g * 128, [[1, 128]])
                        P.op("pe", lambda E, b=b, gi_=gi_, ebg=ebg: E.matmul(b[:, gi_ * 128:(gi_ + 1) * 128], ebg, ident, start=True, stop=True),
                             reads=K + ["cst"], writes=[k], inc=(gi_ == 3))
                    for gi_ in range(4):
                        g = g4 * 4 + gi_
                        ebg = V(EB, g * 128, [[1, 128]])
                        fpg = V(FCp, g * 128, [[1, 128]])
                        P.op("pe", lambda E, b2=b2, gi_=gi_, ebg=ebg, fpg=fpg: E.matmul(b2[:, gi_ * 128:(gi_ + 1) * 128], ebg, fpg, start=True, stop=True),
                             reads=K, writes=[k2], inc=(gi_ == 3))
                    bv = b[:, :].rearrange("p (a c) -> p a c", a=4)
                    gs = slice(g4 * 4, g4 * 4 + 4)
                    P.op("act", lambda E, bv=bv, gs=gs: E.activation(out=E8a[:, gs, :], in_=bv, func=AF.Copy), reads=[k], writes=["E8a"])
                    P.op("act", lambda E, bv=bv, gs=gs: E.activation(out=E8b[:, gs, 0:64], in_=bv[:, :, 64:128], func=AF.Copy, scale=-1.0), reads=[k], writes=["E8b"])
                    P.op("act", lambda E, bv=bv, gs=gs: E.activation(out=E8b[:, gs, 64:128], in_=bv[:, :, 0:64], func=AF.Copy), reads=[k], writes=["E8b"])
                    P.op("dve", lambda E, b2=b2, gs=gs: E.tensor_tensor(out=M0[:, gs, :], in0=b2[:, :].rearrange("p (a c) -> p a c", a=4),
                                                                       in1=V(cm, 0, [[0, 4], [1, 128]]), op=ALU.mult), reads=[k2, "cm"], writes=["M0"])
                P.barrier()
                usT = sb([128, 4, NA], BF16, s1)
                rowc = sb([128, 64], F32, s1)
                P.dma("sp", rowc[:], rowc_d, writes=["rot"])
                sP = ExitStack()
                selP = sb([128, 64, 128], BF16, sP)
                P.dma("pool", selP[:], selP_d, writes=["selP"])
                RA.reset()
                MI = Arena(mgR, 8 * NT * 2)
                MI.off = 8 * 1024 * 3
                hbt = xb
                MI.off = 24 * 1024
                Wssm = MI.alloc([128, 8, 512], BF16)
                P.dma("pool", Wssm, w_in[:, 2048:2560].rearrange("(kc p) n -> p kc n", p=128), writes=["Wssm"])
                NXB = 3
                xt = [ra([128, 8, TT], F32) for _ in range(NXB)]
                def xload(it):
                    P.dma("sp", xt[it % NXB], xT[:, it * TT:(it + 1) * TT].rearrange("(c p) t -> p c t", p=128), writes=[("xt", it % NXB, c) for c in range(8)])

                def xsrc(it):
                    x_t = xt[it % NXB]
                    return (lambda c: x_t[:, c, :])

                NTI = NA // TT
                xdram = lambda it: xT[:, it * TT:(it + 1) * TT].rearrange("(c p) t -> p c t", p=128)
                xload(0)
                split_banks(True)
                xkf = lambda it: (lambda c: ("xt", it % NXB, c))
                xbs = [ra([128, 8, TT], BF16) for _ in range(2)]
                SIN_ = [dict(LNS, xb=xbs[i], xbkey="lnxbI%d" % i) for i in range(2)]

                def xbdma(it):
                    P.dma("pool", xbs[it % 2], xdram(it), writes=["lnxbI%d" % (it % 2)])

                xbdma(0)
                xbdma(1)
                xload(1)
                fr = {0: ln_front_a(xsrc(0), xkf(0), TT, None, SIN_[0])}
                ln_front_b(fr[0], TT)

                def ssm_in(it):
                    own_ = it >= 4
                    for m in range(4):
                        b, k = bank()
                        mm_group(b[:, :], k, [(Wssm[:, kc, m * 128:(m + 1) * 128], hbt[:, kc, :], ["Wssm", "hbt"]) for kc in range(8)])
                        dst = V(usT, m * NA + it * 64, [[512, 8], [1, 64]])
                        srcv = V(b, 0, [[1, 8], [8, 64]])
                        if own_:
                            P.op("dve", lambda E, dst=dst, srcv=srcv: E.tensor_copy(out=dst, in_=srcv),
                                 reads=[k], writes=["usT"])
                        else:
                            P.op("dve", lambda E, dst=dst, srcv=srcv: E.tensor_scalar(out=dst, in0=srcv, scalar1=mk[:, 0:1], scalar2=None, op0=ALU.mult),
                                 reads=[k, "mk"], writes=["usT"])

                for it in range(NTI):
                    x_t = xt[it % NXB]
                    if it + 2 < NTI:
                        xload(it + 2)
                    if it + 1 < NTI:
                        fr[it + 1] = ln_front_a(xsrc(it + 1), xkf(it + 1), TT, None, SIN_[(it + 1) % 2])
                        if it + 2 < NTI:
                            xbdma(it + 2)
                    own = it >= 4
                    o0 = (it - 4) * TT
                    if own:
                        ln_back(fr[it], xsrc(it), xkf(it), TT, 0, out_f32=xsrc(it), out_bf=lambda c: hbt[:, c, :], okeys=xkf(it), bfkeys=["hbt"],
                                inter=(ln_front_b_steps(fr[it + 1], TT) if it + 1 < NTI else None))
                        P.dma("act", h0s[:, o0:o0 + TT].rearrange("(c p) t -> p c t", p=128), x_t, reads=[xkf(it)(c) for c in range(8)], writes=["h0s"])
                    else:
                        ln_back(fr[it], xsrc(it), xkf(it), TT, 0, out_bf=lambda c: hbt[:, c, :], okeys=None, bfkeys=["hbt"],
                                inter=(ln_front_b_steps(fr[it + 1], TT) if it + 1 < NTI else None))
                        if it == 3:
                            P.op("act", lambda E: E.activation(out=hh[:], in_=hbt[:, :, TT - 32:TT], func=AF.Copy), reads=["hbt"], writes=["hh"])
                    ssm_in(it)
                split_banks(False)
                P.barrier()
                RA.reset()
                Up = ra([128, G, 512], BF16)
                Yp = ra([128, G, 256], BF16)
                NB = 64
                SAq = ra([128, G, NB], F32)
                SBq = ra([128, G, NB], F32)
                MB = Arena(mgR, 8 * NT * 2)
                Wq = MB.alloc([128, G, NB], F32)
                M8 = MB.alloc([128, G, NB], F32)
                Wsc = Wq
                cosT = MB.alloc([128, G, NB], F32)
                sinT = MB.alloc([128, G, NB], F32)
                tiq = SAq.bitcast(I32)
                KT_ = ["rot"]
                P.op("dve", lambda E: E.tensor_tensor(out=sinT, in0=V(q8, 0, [[1, G], [0, NB]]), in1=V(rowc, 0, [[0, G], [1, NB]]), op=ALU.mult),
                     reads=KT_ + ["A"], writes=KT_)
                P.op("dve", lambda E: E.tensor_scalar(out=cosT, in0=sinT, scalar1=0.25, scalar2=None, op0=ALU.add), reads=KT_, writes=KT_)
                for T_ in (sinT, cosT):
                    P.op("dve", lambda E, T_=T_: E.tensor_copy(out=tiq, in_=T_), reads=KT_, writes=KT_)
                    P.op("dve", lambda E, T_=T_: E.tensor_copy(out=Wq, in_=tiq), reads=KT_, writes=KT_)
                    P.op("dve", lambda E, T_=T_: E.tensor_tensor(out=T_, in0=T_, in1=Wq, op=ALU.subtract), reads=KT_, writes=KT_)
                    P.op("act", lambda E, T_=T_: E.activation(out=T_, in_=T_, func=AF.Sin, scale=6.28318), reads=KT_, writes=KT_)
                if True:
                    for g in range(G):
                        q, gl = g // 8, g % 8
                        b, k = bank()
                        mm_group(b[:, :], k, [(selP[:, gl * 8 + j, :], V(usT, q * NA + j * 512, [[1, 512]]), ["selP", "usT"]) for j in range(8)])
                        P.op("act", lambda E, b=b, g=g: E.activation(out=Up[:, g, :], in_=b[:, :], func=AF.Copy), reads=[k], writes=["Up"])
                    P.barrier()
                    sP.close()
                with ExitStack() as s2:
                    SINc = sb([128, G, NB], BF16, s2)
                    SINs = sb([128, G, NB], BF16, s2)
                    jm = sb([128, 128], F32, s2)
                    P.dma("sp", jm[:], jmat_d, writes=["jm"])
                    carry = sb([128, G], F32, s2)
                    wl = sb([128, G], F32, s2)
                    c1 = sb([128, G], F32, s2)
                    c2_ = sb([128, G], F32, s2)
                    KT_ = ["rot"]
                    selUa = SAq.bitcast(BF16).rearrange("p g (a b) -> p (g a) b", a=1) if False else V(SAq.bitcast(BF16), 0, [[128, 32], [1, 128]])
                    selUb = V(SBq.bitcast(BF16), 0, [[128, 32], [1, 128]])
                    P.dma("pool", selUa, selU_d[:, 0:32, :], reads=KT_, writes=["selUa"])
                    P.op("dve", lambda E: E.memset(carry[:], 0.0), writes=["carry"])
                    P.op("dve", lambda E: E.tensor_copy(out=M8, in_=V(R8, 0, [[1, G], [0, NB]])), reads=["A"], writes=["M8"])
                    P.op("dve", lambda E: E.memset(V(M8, 0, [[NB, G]]), 0.0), reads=["M8"], writes=["M8"])
                    TTd = lambda o, x, y, op, rk, wk: P.op("dve", lambda E: E.tensor_tensor(out=o, in0=x, in1=y, op=op), reads=rk, writes=wk)
                    NE8 = 512 // NB

                    def s8_rotin(e8):
                        for (Et, dst, tab, nm) in ((E8a, Wq, cosT, "Wq"), (E8b, SBq, sinT, "SB")):
                            for j4 in range(4):
                                b, k = bank()
                                for gl in range(8):
                                    g = j4 * 8 + gl
                                    P.op("pe", lambda E, b=b, g=g, gl=gl, Et=Et, e8=e8: E.matmul(b[:, gl * NB:(gl + 1) * NB], Et[:, g, :], Up[:, g, e8 * NB:(e8 + 1) * NB], start=True, stop=True),
                                         reads=["E8a", "E8b", "Up"], writes=[k], inc=(gl == 7))
                                P.op("dve", lambda E, b=b, j4=j4, dst=dst, tab=tab: E.tensor_tensor(out=dst[:, j4 * 8:(j4 + 1) * 8, :], in0=b[:, :].rearrange("p (a b) -> p a b", a=8),
                                                                                                   in1=tab[:, j4 * 8:(j4 + 1) * 8, :], op=ALU.mult),
                                     reads=[k] + KT_, writes=[nm])
                        TTd(Wq, Wq, SBq, ALU.subtract, ["Wq", "SB"], ["Wq"])

                    s8_rotin(0)
                    for e8 in range(NE8):
                        own = e8 >= 256 // NB
                        if own:
                            P.op("act", lambda E: E.activation(out=V(SINc, 0, [[NB, G]]), in_=carry[:], func=AF.Copy), reads=["carry"], writes=["SINc"])
                            P.op("dve", lambda E: E.memset(V(SINs, 0, [[NB, G]]), 0.0), writes=["SINs"])
                        TTd(c1[:], R8[:], carry[:], ALU.mult, ["A", "carry"], ["c1"])
                        TTd(V(Wq, 0, [[NB, G]]), V(Wq, 0, [[NB, G]]), c1[:], ALU.add, ["Wq", "c1"], ["Wq"])
                        P.op("dve", lambda E: E.tensor_tensor_scan(out=V(Wq, 0, [[1, G * NB]]), data0=V(M8, 0, [[1, G * NB]]), data1=V(Wq, 0, [[1, G * NB]]), initial=0.0, op0=ALU.mult, op1=ALU.add),
                             reads=["Wq", "M8"], writes=["Wq"])
                        if own:
                            TTd(SINc[:, :, 1:NB], cosT[:, :, 0:NB - 1], Wsc[:, :, 0:NB - 1], ALU.mult, KT_ + ["Wq"], ["SINc"])
                            TTd(SINs[:, :, 1:NB], sinT[:, :, 0:NB - 1], Wsc[:, :, 0:NB - 1], ALU.mult, KT_ + ["Wq"], ["SINs"])
                        P.op("dve", lambda E: E.tensor_copy(out=wl[:], in_=V(Wsc, NB - 1, [[NB, G]])), reads=["Wq"], writes=["wl"])
                        bj, kj = bank()
                        mm_group(bj[:, 0:G], kj, [(jm[:], wl[:], ["jm", "wl"])])
                        TTd(c1[:], V(cosT, NB - 1, [[NB, G]]), wl[:], ALU.mult, KT_ + ["wl"], ["c1"])
                        TTd(c2_[:], V(sinT, NB - 1, [[NB, G]]), bj[:, 0:G], ALU.mult, KT_ + [kj], ["c2_"])
                        TTd(carry[:], c1[:], c2_[:], ALU.add, ["c1", "c2_"], ["carry"])
                        if e8 + 1 < NE8:
                            s8_rotin(e8 + 1)
                        if own:
                            o0 = (e8 - 256 // NB) * NB
                            for j4 in range(4):
                                b, k = bank()
                                for gl in range(8):
                                    g = j4 * 8 + gl
                                    pso = b[:, gl * NB:(gl + 1) * NB]
                                    parts = [(M0[:, g, :], Up[:, g, e8 * NB:(e8 + 1) * NB], ["M0", "Up"]),
                                             (FC[:, g, :], SINc[:, g, :], ["tbl", "SINc"]),
                                             (FC2[:, g, :], SINs[:, g, :], ["tbl", "SINs"])]
                                    for i_, (l_, r_, rk_) in enumerate(parts):
                                        P.op("pe", lambda E, pso=pso, l_=l_, r_=r_, i_=i_: E.matmul(pso, l_, r_, start=(i_ == 0), stop=(i_ == 2)),
                                             reads=rk_, writes=[k], inc=(gl == 7 and i_ == 2))
                                P.op("act", lambda E, b=b, j4=j4, o0=o0: E.activation(out=Yp[:, j4 * 8:(j4 + 1) * 8, o0:o0 + NB], in_=b[:, :].rearrange("p (a b) -> p a b", a=8), func=AF.Copy),
                                     reads=[k], writes=["Yp"])
                    P.dma("pool", selUb, selU_d[:, 32:64, :], reads=["Wq"], writes=["SB", "selUb"])
                    for i in range(8):
                        sel_, sk_ = (selUa, "selUa") if i < 4 else (selUb, "selUb")
                        for q in range(4):
                            b, k = bank()
                            mm_group(b[:, 0:256], k, [(sel_[:, (i % 4) * 8 + gl, :], Yp[:, q * 8 + gl, :], [sk_, "Yp"]) for gl in range(8)])
                            P.op("dve", lambda E, b=b, q=q, i=i: E.scalar_tensor_tensor(
                                out=V(yT, q * NT + i, [[8, 256]]), in0=V(usT, q * NA + i * 512 + 256, [[1, 256]]), scalar=dv[:, q:q + 1],
                                in1=b[:, 0:256], op0=ALU.mult, op1=ALU.add), reads=[k, "usT", "dv"], writes=["yT"])
                    P.barrier()

            h0b = sb([128, 8, NT], BF16, sM)
            for n_ in range(4):
                P.dma("pool", h0b[:, :, n_ * TT:(n_ + 1) * TT], h0s[:, n_ * TT:(n_ + 1) * TT].rearrange("(c p) t -> p c t", p=128),
                      reads=["h0s"], writes=[("h0b", n_)])
            c2 = sb([128, 8, NT], BF16, sM)
            RA.reset()
            u = ra([128, 8, 32 + NT], BF16)
            dw = ra([128, 8, 31], F32)
            P.dma("sp", dw, dwT, writes=["dw"])
            idf = ra([128, 128], F32)
            P.op("act", lambda E: E.activation(out=idf, in_=ident, func=AF.Copy), reads=["cst"], writes=["idf"])
            acc1 = ra([128, 4, 512], F32)
            acc2 = ra([128, 4, 512], F32)
            cf = ra([128, 512], F32)
            cfs = [cf, ra([128, 512], F32)]
            MC = Arena(mgR, 8 * NT * 2)
            dgs = [MC.alloc([128, 31, 128], BF16) for _ in range(2)]
            sgs = [MC.alloc([128, 512], F32) for _ in range(2)]
            csqs = [MC.alloc([128, 512], BF16) for _ in range(2)]
            MCK = ["dg0", "dg1", "sg0", "sg1", "csq0", "csq1"]
            vg_i = [0]

            def VG(q):
                wv, kv_ = wload(w_in, 0, 8, q * 128)
                wg, kg_ = wload(w_in, 0, 8, 1024 + q * 128)
                tiles = [(None, 32)] + [(n * TT, TT) for n in range(4)]
                for (t0, n) in tiles:
                    rhs = (lambda kc: hh[:, kc, :]) if t0 is None else (lambda kc, t0=t0: h0b[:, kc, t0:t0 + TT])
                    rk = "hh" if t0 is None else ("h0b", t0 // TT)
                    sg = sgs[vg_i[0] % 2]
                    sk = "sg%d" % (vg_i[0] % 2)
                    vg_i[0] += 1
                    bv, kbv = bank()
                    bg, kbg = bank()
                    mm_group(bv[:, 0:n], kbv, [(wv(kc, 0, 128), rhs(kc), [kv_, rk]) for kc in range(8)])
                    mm_group(bg[:, 0:n], kbg, [(wg(kc, 0, 128), rhs(kc), [kg_, rk]) for kc in range(8)])
                    P.op("act", lambda E, bg=bg, n=n, sg=sg: E.activation(out=sg[:, 0:n], in_=bg[:, 0:n], func=AF.Sigmoid), reads=[kbg], writes=[sk])
                    if t0 is None:
                        P.op("dve", lambda E, bv=bv, q=q, sg=sg: E.scalar_tensor_tensor(out=u[:, q, 0:32], in0=sg[:, 0:32], scalar=mk[:, 0:1], in1=bv[:, 0:32],
                                                                                    op0=ALU.mult, op1=ALU.mult), reads=[kbv, sk, "mk"], writes=[("u", q)])
                    else:
                        P.op("dve", lambda E, bv=bv, t0=t0, q=q, sg=sg: E.tensor_tensor(out=u[:, q, 32 + t0:32 + t0 + TT], in0=bv[:, :], in1=sg, op=ALU.mult),
                             reads=[kbv, sk], writes=[("u", q)])
                dg = dgs[q % 2]
                for k in range(31):
                    P.op("dve", lambda E, k=k, q=q, dg=dg: E.tensor_scalar(out=dg[:, k, :], in0=idf, scalar1=dw[:, q, k:k + 1], scalar2=None, op0=ALU.mult),
                         reads=["idf", "dw"], writes=["dg%d" % (q % 2)])

            def CONV(q, after_n=None):
                dg = dgs[q % 2]
                dk = "dg%d" % (q % 2)
                for n in range(4):
                    if after_n is not None and n >= 1:
                        after_n(n - 1)
                    csq = csqs[n % 2]
                    ck = "csq%d" % (n % 2)
                    b, kb = bank()
                    mm_group(b[:, :], kb, [(dg[:, k, :], u[:, q, 2 + n * TT + k: 2 + n * TT + k + TT], [dk, ("u", q)]) for k in range(31)])
                    P.op("act", lambda E, b=b, q=q, n=n: E.activation(out=c2[:, q, n * TT:(n + 1) * TT], in_=b[:, :], func=AF.Identity, bias=vec[:, 10, q:q + 1], scale=1.0),
                         reads=[kb, "vec"], writes=[("c2", n, q)])
                    P.op("act", lambda E, b=b, q=q, csq=csq: E.activation(out=csq, in_=b[:, :], func=AF.Square, bias=vec[:, 10, q:q + 1], scale=1.0),
                         reads=[kb, "vec"], writes=[ck])
                    b1, k1 = bank()
                    b2, k2 = bank()
                    mm_group(b1[:, :], k1, [(onesM, c2[:, q, n * TT:(n + 1) * TT], ["cst", ("c2", n, q)])])
                    mm_group(b2[:, :], k2, [(onesM, csq, ["cst", ck])])
                    if q == 0:
                        P.op("dve", lambda E, b1=b1, n=n: E.tensor_copy(out=acc1[:, n, :], in_=b1[:, :]), reads=[k1], writes=[("acc1", n)])
                        P.op("dve", lambda E, b2=b2, n=n: E.tensor_copy(out=acc2[:, n, :], in_=b2[:, :]), reads=[k2], writes=[("acc2", n)])
                    else:
                        P.op("dve", lambda E, b1=b1, n=n: E.tensor_tensor(out=acc1[:, n, :], in0=b1[:, :], in1=acc1[:, n, :], op=ALU.add), reads=[k1, ("acc1", n)], writes=[("acc1", n)])
                        P.op("dve", lambda E, b2=b2, n=n: E.tensor_tensor(out=acc2[:, n, :], in0=b2[:, :], in1=acc2[:, n, :], op=ALU.add), reads=[k2, ("acc2", n)], writes=[("acc2", n)])
                if after_n is not None:
                    after_n(3)

            def apply_n(n):
                a1, a2 = acc1[:, n, :], acc2[:, n, :]
                Kc = [("acc1", n), ("acc2", n)]
                pr, kpr = bank()
                pm, kpm = bank()
                P.op("act", lambda E, a1=a1: E.activation(out=cf, in_=a1, func=AF.Square), reads=Kc, writes=["cf0"])
                P.op("dve", lambda E, a2=a2: E.tensor_tensor(out=a2, in0=a2, in1=cf, op=ALU.subtract), reads=Kc + ["cf0"], writes=Kc)
                P.op("act", lambda E, a2=a2: E.activation(out=a2, in_=a2, func=AF.Ln, bias=epst[:, 0:1], scale=1.0), reads=Kc + ["eps"], writes=Kc)
                P.op("act", lambda E, a2=a2: E.activation(out=a2, in_=a2, func=AF.Exp, scale=-0.5), reads=Kc, writes=Kc)
                P.op("dve", lambda E, a2=a2, pr=pr: E.tensor_copy(out=pr[:, :], in_=a2), reads=Kc, writes=[kpr])
                P.op("dve", lambda E, a1=a1, a2=a2, pm=pm: E.tensor_tensor(out=pm[:, :], in0=a1, in1=a2, op=ALU.mult), reads=Kc, writes=[kpm])
                for q in range(8):
                    cs = c2[:, q, n * TT:(n + 1) * TT]
                    cfq = cfs[q % 2]
                    ck = "cf%d" % (q % 2)
                    P.op("dve", lambda E, cs=cs, pr=pr, cfq=cfq: E.tensor_tensor(out=cfq, in0=cs, in1=pr[:, :], op=ALU.mult), reads=[kpr, ("c2", n, q)], writes=[ck])
                    P.op("dve", lambda E, pm=pm, cfq=cfq: E.tensor_tensor(out=cfq, in0=cfq, in1=pm[:, :], op=ALU.subtract), reads=[kpm, ck], writes=[ck])
                    P.op("act", lambda E, cs=cs, cfq=cfq, q=q: E.activation(out=cs, in_=cfq, func=AF.Silu, bias=vec[:, 3, q:q + 1], scale=vec[:, 2, q:q + 1]),
                         reads=[ck, "vec"], writes=[("c2", n, q)])

            VG(0)
            for q in range(8):
                if q + 1 < 8:
                    VG(q + 1)
                CONV(q, after_n=(apply_n if q == 7 else None))
            if DBG_BAR:
                P.barrier()
            wext = sb([128, 4 * 1024], BF16, sM)
            ring_bufs.extend([V(wext, i * 1024, [[1, 1024]]) for i in range(4)])
            sA = ra([128, 512], F32); sBt = ra([128, 512], F32); sC = ra([128, 512], F32)
            tA = ra([128, 512], F32); tB = ra([128, 512], F32)
            for m in range(8):
                wga, kga = wload(w_in, 0, 8, 2560 + m * 128)
                wgb, kgb = wload(w_in, 0, 8, 3584 + m * 128)
                wco, kco = wload(w_conv_out, 0, 8, m * 128)
                wzv, kzv = wload(w_glu, 0, 4, m * 128)
                wzg, kzg = wload(w_glu, 0, 4, 1024 + m * 128)
                for n in range(4):
                    ts = slice(n * TT, (n + 1) * TT)
                    b1, k1 = bank(); b2, k2 = bank(); b3, k3 = bank(); b4, k4 = bank(); b5, k5 = bank()
                    mm_group(b1[:, :], k1, [(wga(kc, 0, 128), h0b[:, kc, ts], [kga, ("h0b", n)]) for kc in range(8)])
                    mm_group(b2[:, :], k2, [(wco(kc, 0, 128), c2[:, kc, ts], [kco, ("c2", n, kc)]) for kc in range(8)])
                    mm_group(b3[:, :], k3, [(wgb(kc, 0, 128), h0b[:, kc, ts], [kgb, ("h0b", n)]) for kc in range(8)])
                    mm_group(b4[:, :], k4, [(wzv(kc, 0, 128), yT[:, kc, ts], [kzv, "yT"]) for kc in range(4)])
                    mm_group(b5[:, :], k5, [(wzg(kc, 0, 128), yT[:, kc, ts], [kzg, "yT"]) for kc in range(4)])
                    P.op("act", lambda E, b1=b1: E.activation(out=sA, in_=b1[:, :], func=AF.Sigmoid), reads=[k1], writes=["sA"])
                    P.op("act", lambda E, b3=b3: E.activation(out=sBt, in_=b3[:, :], func=AF.Sigmoid), reads=[k3], writes=["sB"])
                    P.op("act", lambda E, b5=b5: E.activation(out=sC, in_=b5[:, :], func=AF.Sigmoid), reads=[k5], writes=["sC"])
                    P.op("dve", lambda E, b2=b2: E.tensor_tensor(out=tA, in0=b2[:, :], in1=sA, op=ALU.mult), reads=[k2, "sA"], writes=["tA"])
                    P.op("dve", lambda E, b4=b4: E.tensor_tensor(out=tB, in0=b4[:, :], in1=sC, op=ALU.mult), reads=[k4, "sC"], writes=["tB"])
                    P.op("dve", lambda E: E.tensor_tensor(out=tB, in0=tB, in1=sBt, op=ALU.mult), reads=["tB", "sB"], writes=["tB"])
                    P.op("dve", lambda E, m=m, ts=ts: E.tensor_tensor(out=mg[:, m, ts], in0=tA, in1=tB, op=ALU.add), reads=["tA", "tB"], writes=["mg"] + MCK)
            del ring_bufs[NW:]
            P.barrier()
        hb = sb([128, 8, NT], BF16)
        wres = sb([128, 8, 1024], BF16)

        def wres_load(w_ap, r0):
            P.dma("pool", wres[:], w_ap[r0:r0 + 1024, :].rearrange("(kc p) n -> p kc n", p=128), writes=["wres", "wres7"])

        def proj_ln(emit_mm, gi, write_out=False, S=None, f32_affine=None):
            rsrc = lambda n: (lambda c: r[:, c, n * TT:(n + 1) * TT])
            rtile = lambda n: r[:, :, n * TT:(n + 1) * TT]
            rk = lambda n: (lambda c: ("r", n, c))
            A_ = lambda n: ln_front_a(rsrc(n), rk(n), TT, rtile(n), S)
            fr = {}
            split_banks(True)
            if emit_mm:
                emit_mm(0)
                emit_mm(1)
            fr[0] = A_(0)
            ln_front_b(fr[0], TT, S)
            for n in range(4):
                ts = slice(n * TT, (n + 1) * TT)
                if emit_mm and n + 2 < 4:
                    emit_mm(n + 2)
                if n + 1 < 4:
                    fr[n + 1] = A_(n + 1)
                ln_back(fr[n], rsrc(n), rk(n), TT, gi, out_f32=rsrc(n),
                        out_bf=(None if write_out else (lambda c, ts=ts: hb[:, c, ts])), okeys=rk(n), bfkeys=[("hb", n)],
                        inter=(ln_front_b_steps(fr[n + 1], TT, S) if n + 1 < 4 else None), S=S, f32_affine=f32_affine)
                if write_out:
                    P.dma("sp", outT[:, n * TT:(n + 1) * TT].rearrange("(c p) t -> p c t", p=128), r[:, :, ts], reads=[("r", n, c) for c in range(8)], writes=["outT"], is_output=True)
            split_banks(False)

        wres_load(w_mix, 0)
        hx_i = [0]

        def mix_mm(n):
            ts = slice(n * TT, (n + 1) * TT)
            for m in range(8):
                hxi = hx[hx_i[0] % 2]
                hk = ("hx", hx_i[0] % 2)
                hx_i[0] += 1
                P.dma("sp", hxi[:], h0s[m * 128:(m + 1) * 128, n * TT:(n + 1) * TT], reads=["h0s"], writes=[hk])
                b, k = bank()
                mm_group(b[:, :], k, [(wres[:, kc, m * 128:(m + 1) * 128], mg[:, kc, ts], ["wres", "wres7", "mg"]) for kc in range(8)])
                P.op("dve", lambda E, b=b, m=m, ts=ts, hxi=hxi: E.scalar_tensor_tensor(out=r[:, m, ts], in0=hxi[:], scalar=ALPHA, in1=b[:, :],
                                                                                    op0=ALU.mult, op1=ALU.add), reads=[k, hk], writes=[("r", n, m)])

        mT = sb([128, 8, 256], BF16)
        P.dma("pool", mT[:], memT.rearrange("(c p) t -> p c t", p=128), writes=["mT"])
        KT = sb([128, 8, 256], BF16)
        Vt = sb([128, 2, D], BF16)
        wvt = V(wringT, 0, [[512, 8], [1, 512]])
        wvk = [("w", i) for i in range(4)]

        def kv_proj():
            for m in range(8):
                wk_, kk_ = wload(wkv, 0, 8, m * 128)
                b, k = bank()
                mm_group(b[:, 0:256], k, [(wk_(kc, 0, 128), mT[:, kc, :], [kk_, "mT"]) for kc in range(8)])
                P.op("act", lambda E, b=b, m=m: E.activation(out=KT[:, m, :], in_=b[:, 0:256], func=AF.Copy), reads=[k], writes=["KT"])
            for db_ in range(2):
                P.dma("pool", wvt, wkv[:, 1024 + db_ * 512:1024 + (db_ + 1) * 512].rearrange("(kc p) n -> p kc n", p=128), writes=wvk)
                for mc in range(2):
                    b, k = bank()
                    mm_group(b[:, :], k, [(mT[:, kc, mc * 128:(mc + 1) * 128], wvt[:, kc, :], wvk + ["mT"]) for kc in range(8)])
                    P.op("act", lambda E, b=b, mc=mc, db_=db_: E.activation(out=Vt[:, mc, db_ * 512:(db_ + 1) * 512], in_=b[:, :], func=AF.Copy), reads=[k], writes=["Vt"])

        with ExitStack() as sL:
            hx = [sb([128, 512], F32, sL) for _ in range(2)]
            S1 = dict(xb=sb([128, 8, 512], BF16, sL), sq=sb([128, 8, 512], BF16, sL), m2=sb([128, 512], F32, sL), rs=sb([128, 512], F32, sL),
                      tmps=[sb([128, 512], F32, sL) for _ in range(2)], tag="B")
            kv_done = [False]

            def mix_mm2(n):
                mix_mm(n)
                if n == 3 and not kv_done[0]:
                    kv_done[0] = True
                    kv_proj()

            proj_ln(mix_mm2, 4, S=S1)
            P.dma("pool", wres[:, 0:7, :], wo[0:896, :].rearrange("(kc p) n -> p kc n", p=128), writes=["wres"])
            P.barrier()

        with ExitStack() as s1:
            oT = sb([128, 8, NT], BF16, s1)
            qh = sb([128, 1, 2, TT], BF16, s1)
            PT = sb([128, 2, 512], BF16, s1)
            rden = sb([128, 512], F32, s1)
            qh2 = V(wres, 7 * 1024, [[512, 2], [1, 512]])
            qbuf = [lambda dc: qh[:, 0, dc, :], lambda dc: qh2[:, dc, :]]
            items = [(h, n) for h in range(4) for n in range(4)]
            wqs_h = {}

            def stage_q(i):
                h, n = items[i]
                if h not in wqs_h:
                    wqs_h[h] = [wload(wq, 0, 8, (2 * h + dc) * 128) for dc in range(2)]
                ts = slice(n * TT, (n + 1) * TT)
                qb = i % 2
                for dc in range(2):
                    wq_, kq_ = wqs_h[h][dc]
                    b, k = bank()
                    mm_group(b[:, :], k, [(wq_(kc, 0, 128), hb[:, kc, ts], [kq_, ("hb", n)]) for kc in range(8)])
                    dst = qbuf[qb](dc)
                    P.op("act", lambda E, b=b, dst=dst: E.activation(out=dst, in_=b[:, :], func=AF.Copy), reads=[k], writes=[("qh", qb, dc)])

            def stage_att(i):
                h, n = items[i]
                ts = slice(n * TT, (n + 1) * TT)
                qb = i % 2
                for mc in range(2):
                    b, k = bank()
                    mm_group(b[:, :], k, [(KT[:, 2 * h + dc, mc * 128:(mc + 1) * 128], qbuf[qb](dc), ["KT", ("qh", qb, dc)]) for dc in range(2)])
                    P.op("act", lambda E, b=b, mc=mc: E.activation(out=PT[:, mc, :], in_=b[:, :], func=AF.Exp, scale=1.0 / 16.0), reads=[k], writes=[("PT", mc)])
                bd, kd = bank()
                mm_group(bd[:, :], kd, [(ones1, PT[:, mc, :], ["cst", ("PT", mc)]) for mc in range(2)])
                P.op("act", lambda E, bd=bd: E.activation(out=rden[:], in_=bd[:, :], func=AF.Ln), reads=[kd], writes=["rden"])
                P.op("act", lambda E: E.activation(out=rden[:], in_=rden[:], func=AF.Exp, scale=-1.0), reads=["rden"], writes=["rden"])
                for dc in range(2):
                    b, k = bank()
                    mm_group(b[:, :], k, [(Vt[:, mc, (2 * h + dc) * 128:(2 * h + dc + 1) * 128], PT[:, mc, :], ["Vt", ("PT", mc)]) for mc in range(2)])
                    P.op("dve", lambda E, b=b, h=h, dc=dc, ts=ts: E.tensor_tensor(out=oT[:, 2 * h + dc, ts], in0=b[:, :], in1=rden[:], op=ALU.mult),
                         reads=[k, "rden"], writes=["oT"])

            stage_q(0)
            for i in range(len(items)):
                if i + 1 < len(items):
                    stage_q(i + 1)
                stage_att(i)
            P.dma("pool", wres[:, 7:8, :], wo[896:1024, :].rearrange("(kc p) n -> p kc n", p=128), writes=["wres7", ("qh", 1, 0), ("qh", 1, 1)])

            def wo_mm(n):
                ts = slice(n * TT, (n + 1) * TT)
                for m in range(8):
                    b, k = bank()
                    mm_group(b[:, :], k, [(wres[:, kc, m * 128:(m + 1) * 128], oT[:, kc, ts], ["wres", "wres7", "oT"]) for kc in range(8)])
                    P.op("dve", lambda E, b=b, m=m, ts=ts: E.scalar_tensor_tensor(out=r[:, m, ts], in0=r[:, m, ts], scalar=ALPHA, in1=b[:, :],
                                                                               op0=ALU.mult, op1=ALU.add), reads=[k, ("r", n, m)], writes=[("r", n, m)])

            proj_ln(wo_mm, 6, f32_affine=vecA)
            P.barrier()

        with ExitStack() as s1:
            z = sb([128, 8, NT], BF16, s1)
            rl = sb([128, 512], F32, s1)
            wres_load(w_down, 3 * 1024)
            for hbk in range(4):
                for mm in range(8):
                    wu, ku = wload(w_up, 0, 8, hbk * 1024 + mm * 128)
                    for n in range(4):
                        ts = slice(n * TT, (n + 1) * TT)
                        b, k = bank()
                        mm_group(b[:, :], k, [(wu(kc, 0, 128), hb[:, kc, ts], [ku, ("hb", n)]) for kc in range(8)])
                        P.op("act", lambda E, b=b: E.activation(out=rl[:], in_=b[:, :], func=AF.Relu), reads=[k], writes=["rl"])
                        P.op("dve", lambda E, b=b, mm=mm, ts=ts: E.tensor_tensor(out=z[:, mm, ts], in0=b[:, :], in1=rl[:], op=ALU.mult), reads=[k, "rl"], writes=["z"])
                if hbk == 3:
                    break
                for m in range(8):
                    wd, kd = wload(w_down, hbk * 1024, 8, m * 128)
                    for n in range(4):
                        ts = slice(n * TT, (n + 1) * TT)
                        b, k = bank()
                        mm_group(b[:, :], k, [(wd(kc, 0, 128), z[:, kc, ts], [kd, "z"]) for kc in range(8)])
                        P.op("dve", lambda E, b=b, m=m, ts=ts: E.tensor_tensor(out=r[:, m, ts], in0=b[:, :], in1=r[:, m, ts], op=ALU.add), reads=[k, ("r", n, m)], writes=[("r", n, m)])

            def down_mm(n):
                ts = slice(n * TT, (n + 1) * TT)
                for m in range(8):
                    b, k = bank()
                    mm_group(b[:, :], k, [(wres[:, kc, m * 128:(m + 1) * 128], z[:, kc, ts], ["wres", "wres7", "z"]) for kc in range(8)])
                    P.op("dve", lambda E, b=b, m=m, ts=ts: E.tensor_tensor(out=r[:, m, ts], in0=b[:, :], in1=r[:, m, ts], op=ALU.add), reads=[k, ("r", n, m)], writes=[("r", n, m)])

            proj_ln(down_mm, 8, write_out=True)
            P.barrier()
        P.finish()
        with nc.Block() as block:
            P.run(block)
    return nc


_NC = [None]


def kernel(**inp):
    f = lambda a: np.ascontiguousarray(np.asarray(a, dtype=np.float32))
    x = f(inp["x"]); mem = f(inp["mem"])
    fm = lambda v: f(v).reshape(8, 128).T
    vecs = np.zeros((128, 12, 8), np.float32)
    for i, (gk, bk) in enumerate([("in_norm_g", "in_norm_b"), ("conv_norm_g", "conv_norm_b"), ("ln1_g", "ln1_b"), ("ln2_g", "ln2_b"), ("ln3_g", "ln3_b")]):
        vecs[:, 2 * i] = fm(np.asarray(inp[gk]).reshape(-1))
        vecs[:, 2 * i + 1] = fm(np.asarray(inp[bk]).reshape(-1))
    vecs[:, 10] = fm(np.asarray(inp["conv_db"]).reshape(-1))
    dwT = f(inp["conv_dw"])[0].T.reshape(8, 128, 31).transpose(1, 0, 2)
    dup = lambda a: np.concatenate([a, a], axis=0)
    lam = np.stack([dup(f(inp["ssm_lambda_re"])[0].T), dup(f(inp["ssm_lambda_im"])[0].T),
                    np.broadcast_to(f(inp["ssm_log_step"])[0][None, :], (128, G))], axis=1)
    Br = dup(f(inp["ssm_b_re"])[0].transpose(1, 0, 2)); Bi = dup(f(inp["ssm_b_im"])[0].transpose(1, 0, 2))
    Cr = dup(f(inp["ssm_c_re"])[0].transpose(2, 0, 1)); Ci = dup(f(inp["ssm_c_im"])[0].transpose(2, 0, 1))
    bc_in = np.stack([Br, Bi, Cr, Ci], axis=1)
    dvec = f(inp["ssm_d"])[0].reshape(4, 128).T
    consts = np.stack([np.full((128, 128), 1.0 / 1024, np.float32), np.ones((128, 128), np.float32), np.eye(128, dtype=np.float32)], axis=1)
    jl = np.arange(128) // 16
    cmask = (jl[None, :] >= jl[:, None]).astype(np.float32)
    selP = np.zeros((128, 64, 128), np.float32)
    selU = np.zeros((128, 64, 128), np.float32)
    for gl in range(8):
        for j in range(8):
            for h in range(16):
                selP[gl * 16 + h, gl * 8 + j, j * 16 + h] = 1.0
                selU[j * 16 + h, j * 8 + gl, gl * 16 + h] = 1.0
    rowc = np.broadcast_to(np.arange(1, 65, dtype=np.float32)[None, :], (128, 64)).copy()
    jmat = np.zeros((128, 128), np.float32)
    for m_ in range(64):
        jmat[m_ + 64, m_] = -1.0
        jmat[m_, m_ + 64] = 1.0
    common = dict(rowc=rowc, jmat=jmat,
        w_in=f(inp["w_in"])[0], w_conv_out=f(inp["w_conv_out"])[0], w_glu=f(inp["w_ssm_glu"])[0], w_mix=f(inp["w_mix_out"])[0],
        wq=f(inp["xa_wq"])[0], wkv=f(inp["xa_wkv"])[0], wo=f(inp["xa_wo"])[0], w_up=f(inp["mlp_w_up"])[0], w_down=f(inp["mlp_w_down"])[0],
        vecs=vecs, dwT=f(dwT), lam=f(lam), bc_in=f(bc_in), dvec=f(dvec), consts=f(consts), cmask=cmask, selP=selP, selU=selU)
    in_maps = []
    for core in range(8):
        b, half = core // 2, core % 2
        xT = np.zeros((D, NA), np.float32)
        xT[:, NT:] = x[b, half * NT:(half + 1) * NT].T
        if half == 1:
            xT[:, :NT] = x[b, 0:NT].T
        m = dict(common)
        m["xT"] = xT
        m["maskv"] = np.full((128, 1), float(half), np.float32)
        m["memT"] = f(mem[b].T)
        in_maps.append(m)
    if _NC[0] is None:
        _NC[0] = build()
    res = run_bass_kernel_spmd(_NC[0], in_maps, core_ids=list(range(8)))
    out = np.zeros((4, 4096, D), np.float32)
    for core in range(8):
        b, half = core // 2, core % 2
        out[b, half * NT:(half + 1) * NT] = res.results[core]["outT"].T
    return out
```

```python
import math
from contextlib import ExitStack
import numpy as np
import ml_dtypes
import concourse.bass as bass
import concourse.mybir as mybir
from concourse.bass_utils import run_bass_kernel_spmd

F32 = mybir.dt.float32
BF16 = mybir.dt.bfloat16
I32 = mybir.dt.int32
ALU = mybir.AluOpType
AF = mybir.ActivationFunctionType
AX = mybir.AxisListType

ENGS = ("pe", "act", "dve", "pool", "sp")
D = 1024
NT = 2048
NA = 4096
TT = 512
G = 32
ALPHA = 2.0 ** 0.25
EPS = 1e-5
DBG_BAR = False
SCAN_SINGLE = True


class Prog:
    def __init__(self, nc, stack, n_lanes=8):
        self.nc = nc
        self.ops = {e: [] for e in ENGS}
        self.cnt = {}
        self.sem = {}
        for e in ENGS:
            self.sem[e] = stack.enter_context(nc.semaphore("clk_" + e))
            self.cnt[e] = 0
        self.lanes = {}
        for q in ("sp", "pool", "act"):
            self.lanes[q] = []
            for i in range(n_lanes):
                k = "dma_%s_%d" % (q, i)
                self.sem[k] = stack.enter_context(nc.semaphore(k))
                self.cnt[k] = 0
                self.lanes[q].append(k)
        self.lane_rr = {q: 0 for q in self.lanes}
        self.waited = {e: {} for e in ENGS}
        self.last_w = {}
        self.readers = {}
        self.out_events = []

    def _deps(self, reads, writes):
        deps = {}

        def add(ev):
            if ev is None:
                return
            s, v = ev
            if deps.get(s, 0) < v:
                deps[s] = v

        for k in reads:
            add(self.last_w.get(k))
            if isinstance(k, tuple) and k and k[0] == "ps":
                for s_, v_ in self.readers.get(k, {}).items():
                    add((s_, v_))
        for k in writes:
            add(self.last_w.get(k))
            for s, v in self.readers.get(k, {}).items():
                add((s, v))
        return deps

    def _emit_waits(self, eng, deps):
        for s, v in deps.items():
            if s == eng and eng == "pe":
                continue
            if self.waited[eng].get(s, 0) >= v:
                continue
            self.waited[eng][s] = v
            sem = self.sem[s]
            self.ops[eng].append(lambda E, sem=sem, v=v: E.wait_ge(sem, v))

    def _record(self, ev, reads, writes):
        for k in reads:
            d = self.readers.setdefault(k, {})
            if d.get(ev[0], 0) < ev[1]:
                d[ev[0]] = ev[1]
        for k in writes:
            self.last_w[k] = ev
            self.readers[k] = {}

    def op(self, eng, fn, reads=(), writes=(), inc=True):
        deps = self._deps(reads, writes)
        self._emit_waits(eng, deps)
        if inc:
            self.cnt[eng] += 1
            ev = (eng, self.cnt[eng])
            sem = self.sem[eng]
            self.ops[eng].append(lambda E, fn=fn, sem=sem: fn(E).then_inc(sem, 1))
        else:
            ev = (eng, self.cnt[eng] + 1)
            self.ops[eng].append(lambda E, fn=fn: fn(E))
        self._record(ev, reads, writes)
        return ev

    def dma(self, q, out, in_, reads=(), writes=(), is_output=False, **kw):
        deps = self._deps(reads, writes)
        self._emit_waits(q, deps)
        lanes = self.lanes[q]
        lane = lanes[self.lane_rr[q] % len(lanes)]
        self.lane_rr[q] += 1
        if self.cnt[lane] > 0:
            self._emit_waits(q, {lane: self.cnt[lane]})
        self.cnt[lane] += 16
        ev = (lane, self.cnt[lane])
        sem = self.sem[lane]
        self.ops[q].append(
            lambda E, out=out, in_=in_, sem=sem, kw=kw: E.dma_start(out=out, in_=in_, **kw).then_inc(sem, 16)
        )
        self._record(ev, reads, writes)
        if is_output:
            self.out_events.append(ev)
        return ev

    def barrier(self):
        for e in ENGS:
            deps = {s: c for s, c in self.cnt.items() if c > 0 and s != e}
            self._emit_waits(e, deps)

    def finish(self):
        self.barrier()

    def run(self, block):
        ops = self.ops

        @block.tensor
        def _(E):
            for f in ops["pe"]:
                f(E)

        @block.scalar
        def _(E):
            for f in ops["act"]:
                f(E)

        @block.vector
        def _(E):
            for f in ops["dve"]:
                f(E)

        @block.gpsimd
        def _(E):
            for f in ops["pool"]:
                f(E)

        @block.sync
        def _(E):
            for f in ops["sp"]:
                f(E)


def V(t, off, dims):
    a = t if (hasattr(t, "ap") and hasattr(t, "offset")) else t[:]
    return bass.AP(a.tensor, a.offset + off, [list(a.ap[0])] + [list(d) for d in dims])


class Arena:
    def __init__(self, tile, nbytes):
        self.t, self.n, self.off = tile, nbytes, 0

    def reset(self):
        self.off = 0

    def alloc(self, shape, dt):
        esz = 2 if dt == BF16 else 4
        n = 1
        for d in shape[1:]:
            n *= d
        self.off = (self.off + 3) // 4 * 4
        base = self.t[:].bitcast(dt)
        e0 = self.off // esz
        a = base[:, e0:e0 + n]
        self.off += n * esz
        assert self.off <= self.n, (self.off, self.n)
        if len(shape) == 3:
            a = a.rearrange("p (a b) -> p a b", a=shape[1])
        elif len(shape) == 4:
            a = a.rearrange("p (a b c) -> p a b c", a=shape[1], b=shape[2])
        return a


def build():
    nc = bass.Bass("TRN2", target_bir_lowering=False)
    dr = {}

    def din(name, shape, dt=F32):
        dr[name] = nc.dram_tensor(name, list(shape), dt, kind="ExternalInput").ap()
        return dr[name]

    xT = din("xT", [D, NA])
    maskv = din("maskv", [128, 1])
    memT = din("memT", [D, 256])
    w_in = din("w_in", [D, 4608])
    w_conv_out = din("w_conv_out", [D, D])
    w_glu = din("w_glu", [512, 2048])
    w_mix = din("w_mix", [D, D])
    wq = din("wq", [D, D])
    wkv = din("wkv", [D, 2048])
    wo = din("wo", [D, D])
    w_up = din("w_up", [D, 4096])
    w_down = din("w_down", [4096, D])
    vecs = din("vecs", [128, 12, 8])
    dwT = din("dwT", [128, 8, 31])
    lam = din("lam", [128, 3, G])
    bc_in = din("bc_in", [128, 4, G, 16])
    dvec = din("dvec", [128, 4])
    consts = din("consts", [128, 3, 128])
    cmask = din("cmask", [128, 128])
    selP_d = din("selP", [128, 64, 128])
    selU_d = din("selU", [128, 64, 128])
    rowc_d = din("rowc", [128, 64])
    jmat_d = din("jmat", [128, 128])
    outT = nc.dram_tensor("outT", [D, NT], F32, kind="ExternalOutput").ap()
    h0s = nc.dram_tensor("h0s", [D, NT], F32, kind="Internal").ap()

    with ExitStack() as st:
        P = Prog(nc, st)
        cnt = [0]

        def sb(shape, dt, stack=st, name=None):
            cnt[0] += 1
            return stack.enter_context(nc.sbuf_tensor(name or ("t%d" % cnt[0]), list(shape), dt))

        banks = [st.enter_context(nc.psum_tensor("ps%d" % i, [128, 512], F32)) for i in range(8)]
        bank_i = {"g": 0, "ln": 0}
        bank_pool = {"g": list(range(8)), "ln": None}

        def bank(pool="g"):
            if pool == "ln" and bank_pool["ln"] is None:
                pool = "g"
            lst = bank_pool[pool]
            i = lst[bank_i[pool] % len(lst)]
            bank_i[pool] += 1
            return banks[i], ("ps", i)

        def split_banks(on):
            bank_pool["g"] = [0, 1, 2, 3] if on else list(range(8))
            bank_pool["ln"] = [4, 5, 6, 7] if on else None

        cst = sb([128, 3, 128], BF16)
        P.dma("pool", cst[:], consts, writes=["cst"])
        onesM, ones1, ident = cst[:, 0, :], cst[:, 1, :], cst[:, 2, :]
        vec = sb([128, 12, 8], F32)
        P.dma("sp", vec[:], vecs, writes=["vec"])
        mk = sb([128, 1], F32)
        P.dma("sp", mk[:], maskv, writes=["mk"])
        dv = sb([128, 4], F32)
        P.dma("sp", dv[:], dvec, writes=["dv"])
        epst = sb([128, 1], F32)
        vecA = sb([128, 2, 8], F32)
        P.op("dve", lambda E: E.tensor_scalar(out=vecA[:], in0=vec[:, 6:8, :], scalar1=ALPHA, scalar2=None, op0=ALU.mult), reads=["vec"], writes=["vecA"])
        P.op("dve", lambda E: E.memset(epst[:], EPS), writes=["eps"])
        hh = sb([128, 8, 32], BF16)

        NW = 6
        wringT = sb([128, NW * 1024], BF16)
        wring = [V(wringT, i * 1024, [[1, 1024]]) for i in range(NW)]
        wr_i = [0]

        ring_bufs = list(wring)

        def wload(w_ap, r0, KC, c0, width=128):
            i = wr_i[0] % len(ring_bufs)
            wr_i[0] += 1
            t = ring_bufs[i]
            key = ("w", i)
            src = w_ap[r0:r0 + KC * 128, c0:c0 + width].rearrange("(kc p) n -> p kc n", p=128)
            dst = V(t, 0, [[width, KC], [1, width]])
            P.dma("pool", dst, src, writes=[key])
            return (lambda kc, a, b: V(t, kc * width + a, [[1, b - a]])), key

        def mm_group(ps_ap, pskey, parts):
            n = len(parts)
            for i, (l, r, rk) in enumerate(parts):
                P.op("pe", lambda E, l=l, r=r, i=i: E.matmul(ps_ap, l, r, start=(i == 0), stop=(i == n - 1)),
                     reads=list(rk), writes=[pskey], inc=(i == n - 1))

        rR = sb([128, 8 * NT], F32)
        mgR = sb([128, 8 * NT], BF16)
        RA = Arena(rR, 8 * NT * 4)
        MA = Arena(mgR, 8 * NT * 2)
        r = rR[:].rearrange("p (c t) -> p c t", c=8)
        mg = mgR[:].rearrange("p (c t) -> p c t", c=8)
        xb = MA.alloc([128, 8, 512], BF16)
        sq = MA.alloc([128, 8, 512], BF16)
        m2 = MA.alloc([128, 512], F32)
        rs = MA.alloc([128, 512], F32)
        tmps = [MA.alloc([128, 512], F32) for _ in range(2)]

        LNS = dict(xb=xb, sq=sq, m2=m2, rs=rs, tmps=tmps, tag="A")

        def ln_front_a(src, srckey, n, xb_dma_src, S=None):
            S = S or LNS
            xb_, sq_, tg = S["xb"], S["sq"], S["tag"]
            xbk = S.get("xbkey", "lnxb" + tg)
            if xb_dma_src is not None:
                P.dma("pool", xb_[:, :, 0:n], xb_dma_src, reads=[srckey(c) for c in range(8)], writes=[xbk])
            for c in range(8):
                P.op("act", lambda E, c=c, sq_=sq_: E.activation(out=sq_[:, c, 0:n], in_=src(c), func=AF.Square),
                     reads=[srckey(c)], writes=[("lnsq" + tg, c)])
            b1, k1 = bank("ln")
            b2, k2 = bank("ln")
            mm_group(b1[:, 0:n], k1, [(onesM, xb_[:, c, 0:n], ["cst", xbk]) for c in range(8)])
            mm_group(b2[:, 0:n], k2, [(onesM, sq_[:, c, 0:n], ["cst", ("lnsq" + tg, c)]) for c in range(8)])
            return (b1, k1, b2, k2)

        def ln_front_b_steps(st_, n, S=None):
            S = S or LNS
            m2_, rs_, tg = S["m2"], S["rs"], S["tag"]
            km, kr = "lnm2" + tg, "lnrs" + tg
            b1, k1, b2, k2 = st_
            return [
                lambda: P.op("act", lambda E: E.activation(out=m2_[:, 0:n], in_=b1[:, 0:n], func=AF.Square), reads=[k1], writes=[km]),
                lambda: P.op("dve", lambda E: E.tensor_tensor(out=rs_[:, 0:n], in0=b2[:, 0:n], in1=m2_[:, 0:n], op=ALU.subtract),
                             reads=[k2, km], writes=[kr]),
                lambda: P.op("act", lambda E: E.activation(out=rs_[:, 0:n], in_=rs_[:, 0:n], func=AF.Ln, bias=epst[:, 0:1], scale=1.0),
                             reads=[kr, "eps"], writes=[kr]),
                lambda: P.op("act", lambda E: E.activation(out=rs_[:, 0:n], in_=rs_[:, 0:n], func=AF.Exp, scale=-0.5),
                             reads=[kr], writes=[kr]),
                lambda: P.op("dve", lambda E: E.tensor_copy(out=b2[:, 0:n], in_=rs_[:, 0:n]), reads=[kr], writes=[k2]),
                lambda: P.op("dve", lambda E: E.tensor_tensor(out=b1[:, 0:n], in0=b1[:, 0:n], in1=rs_[:, 0:n], op=ALU.mult),
                             reads=[k1, kr], writes=[k1]),
            ]

        def ln_front_b(st_, n, S=None):
            for f_ in ln_front_b_steps(st_, n, S):
                f_()

        def ln_back(st_, src, srckey, n, gi, out_f32=None, out_bf=None, okeys=(), bfkeys=None, inter=None, S=None, f32_affine=None):
            S = S or LNS
            fsc = (lambda c: vec[:, gi, c:c + 1]) if f32_affine is None else (lambda c: f32_affine[:, 0, c:c + 1])
            fbi = (lambda c: vec[:, gi + 1, c:c + 1]) if f32_affine is None else (lambda c: f32_affine[:, 1, c:c + 1])
            tmps_, tg = S["tmps"], S["tag"]
            b1, k1, b2, k2 = st_
            bfkeys = list(bfkeys)
            inter = list(inter) if inter else []
            for c in range(8):
                if c >= 1 and inter:
                    inter.pop(0)()
                tmp = tmps_[c % 2]
                tk = ("lntmp" + tg, c % 2)
                P.op("dve", lambda E, c=c, tmp=tmp: E.tensor_tensor(out=tmp[:, 0:n], in0=src(c), in1=b2[:, 0:n], op=ALU.mult),
                     reads=[srckey(c), k2], writes=[tk])
                P.op("dve", lambda E, c=c, tmp=tmp: E.tensor_tensor(out=tmp[:, 0:n], in0=tmp[:, 0:n], in1=b1[:, 0:n], op=ALU.subtract),
                     reads=[tk, k1], writes=[tk])
                if out_f32 is not None:
                    P.op("act", lambda E, c=c, tmp=tmp: E.activation(out=out_f32(c), in_=tmp[:, 0:n], func=AF.Identity,
                                                                     bias=fbi(c), scale=fsc(c)),
                         reads=[tk, "vec", "vecA"], writes=[okeys(c)])
                    if out_bf is not None:
                        if c % 2 == 0:
                            P.op("dve", lambda E, c=c, tmp=tmp: E.tensor_scalar(out=out_bf(c), in0=tmp[:, 0:n], scalar1=vec[:, gi, c:c + 1], scalar2=vec[:, gi + 1, c:c + 1],
                                                                                op0=ALU.mult, op1=ALU.add),
                                 reads=[tk, "vec"], writes=bfkeys)
                        else:
                            P.op("act", lambda E, c=c, tmp=tmp: E.activation(out=out_bf(c), in_=tmp[:, 0:n], func=AF.Identity,
                                                                             bias=vec[:, gi + 1, c:c + 1], scale=vec[:, gi, c:c + 1]),
                                 reads=[tk, "vec"], writes=bfkeys)
                else:
                    P.op("act", lambda E, c=c, tmp=tmp: E.activation(out=out_bf(c), in_=tmp[:, 0:n], func=AF.Identity,
                                                                     bias=vec[:, gi + 1, c:c + 1], scale=vec[:, gi, c:c + 1]),
                         reads=[tk, "vec"], writes=bfkeys)
            for f_ in inter:
                f_()

        with ExitStack() as sM:
            yT = sb([128, 4, NT], BF16, sM)
            with ExitStack() as s1:
                E8a = sb([128, G, 128], BF16, s1)
                E8b = sb([128, G, 128], BF16, s1)
                M0 = sb([128, G, 128], BF16, s1)
                FC = sb([128, G, 128], BF16, s1)
                R8 = sb([128, G], F32, s1)
                q8 = sb([128, G], F32, s1)
                FC2 = V(wringT, 0, [[128, G], [1, 128]])
                RA.reset()
                ra = RA.alloc
                lm = ra([128, 3, G], F32)
                P.dma("sp", lm, lam, writes=["lm"])
                bcin = ra([128, 4, G, 16], F32)
                P.dma("sp", bcin, bc_in, writes=["bcin"])
                cm = ra([128, 128], F32)
                P.dma("sp", cm, cmask, writes=["cm"])
                step = ra([128, G], F32); xx = ra([128, G], F32); th = ra([128, G], F32); q0 = ra([128, G], F32)
                NE = 16
                pw = ra([128, 2, NE, G], F32)
                ang = ra([128, 2, NE, G], F32)
                mag = ra([128, NE, G], F32)
                ti = ra([128, 2, NE, G], I32)
                tf = ra([128, 2, NE, G], F32)
                K = ["tbl"]
                KR = K + ["lm", "bcin"]
                P.op("act", lambda E: E.activation(out=step, in_=lm[:, 2, :], func=AF.Exp), reads=KR, writes=K)
                P.op("dve", lambda E: E.tensor_tensor(out=xx, in0=lm[:, 0, :], in1=step, op=ALU.mult), reads=KR, writes=K)
                P.op("dve", lambda E: E.tensor_tensor(out=th, in0=lm[:, 1, :], in1=step, op=ALU.mult), reads=KR, writes=K)
                INV2PI = 1.0 / (2.0 * math.pi)

                def reduce_(dst, src_ap, tiv, tfv):
                    P.op("dve", lambda E: E.tensor_copy(out=tiv, in_=src_ap), reads=K, writes=K)
                    P.op("dve", lambda E: E.tensor_copy(out=tfv, in_=tiv), reads=K, writes=K)
                    P.op("dve", lambda E: E.tensor_tensor(out=dst, in0=src_ap, in1=tfv, op=ALU.subtract), reads=K, writes=K)

                P.op("dve", lambda E: E.tensor_scalar(out=q0, in0=th, scalar1=INV2PI, scalar2=None, op0=ALU.mult), reads=K, writes=K)
                reduce_(q0, q0, ti[:, 0, 0, :], tf[:, 0, 0, :])
                for e in range(NE):
                    ex = float(e - 7)
                    P.op("dve", lambda E, e=e, ex=ex: E.tensor_scalar(out=ang[:, 1, e, :], in0=q0, scalar1=ex, scalar2=None, op0=ALU.mult), reads=K, writes=K)
                    P.op("dve", lambda E, e=e, ex=ex: E.tensor_scalar(out=ang[:, 0, e, :], in0=q0, scalar1=ex, scalar2=0.25, op0=ALU.mult, op1=ALU.add), reads=K, writes=K)
                    P.op("act", lambda E, e=e, ex=ex: E.activation(out=mag[:, e, :], in_=xx, func=AF.Exp, scale=ex), reads=K, writes=K)
                reduce_(ang, ang, ti, tf)
                P.op("act", lambda E: E.activation(out=pw, in_=ang, func=AF.Sin, scale=6.28318), reads=K, writes=K)
                for ri in range(2):
                    P.op("dve", lambda E, ri=ri: E.tensor_tensor(out=pw[:, ri], in0=pw[:, ri], in1=mag, op=ALU.mult), reads=K, writes=K)
                P.op("dve", lambda E: E.tensor_copy(out=R8[:], in_=mag[:, 15, :]), reads=K, writes=["A"])
                P.op("dve", lambda E: E.tensor_scalar(out=q8[:], in0=q0, scalar1=8.0, scalar2=None, op0=ALU.mult), reads=K, writes=["A"])
                P.op("dve", lambda E: E.tensor_copy(out=ti[:, 0, 0, :], in_=q8[:]), reads=K + ["A"], writes=K)
                P.op("dve", lambda E: E.tensor_copy(out=tf[:, 0, 0, :], in_=ti[:, 0, 0, :]), reads=K, writes=K)
                P.op("dve", lambda E: E.tensor_tensor(out=q8[:], in0=q8[:], in1=tf[:, 0, 0, :], op=ALU.subtract), reads=K + ["A"], writes=["A"])
                ar, ai = pw[:, 0, 8, :], pw[:, 1, 8, :]
                t1 = ra([128, G], F32); t2 = ra([128, G], F32); den = ra([128, G], F32)
                cr = ra([128, G], F32); ci = ra([128, G], F32); nr = ra([128, G], F32)
                lr, li = lm[:, 0, :], lm[:, 1, :]
                TT_ = lambda o, a, b, op: P.op("dve", lambda E: E.tensor_tensor(out=o, in0=a, in1=b, op=op), reads=KR, writes=K)
                P.op("dve", lambda E: E.tensor_scalar(out=nr, in0=ar, scalar1=-1.0, scalar2=None, op0=ALU.add), reads=K, writes=K)
                TT_(t1, lr, lr, ALU.mult); TT_(t2, li, li, ALU.mult); TT_(den, t1, t2, ALU.add)
                P.op("dve", lambda E: E.reciprocal(out=den, in_=den), reads=K, writes=K)
                TT_(t1, nr, lr, ALU.mult); TT_(t2, ai, li, ALU.mult); TT_(cr, t1, t2, ALU.add); TT_(cr, cr, den, ALU.mult)
                TT_(t1, ai, lr, ALU.mult); TT_(t2, nr, li, ALU.mult); TT_(ci, t1, t2, ALU.subtract); TT_(ci, ci, den, ALU.mult)
                bbr = ra([128, G, 16], F32); bbi = ra([128, G, 16], F32); tb = ra([128, G, 16], F32)
                bcg = lambda t: V(t, 0, [[1, G], [0, 16]])
                Br, Bi, Cr, Ci = (bcin[:, k] for k in range(4))
                TT_(bbr, Br, bcg(cr), ALU.mult); TT_(tb, Bi, bcg(ci), ALU.mult); TT_(bbr, bbr, tb, ALU.subtract)
                TT_(bbi, Bi, bcg(cr), ALU.mult); TT_(tb, Br, bcg(ci), ALU.mult); TT_(bbi, bbi, tb, ALU.add)
                X1 = ra([128, G, 16], F32); X2 = ra([128, G, 16], F32)
                Y1 = ra([128, G, 16], F32); Y2 = ra([128, G, 16], F32)
                Y3 = ra([128, G, 16], F32); Y4 = ra([128, G, 16], F32)
                cp = lambda o, a, s_: P.op("dve", lambda E: E.tensor_scalar(out=o, in0=a, scalar1=s_, scalar2=None, op0=ALU.mult), reads=KR, writes=K)
                cp(X1[0:64], bbr[0:64], 1.0); cp(X1[64:128], bbi[64:128], 1.0)
                cp(X2[0:64], bbi[0:64], -1.0); cp(X2[64:128], bbr[64:128], 1.0)
                cp(Y1[0:64], Cr[0:64], 1.0); cp(Y1[64:128], Ci[64:128], -1.0)
                cp(Y2[0:64], Ci[0:64], -1.0); cp(Y2[64:128], Cr[64:128], -1.0)
                cp(Y3[0:64], Ci[0:64], -1.0); cp(Y3[64:128], Cr[64:128], -1.0)
                cp(Y4[0:64], Cr[0:64], -1.0); cp(Y4[64:128], Ci[64:128], 1.0)
                EB = ra([128, G, 8, 16], BF16)
                FCp = ra([128, G, 8, 16], BF16)
                FCv = V(FC, 0, [[128, G], [16, 8], [1, 16]])
                FC2v = V(wringT, 0, [[128, G], [16, 8], [1, 16]])
                MT = Arena(mgR, 8 * NT * 2)
                ta = MT.alloc([128, G, 16], F32); tc = MT.alloc([128, G, 16], F32)

                def ctab(dst, e_idx, Xa, Xb):
                    prr = V(pw, (0 * NE + e_idx) * G, [[1, G], [0, 16]])
                    pii = V(pw, (1 * NE + e_idx) * G, [[1, G], [0, 16]])
                    TT_(ta, Xa, prr, ALU.mult); TT_(tc, Xb, pii, ALU.mult); TT_(dst, ta, tc, ALU.add)

                for j in range(8):
                    ctab(EB[:, :, j, :], (7 - j) + 7, X1, X2)
                    ctab(FCv[:, :, j, :], (j + 1) + 7, Y1, Y2)
                    ctab(FCp[:, :, j, :], (j - 7) + 7, Y1, Y2)
                    ctab(FC2v[:, :, j, :], (j + 1) + 7, Y3, Y4)
                for g4 in range(G // 4):
                    b, k = bank()
                    b2, k2 = bank()
                    for gi_ in range(4):
                        g = g4 * 4 + gi_
                        ebg = V(EB, g * 128, [[1, 128]])
                        P.op("pe", lambda E, b=b, gi_=gi_, ebg=ebg: E.matmul(b[:, gi_ * 128:(gi_ + 1) * 128], ebg, ident, start=True, stop=True),
                             reads=K + ["cst"], writes=[k], inc=(gi_ == 3))
                    for gi_ in range(4):
                        g = g4 * 4 + gi_
                        ebg = V(EB, g * 128, [[1, 128]])
                        fpg = V(FCp, g * 128, [[1, 128]])
                        P.op("pe", lambda E, b2=b2, gi_=gi_, ebg=ebg, fpg=fpg: E.matmul(b2[:, gi_ * 128:(gi_ + 1) * 128], ebg, fpg, start=True, stop=True),
                             reads=K, writes=[k2], inc=(gi_ == 3))
                    bv = b[:, :].rearrange("p (a c) -> p a c", a=4)
                    gs = slice(g4 * 4, g4 * 4 + 4)
                    P.op("act", lambda E, bv=bv, gs=gs: E.activation(out=E8a[:, gs, :], in_=bv, func=AF.Copy), reads=[k], writes=["E8a"])
                    P.op("act", lambda E, bv=bv, gs=gs: E.activation(out=E8b[:, gs, 0:64], in_=bv[:, :, 64:128], func=AF.Copy, scale=-1.0), reads=[k], writes=["E8b"])
                    P.op("act", lambda E, bv=bv, gs=gs: E.activation(out=E8b[:, gs, 64:128], in_=bv[:, :, 0:64], func=AF.Copy), reads=[k], writes=["E8b"])
                    P.op("dve", lambda E, b2=b2, gs=gs: E.tensor_tensor(out=M0[:, gs, :], in0=b2[:, :].rearrange("p (a c) -> p a c", a=4),
                                                                       in1=V(cm, 0, [[0, 4], [1, 128]]), op=ALU.mult), reads=[k2, "cm"], writes=["M0"])
                P.barrier()
                usT = sb([128, 4, NA], BF16, s1)
                rowc = sb([128, 64], F32, s1)
                P.dma("sp", rowc[:], rowc_d, writes=["rot"])
                sP = ExitStack()
                selP = sb([128, 64, 128], BF16, sP)
                P.dma("pool", selP[:], selP_d, writes=["selP"])
                RA.reset()
                MI = Arena(mgR, 8 * NT * 2)
                MI.off = 8 * 1024 * 3
                hbt = xb
                MI.off = 24 * 1024
                Wssm = MI.alloc([128, 8, 512], BF16)
                P.dma("pool", Wssm, w_in[:, 2048:2560].rearrange("(kc p) n -> p kc n", p=128), writes=["Wssm"])
                NXB = 3
                xt = [ra([128, 8, TT], F32) for _ in range(NXB)]
                def xload(it):
                    P.dma("sp", xt[it % NXB], xT[:, it * TT:(it + 1) * TT].rearrange("(c p) t -> p c t", p=128), writes=[("xt", it % NXB, c) for c in range(8)])

                def xsrc(it):
                    x_t = xt[it % NXB]
                    return (lambda c: x_t[:, c, :])

                NTI = NA // TT
                xdram = lambda it: xT[:, it * TT:(it + 1) * TT].rearrange("(c p) t -> p c t", p=128)
                xload(0)
                split_banks(True)
                xkf = lambda it: (lambda c: ("xt", it % NXB, c))
                xbs = [ra([128, 8, TT], BF16) for _ in range(2)]
                SIN_ = [dict(LNS, xb=xbs[i], xbkey="lnxbI%d" % i) for i in range(2)]

                def xbdma(it):
                    P.dma("pool", xbs[it % 2], xdram(it), writes=["lnxbI%d" % (it % 2)])

                xbdma(0)
                xbdma(1)
                xload(1)
                fr = {0: ln_front_a(xsrc(0), xkf(0), TT, None, SIN_[0])}
                ln_front_b(fr[0], TT)

                def ssm_in(it):
                    own_ = it >= 4
                    for m in range(4):
                        b, k = bank()
                        mm_group(b[:, :], k, [(Wssm[:, kc, m * 128:(m + 1) * 128], hbt[:, kc, :], ["Wssm", "hbt"]) for kc in range(8)])
                        dst = V(usT, m * NA + it * 64, [[512, 8], [1, 64]])
                        srcv = V(b, 0, [[1, 8], [8, 64]])
                        if own_:
                            P.op("dve", lambda E, dst=dst, srcv=srcv: E.tensor_copy(out=dst, in_=srcv),
                                 reads=[k], writes=["usT"])
                        else:
                            P.op("dve", lambda E, dst=dst, srcv=srcv: E.tensor_scalar(out=dst, in0=srcv, scalar1=mk[:, 0:1], scalar2=None, op0=ALU.mult),
                                 reads=[k, "mk"], writes=["usT"])

                for it in range(NTI):
                    x_t = xt[it % NXB]
                    if it + 2 < NTI:
                        xload(it + 2)
                    if it + 1 < NTI:
                        fr[it + 1] = ln_front_a(xsrc(it + 1), xkf(it + 1), TT, None, SIN_[(it + 1) % 2])
                        if it + 2 < NTI:
                            xbdma(it + 2)
                    own = it >= 4
                    o0 = (it - 4) * TT
                    if own:
                        ln_back(fr[it], xsrc(it), xkf(it), TT, 0, out_f32=xsrc(it), out_bf=lambda c: hbt[:, c, :], okeys=xkf(it), bfkeys=["hbt"],
                                inter=(ln_front_b_steps(fr[it + 1], TT) if it + 1 < NTI else None))
                        P.dma("act", h0s[:, o0:o0 + TT].rearrange("(c p) t -> p c t", p=128), x_t, reads=[xkf(it)(c) for c in range(8)], writes=["h0s"])
                    else:
                        ln_back(fr[it], xsrc(it), xkf(it), TT, 0, out_bf=lambda c: hbt[:, c, :], okeys=None, bfkeys=["hbt"],
                                inter=(ln_front_b_steps(fr[it + 1], TT) if it + 1 < NTI else None))
                        if it == 3:
                            P.op("act", lambda E: E.activation(out=hh[:], in_=hbt[:, :, TT - 32:TT], func=AF.Copy), reads=["hbt"], writes=["hh"])
                    ssm_in(it)
                split_banks(False)
                P.barrier()
                RA.reset()
                Up = ra([128, G, 512], BF16)
                Yp = ra([128, G, 256], BF16)
                NB = 64
                SAq = ra([128, G, NB], F32)
                SBq = ra([128, G, NB], F32)
                MB = Arena(mgR, 8 * NT * 2)
                Wq = MB.alloc([128, G, NB], F32)
                M8 = MB.alloc([128, G, NB], F32)
                Wsc = Wq
                cosT = MB.alloc([128, G, NB], F32)
                sinT = MB.alloc([128, G, NB], F32)
                tiq = SAq.bitcast(I32)
                KT_ = ["rot"]
                P.op("dve", lambda E: E.tensor_tensor(out=sinT, in0=V(q8, 0, [[1, G], [0, NB]]), in1=V(rowc, 0, [[0, G], [1, NB]]), op=ALU.mult),
                     reads=KT_ + ["A"], writes=KT_)
                P.op("dve", lambda E: E.tensor_scalar(out=cosT, in0=sinT, scalar1=0.25, scalar2=None, op0=ALU.add), reads=KT_, writes=KT_)
                for T_ in (sinT, cosT):
                    P.op("dve", lambda E, T_=T_: E.tensor_copy(out=tiq, in_=T_), reads=KT_, writes=KT_)
                    P.op("dve", lambda E, T_=T_: E.tensor_copy(out=Wq, in_=tiq), reads=KT_, writes=KT_)
                    P.op("dve", lambda E, T_=T_: E.tensor_tensor(out=T_, in0=T_, in1=Wq, op=ALU.subtract), reads=KT_, writes=KT_)
                    P.op("act", lambda E, T_=T_: E.activation(out=T_, in_=T_, func=AF.Sin, scale=6.28318), reads=KT_, writes=KT_)
                if True:
                    for g in range(G):
                        q, gl = g // 8, g % 8
                        b, k = bank()
                        mm_group(b[:, :], k, [(selP[:, gl * 8 + j, :], V(usT, q * NA + j * 512, [[1, 512]]), ["selP", "usT"]) for j in range(8)])
                        P.op("act", lambda E, b=b, g=g: E.activation(out=Up[:, g, :], in_=b[:, :], func=AF.Copy), reads=[k], writes=["Up"])
                    P.barrier()
                    sP.close()
                with ExitStack() as s2:
                    SINc = sb([128, G, NB], BF16, s2)
                    SINs = sb([128, G, NB], BF16, s2)
                    jm = sb([128, 128], F32, s2)
                    P.dma("sp", jm[:], jmat_d, writes=["jm"])
                    carry = sb([128, G], F32, s2)
                    wl = sb([128, G], F32, s2)
                    c1 = sb([128, G], F32, s2)
                    c2_ = sb([128, G], F32, s2)
                    KT_ = ["rot"]
                    selUa = SAq.bitcast(BF16).rearrange("p g (a b) -> p (g a) b", a=1) if False else V(SAq.bitcast(BF16), 0, [[128, 32], [1, 128]])
                    selUb = V(SBq.bitcast(BF16), 0, [[128, 32], [1, 128]])
                    P.dma("pool", selUa, selU_d[:, 0:32, :], reads=KT_, writes=["selUa"])
                    P.op("dve", lambda E: E.memset(carry[:], 0.0), writes=["carry"])
                    P.op("dve", lambda E: E.tensor_copy(out=M8, in_=V(R8, 0, [[1, G], [0, NB]])), reads=["A"], writes=["M8"])
                    P.op("dve", lambda E: E.memset(V(M8, 0, [[NB, G]]), 0.0), reads=["M8"], writes=["M8"])
                    TTd = lambda o, x, y, op, rk, wk: P.op("dve", lambda E: E.tensor_tensor(out=o, in0=x, in1=y, op=op), reads=rk, writes=wk)
                    NE8 = 512 // NB

                    def s8_rotin(e8):
                        for (Et, dst, tab, nm) in ((E8a, Wq, cosT, "Wq"), (E8b, SBq, sinT, "SB")):
                            for j4 in range(4):
                                b, k = bank()
                                for gl in range(8):
                                    g = j4 * 8 + gl
                                    P.op("pe", lambda E, b=b, g=g, gl=gl, Et=Et, e8=e8: E.matmul(b[:, gl * NB:(gl + 1) * NB], Et[:, g, :], Up[:, g, e8 * NB:(e8 + 1) * NB], start=True, stop=True),
                                         reads=["E8a", "E8b", "Up"], writes=[k], inc=(gl == 7))
                                P.op("dve", lambda E, b=b, j4=j4, dst=dst, tab=tab: E.tensor_tensor(out=dst[:, j4 * 8:(j4 + 1) * 8, :], in0=b[:, :].rearrange("p (a b) -> p a b", a=8),
                                                                                                   in1=tab[:, j4 * 8:(j4 + 1) * 8, :], op=ALU.mult),
                                     reads=[k] + KT_, writes=[nm])
                        TTd(Wq, Wq, SBq, ALU.subtract, ["Wq", "SB"], ["Wq"])

                    s8_rotin(0)
                    for e8 in range(NE8):
                        own = e8 >= 256 // NB
                        if own:
                            P.op("act", lambda E: E.activation(out=V(SINc, 0, [[NB, G]]), in_=carry[:], func=AF.Copy), reads=["carry"], writes=["SINc"])
                            P.op("dve", lambda E: E.memset(V(SINs, 0, [[NB, G]]), 0.0), writes=["SINs"])
                        TTd(c1[:], R8[:], carry[:], ALU.mult, ["A", "carry"], ["c1"])
                        TTd(V(Wq, 0, [[NB, G]]), V(Wq, 0, [[NB, G]]), c1[:], ALU.add, ["Wq", "c1"], ["Wq"])
                        P.op("dve", lambda E: E.tensor_tensor_scan(out=V(Wq, 0, [[1, G * NB]]), data0=V(M8, 0, [[1, G * NB]]), data1=V(Wq, 0, [[1, G * NB]]), initial=0.0, op0=ALU.mult, op1=ALU.add),
                             reads=["Wq", "M8"], writes=["Wq"])
                        if own:
                            TTd(SINc[:, :, 1:NB], cosT[:, :, 0:NB - 1], Wsc[:, :, 0:NB - 1], ALU.mult, KT_ + ["Wq"], ["SINc"])
                            TTd(SINs[:, :, 1:NB], sinT[:, :, 0:NB - 1], Wsc[:, :, 0:NB - 1], ALU.mult, KT_ + ["Wq"], ["SINs"])
                        P.op("dve", lambda E: E.tensor_copy(out=wl[:], in_=V(Wsc, NB - 1, [[NB, G]])), reads=["Wq"], writes=["wl"])
                        bj, kj = bank()
                        mm_group(bj[:, 0:G], kj, [(jm[:], wl[:], ["jm", "wl"])])
                        TTd(c1[:], V(cosT, NB - 1, [[NB, G]]), wl[:], ALU.mult, KT_ + ["wl"], ["c1"])
                        TTd(c2_[:], V(sinT, NB - 1, [[NB, G]]), bj[:, 0:G], ALU.mult, KT_ + [kj], ["c2_"])
                        TTd(carry[:], c1[:], c2_[:], ALU.add, ["c1", "c2_"], ["carry"])
                        if e8 + 1 < NE8:
                            s8_rotin(e8 + 1)
                        if own:
                            o0 = (e8 - 256 // NB) * NB
                            for j4 in range(4):
                                b, k = bank()
                                for gl in range(8):
                                    g = j4 * 8 + gl
                                    pso = b[:, gl * NB:(gl + 1) * NB]
                                    parts = [(M0[:, g, :], Up[:, g, e8 * NB:(e8 + 1) * NB], ["M0", "Up"]),
                                             (FC[:, g, :], SINc[:, g, :], ["tbl", "SINc"]),
                                             (FC2[:, g, :], SINs[:, g, :], ["tbl", "SINs"])]
                                    for i_, (l_, r_, rk_) in enumerate(parts):
                                        P.op("pe", lambda E, pso=pso, l_=l_, r_=r_, i_=i_: E.matmul(pso, l_, r_, start=(i_ == 0), stop=(i_ == 2)),
                                             reads=rk_, writes=[k], inc=(gl == 7 and i_ == 2))
                                P.op("act", lambda E, b=b, j4=j4, o0=o0: E.activation(out=Yp[:, j4 * 8:(j4 + 1) * 8, o0:o0 + NB], in_=b[:, :].rearrange("p (a b) -> p a b", a=8), func=AF.Copy),
                                     reads=[k], writes=["Yp"])
                    P.dma("pool", selUb, selU_d[:, 32:64, :], reads=["Wq"], writes=["SB", "selUb"])
                    for i in range(8):
                        sel_, sk_ = (selUa, "selUa") if i < 4 else (selUb, "selUb")
                        for q in range(4):
                            b, k = bank()
                            mm_group(b[:, 0:256], k, [(sel_[:, (i % 4) * 8 + gl, :], Yp[:, q * 8 + gl, :], [sk_, "Yp"]) for gl in range(8)])
                            P.op("dve", lambda E, b=b, q=q, i=i: E.scalar_tensor_tensor(
                                out=V(yT, q * NT + i, [[8, 256]]), in0=V(usT, q * NA + i * 512 + 256, [[1, 256]]), scalar=dv[:, q:q + 1],
                                in1=b[:, 0:256], op0=ALU.mult, op1=ALU.add), reads=[k, "usT", "dv"], writes=["yT"])
                    P.barrier()

            h0b = sb([128, 8, NT], BF16, sM)
            for n_ in range(4):
                P.dma("pool", h0b[:, :, n_ * TT:(n_ + 1) * TT], h0s[:, n_ * TT:(n_ + 1) * TT].rearrange("(c p) t -> p c t", p=128),
                      reads=["h0s"], writes=[("h0b", n_)])
            c2 = sb([128, 8, NT], BF16, sM)
            RA.reset()
            u = ra([128, 8, 32 + NT], BF16)
            dw = ra([128, 8, 31], F32)
            P.dma("sp", dw, dwT, writes=["dw"])
            idf = ra([128, 128], F32)
            P.op("act", lambda E: E.activation(out=idf, in_=ident, func=AF.Copy), reads=["cst"], writes=["idf"])
            acc1 = ra([128, 4, 512], F32)
            acc2 = ra([128, 4, 512], F32)
            cf = ra([128, 512], F32)
            cfs = [cf, ra([128, 512], F32)]
            MC = Arena(mgR, 8 * NT * 2)
            dgs = [MC.alloc([128, 31, 128], BF16) for _ in range(2)]
            sgs = [MC.alloc([128, 512], F32) for _ in range(2)]
            csqs = [MC.alloc([128, 512], BF16) for _ in range(2)]
            MCK = ["dg0", "dg1", "sg0", "sg1", "csq0", "csq1"]
            vg_i = [0]

            def VG(q):
                wv, kv_ = wload(w_in, 0, 8, q * 128)
                wg, kg_ = wload(w_in, 0, 8, 1024 + q * 128)
                tiles = [(None, 32)] + [(n * TT, TT) for n in range(4)]
                for (t0, n) in tiles:
                    rhs = (lambda kc: hh[:, kc, :]) if t0 is None else (lambda kc, t0=t0: h0b[:, kc, t0:t0 + TT])
                    rk = "hh" if t0 is None else ("h0b", t0 // TT)
                    sg = sgs[vg_i[0] % 2]
                    sk = "sg%d" % (vg_i[0] % 2)
                    vg_i[0] += 1
                    bv, kbv = bank()
                    bg, kbg = bank()
                    mm_group(bv[:, 0:n], kbv, [(wv(kc, 0, 128), rhs(kc), [kv_, rk]) for kc in range(8)])
                    mm_group(bg[:, 0:n], kbg, [(wg(kc, 0, 128), rhs(kc), [kg_, rk]) for kc in range(8)])
                    P.op("act", lambda E, bg=bg, n=n, sg=sg: E.activation(out=sg[:, 0:n], in_=bg[:, 0:n], func=AF.Sigmoid), reads=[kbg], writes=[sk])
                    if t0 is None:
                        P.op("dve", lambda E, bv=bv, q=q, sg=sg: E.scalar_tensor_tensor(out=u[:, q, 0:32], in0=sg[:, 0:32], scalar=mk[:, 0:1], in1=bv[:, 0:32],
                                                                                    op0=ALU.mult, op1=ALU.mult), reads=[kbv, sk, "mk"], writes=[("u", q)])
                    else:
                        P.op("dve", lambda E, bv=bv, t0=t0, q=q, sg=sg: E.tensor_tensor(out=u[:, q, 32 + t0:32 + t0 + TT], in0=bv[:, :], in1=sg, op=ALU.mult),
                             reads=[kbv, sk], writes=[("u", q)])
                dg = dgs[q % 2]
                for k in range(31):
                    P.op("dve", lambda E, k=k, q=q, dg=dg: E.tensor_scalar(out=dg[:, k, :], in0=idf, scalar1=dw[:, q, k:k + 1], scalar2=None, op0=ALU.mult),
                         reads=["idf", "dw"], writes=["dg%d" % (q % 2)])

            def CONV(q, after_n=None):
                dg = dgs[q % 2]
                dk = "dg%d" % (q % 2)
                for n in range(4):
                    if after_n is not None and n >= 1:
                        after_n(n - 1)
                    csq = csqs[n % 2]
                    ck = "csq%d" % (n % 2)
                    b, kb = bank()
                    mm_group(b[:, :], kb, [(dg[:, k, :], u[:, q, 2 + n * TT + k: 2 + n * TT + k + TT], [dk, ("u", q)]) for k in range(31)])
                    P.op("act", lambda E, b=b, q=q, n=n: E.activation(out=c2[:, q, n * TT:(n + 1) * TT], in_=b[:, :], func=AF.Identity, bias=vec[:, 10, q:q + 1], scale=1.0),
                         reads=[kb, "vec"], writes=[("c2", n, q)])
                    P.op("act", lambda E, b=b, q=q, csq=csq: E.activation(out=csq, in_=b[:, :], func=AF.Square, bias=vec[:, 10, q:q + 1], scale=1.0),
                         reads=[kb, "vec"], writes=[ck])
                    b1, k1 = bank()
                    b2, k2 = bank()
                    mm_group(b1[:, :], k1, [(onesM, c2[:, q, n * TT:(n + 1) * TT], ["cst", ("c2", n, q)])])
                    mm_group(b2[:, :], k2, [(onesM, csq, ["cst", ck])])
                    if q == 0:
                        P.op("dve", lambda E, b1=b1, n=n: E.tensor_copy(out=acc1[:, n, :], in_=b1[:, :]), reads=[k1], writes=[("acc1", n)])
                        P.op("dve", lambda E, b2=b2, n=n: E.tensor_copy(out=acc2[:, n, :], in_=b2[:, :]), reads=[k2], writes=[("acc2", n)])
                    else:
                        P.op("dve", lambda E, b1=b1, n=n: E.tensor_tensor(out=acc1[:, n, :], in0=b1[:, :], in1=acc1[:, n, :], op=ALU.add), reads=[k1, ("acc1", n)], writes=[("acc1", n)])
                        P.op("dve", lambda E, b2=b2, n=n: E.tensor_tensor(out=acc2[:, n, :], in0=b2[:, :], in1=acc2[:, n, :], op=ALU.add), reads=[k2, ("acc2", n)], writes=[("acc2", n)])
                if after_n is not None:
                    after_n(3)

            def apply_n(n):
                a1, a2 = acc1[:, n, :], acc2[:, n, :]
                Kc = [("acc1", n), ("acc2", n)]
                pr, kpr = bank()
                pm, kpm = bank()
                P.op("act", lambda E, a1=a1: E.activation(out=cf, in_=a1, func=AF.Square), reads=Kc, writes=["cf0"])
                P.op("dve", lambda E, a2=a2: E.tensor_tensor(out=a2, in0=a2, in1=cf, op=ALU.subtract), reads=Kc + ["cf0"], writes=Kc)
                P.op("act", lambda E, a2=a2: E.activation(out=a2, in_=a2, func=AF.Ln, bias=epst[:, 0:1], scale=1.0), reads=Kc + ["eps"], writes=Kc)
                P.op("act", lambda E, a2=a2: E.activation(out=a2, in_=a2, func=AF.Exp, scale=-0.5), reads=Kc, writes=Kc)
                P.op("dve", lambda E, a2=a2, pr=pr: E.tensor_copy(out=pr[:, :], in_=a2), reads=Kc, writes=[kpr])
                P.op("dve", lambda E, a1=a1, a2=a2, pm=pm: E.tensor_tensor(out=pm[:, :], in0=a1, in1=a2, op=ALU.mult), reads=Kc, writes=[kpm])
                for q in range(8):
                    cs = c2[:, q, n * TT:(n + 1) * TT]
                    cfq = cfs[q % 2]
                    ck = "cf%d" % (q % 2)
                    P.op("dve", lambda E, cs=cs, pr=pr, cfq=cfq: E.tensor_tensor(out=cfq, in0=cs, in1=pr[:, :], op=ALU.mult), reads=[kpr, ("c2", n, q)], writes=[ck])
                    P.op("dve", lambda E, pm=pm, cfq=cfq: E.tensor_tensor(out=cfq, in0=cfq, in1=pm[:, :], op=ALU.subtract), reads=[kpm, ck], writes=[ck])
                    P.op("act", lambda E, cs=cs, cfq=cfq, q=q: E.activation(out=cs, in_=cfq, func=AF.Silu, bias=vec[:, 3, q:q + 1], scale=vec[:, 2, q:q + 1]),
                         reads=[ck, "vec"], writes=[("c2", n, q)])

            VG(0)
            for q in range(8):
                if q + 1 < 8:
                    VG(q + 1)
                CONV(q, after_n=(apply_n if q == 7 else None))
            if DBG_BAR:
                P.barrier()
            wext = sb([128, 4 * 1024], BF16, sM)
            ring_bufs.extend([V(wext, i * 1024, [[1, 1024]]) for i in range(4)])
            sA = ra([128, 512], F32); sBt = ra([128, 512], F32); sC = ra([128, 512], F32)
            tA = ra([128, 512], F32); tB = ra([128, 512], F32)
            for m in range(8):
                wga, kga = wload(w_in, 0, 8, 2560 + m * 128)
                wgb, kgb = wload(w_in, 0, 8, 3584 + m * 128)
                wco, kco = wload(w_conv_out, 0, 8, m * 128)
                wzv, kzv = wload(w_glu, 0, 4, m * 128)
                wzg, kzg = wload(w_glu, 0, 4, 1024 + m * 128)
                for n in range(4):
                    ts = slice(n * TT, (n + 1) * TT)
                    b1, k1 = bank(); b2, k2 = bank(); b3, k3 = bank(); b4, k4 = bank(); b5, k5 = bank()
                    mm_group(b1[:, :], k1, [(wga(kc, 0, 128), h0b[:, kc, ts], [kga, ("h0b", n)]) for kc in range(8)])
                    mm_group(b2[:, :], k2, [(wco(kc, 0, 128), c2[:, kc, ts], [kco, ("c2", n, kc)]) for kc in range(8)])
                    mm_group(b3[:, :], k3, [(wgb(kc, 0, 128), h0b[:, kc, ts], [kgb, ("h0b", n)]) for kc in range(8)])
                    mm_group(b4[:, :], k4, [(wzv(kc, 0, 128), yT[:, kc, ts], [kzv, "yT"]) for kc in range(4)])
                    mm_group(b5[:, :], k5, [(wzg(kc, 0, 128), yT[:, kc, ts], [kzg, "yT"]) for kc in range(4)])
                    P.op("act", lambda E, b1=b1: E.activation(out=sA, in_=b1[:, :], func=AF.Sigmoid), reads=[k1], writes=["sA"])
                    P.op("act", lambda E, b3=b3: E.activation(out=sBt, in_=b3[:, :], func=AF.Sigmoid), reads=[k3], writes=["sB"])
                    P.op("act", lambda E, b5=b5: E.activation(out=sC, in_=b5[:, :], func=AF.Sigmoid), reads=[k5], writes=["sC"])
                    P.op("dve", lambda E, b2=b2: E.tensor_tensor(out=tA, in0=b2[:, :], in1=sA, op=ALU.mult), reads=[k2, "sA"], writes=["tA"])
                    P.op("dve", lambda E, b4=b4: E.tensor_tensor(out=tB, in0=b4[:, :], in1=sC, op=ALU.mult), reads=[k4, "sC"], writes=["tB"])
                    P.op("dve", lambda E: E.tensor_tensor(out=tB, in0=tB, in1=sBt, op=ALU.mult), reads=["tB", "sB"], writes=["tB"])
                    P.op("dve", lambda E, m=m, ts=ts: E.tensor_tensor(out=mg[:, m, ts], in0=tA, in1=tB, op=ALU.add), reads=["tA", "tB"], writes=["mg"] + MCK)
            del ring_bufs[NW:]
            P.barrier()
        hb = sb([128, 8, NT], BF16)
        wres = sb([128, 8, 1024], BF16)

        def wres_load(w_ap, r0):
            P.dma("pool", wres[:], w_ap[r0:r0 + 1024, :].rearrange("(kc p) n -> p kc n", p=128), writes=["wres", "wres7"])

        def proj_ln(emit_mm, gi, write_out=False, S=None, f32_affine=None):
            rsrc = lambda n: (lambda c: r[:, c, n * TT:(n + 1) * TT])
            rtile = lambda n: r[:, :, n * TT:(n + 1) * TT]
            rk = lambda n: (lambda c: ("r", n, c))
            A_ = lambda n: ln_front_a(rsrc(n), rk(n), TT, rtile(n), S)
            fr = {}
            split_banks(True)
            if emit_mm:
                emit_mm(0)
                emit_mm(1)
            fr[0] = A_(0)
            ln_front_b(fr[0], TT, S)
            for n in range(4):
                ts = slice(n * TT, (n + 1) * TT)
                if emit_mm and n + 2 < 4:
                    emit_mm(n + 2)
                if n + 1 < 4:
                    fr[n + 1] = A_(n + 1)
                ln_back(fr[n], rsrc(n), rk(n), TT, gi, out_f32=rsrc(n),
                        out_bf=(None if write_out else (lambda c, ts=ts: hb[:, c, ts])), okeys=rk(n), bfkeys=[("hb", n)],
                        inter=(ln_front_b_steps(fr[n + 1], TT, S) if n + 1 < 4 else None), S=S, f32_affine=f32_affine)
                if write_out:
                    P.dma("sp", outT[:, n * TT:(n + 1) * TT].rearrange("(c p) t -> p c t", p=128), r[:, :, ts], reads=[("r", n, c) for c in range(8)], writes=["outT"], is_output=True)
            split_banks(False)

        wres_load(w_mix, 0)
        hx_i = [0]

        def mix_mm(n):
            ts = slice(n * TT, (n + 1) * TT)
            for m in range(8):
                hxi = hx[hx_i[0] % 2]
                hk = ("hx", hx_i[0] % 2)
                hx_i[0] += 1
                P.dma("sp", hxi[:], h0s[m * 128:(m + 1) * 128, n * TT:(n + 1) * TT], reads=["h0s"], writes=[hk])
                b, k = bank()
                mm_group(b[:, :], k, [(wres[:, kc, m * 128:(m + 1) * 128], mg[:, kc, ts], ["wres", "wres7", "mg"]) for kc in range(8)])
                P.op("dve", lambda E, b=b, m=m, ts=ts, hxi=hxi: E.scalar_tensor_tensor(out=r[:, m, ts], in0=hxi[:], scalar=ALPHA, in1=b[:, :],
                                                                                    op0=ALU.mult, op1=ALU.add), reads=[k, hk], writes=[("r", n, m)])

        mT = sb([128, 8, 256], BF16)
        P.dma("pool", mT[:], memT.rearrange("(c p) t -> p c t", p=128), writes=["mT"])
        KT = sb([128, 8, 256], BF16)
        Vt = sb([128, 2, D], BF16)
        wvt = V(wringT, 0, [[512, 8], [1, 512]])
        wvk = [("w", i) for i in range(4)]

        def kv_proj():
            for m in range(8):
                wk_, kk_ = wload(wkv, 0, 8, m * 128)
                b, k = bank()
                mm_group(b[:, 0:256], k, [(wk_(kc, 0, 128), mT[:, kc, :], [kk_, "mT"]) for kc in range(8)])
                P.op("act", lambda E, b=b, m=m: E.activation(out=KT[:, m, :], in_=b[:, 0:256], func=AF.Copy), reads=[k], writes=["KT"])
            for db_ in range(2):
                P.dma("pool", wvt, wkv[:, 1024 + db_ * 512:1024 + (db_ + 1) * 512].rearrange("(kc p) n -> p kc n", p=128), writes=wvk)
                for mc in range(2):
                    b, k = bank()
                    mm_group(b[:, :], k, [(mT[:, kc, mc * 128:(mc + 1) * 128], wvt[:, kc, :], wvk + ["mT"]) for kc in range(8)])
                    P.op("act", lambda E, b=b, mc=mc, db_=db_: E.activation(out=Vt[:, mc, db_ * 512:(db_ + 1) * 512], in_=b[:, :], func=AF.Copy), reads=[k], writes=["Vt"])

        with ExitStack() as sL:
            hx = [sb([128, 512], F32, sL) for _ in range(2)]
            S1 = dict(xb=sb([128, 8, 512], BF16, sL), sq=sb([128, 8, 512], BF16, sL), m2=sb([128, 512], F32, sL), rs=sb([128, 512], F32, sL),
                      tmps=[sb([128, 512], F32, sL) for _ in range(2)], tag="B")
            kv_done = [False]

            def mix_mm2(n):
                if n == 0 and not kv_done[0]:
                    kv_done[0] = True
                    kv_proj()
                mix_mm(n)

            proj_ln(mix_mm2, 4, S=S1)
            P.dma("pool", wres[:, 0:7, :], wo[0:896, :].rearrange("(kc p) n -> p kc n", p=128), writes=["wres"])
            P.barrier()

        with ExitStack() as s1:
            oT = sb([128, 8, NT], BF16, s1)
            qh = sb([128, 1, 2, TT], BF16, s1)
            PT = sb([128, 2, 512], BF16, s1)
            rden = sb([128, 512], F32, s1)
            qh2 = V(wres, 7 * 1024, [[512, 2], [1, 512]])
            qbuf = [lambda dc: qh[:, 0, dc, :], lambda dc: qh2[:, dc, :]]
            items = [(h, n) for h in range(4) for n in range(4)]
            wqs_h = {}

            def stage_q(i):
                h, n = items[i]
                if h not in wqs_h:
                    wqs_h[h] = [wload(wq, 0, 8, (2 * h + dc) * 128) for dc in range(2)]
                ts = slice(n * TT, (n + 1) * TT)
                qb = i % 2
                for dc in range(2):
                    wq_, kq_ = wqs_h[h][dc]
                    b, k = bank()
                    mm_group(b[:, :], k, [(wq_(kc, 0, 128), hb[:, kc, ts], [kq_, ("hb", n)]) for kc in range(8)])
                    dst = qbuf[qb](dc)
                    P.op("act", lambda E, b=b, dst=dst: E.activation(out=dst, in_=b[:, :], func=AF.Copy), reads=[k], writes=[("qh", qb, dc)])

            def stage_att(i):
                h, n = items[i]
                ts = slice(n * TT, (n + 1) * TT)
                qb = i % 2
                for mc in range(2):
                    b, k = bank()
                    mm_group(b[:, :], k, [(KT[:, 2 * h + dc, mc * 128:(mc + 1) * 128], qbuf[qb](dc), ["KT", ("qh", qb, dc)]) for dc in range(2)])
                    P.op("act", lambda E, b=b, mc=mc: E.activation(out=PT[:, mc, :], in_=b[:, :], func=AF.Exp, scale=1.0 / 16.0), reads=[k], writes=[("PT", mc)])
                bd, kd = bank()
                mm_group(bd[:, :], kd, [(ones1, PT[:, mc, :], ["cst", ("PT", mc)]) for mc in range(2)])
                P.op("act", lambda E, bd=bd: E.activation(out=rden[:], in_=bd[:, :], func=AF.Ln), reads=[kd], writes=["rden"])
                P.op("act", lambda E: E.activation(out=rden[:], in_=rden[:], func=AF.Exp, scale=-1.0), reads=["rden"], writes=["rden"])
                for dc in range(2):
                    b, k = bank()
                    mm_group(b[:, :], k, [(Vt[:, mc, (2 * h + dc) * 128:(2 * h + dc + 1) * 128], PT[:, mc, :], ["Vt", ("PT", mc)]) for mc in range(2)])
                    P.op("dve", lambda E, b=b, h=h, dc=dc, ts=ts: E.tensor_tensor(out=oT[:, 2 * h + dc, ts], in0=b[:, :], in1=rden[:], op=ALU.mult),
                         reads=[k, "rden"], writes=["oT"])

            stage_q(0)
            for i in range(len(items)):
                if i + 1 < len(items):
                    stage_q(i + 1)
                stage_att(i)
            P.dma("pool", wres[:, 7:8, :], wo[896:1024, :].rearrange("(kc p) n -> p kc n", p=128), writes=["wres7", ("qh", 1, 0), ("qh", 1, 1)])

            def wo_mm(n):
                ts = slice(n * TT, (n + 1) * TT)
                for m in range(8):
                    b, k = bank()
                    mm_group(b[:, :], k, [(wres[:, kc, m * 128:(m + 1) * 128], oT[:, kc, ts], ["wres", "wres7", "oT"]) for kc in range(8)])
                    P.op("dve", lambda E, b=b, m=m, ts=ts: E.scalar_tensor_tensor(out=r[:, m, ts], in0=r[:, m, ts], scalar=ALPHA, in1=b[:, :],
                                                                               op0=ALU.mult, op1=ALU.add), reads=[k, ("r", n, m)], writes=[("r", n, m)])

            proj_ln(wo_mm, 6, f32_affine=vecA)
            P.barrier()

        with ExitStack() as s1:
            z = sb([128, 8, NT], BF16, s1)
            rl = sb([128, 512], F32, s1)
            wres_load(w_down, 3 * 1024)
            for hbk in range(4):
                for mm in range(8):
                    wu, ku = wload(w_up, 0, 8, hbk * 1024 + mm * 128)
                    for n in range(4):
                        ts = slice(n * TT, (n + 1) * TT)
                        b, k = bank()
                        mm_group(b[:, :], k, [(wu(kc, 0, 128), hb[:, kc, ts], [ku, ("hb", n)]) for kc in range(8)])
                        P.op("act", lambda E, b=b: E.activation(out=rl[:], in_=b[:, :], func=AF.Relu), reads=[k], writes=["rl"])
                        P.op("dve", lambda E, b=b, mm=mm, ts=ts: E.tensor_tensor(out=z[:, mm, ts], in0=b[:, :], in1=rl[:], op=ALU.mult), reads=[k, "rl"], writes=["z"])
                if hbk == 3:
                    break
                for m in range(8):
                    wd, kd = wload(w_down, hbk * 1024, 8, m * 128)
                    for n in range(4):
                        ts = slice(n * TT, (n + 1) * TT)
                        b, k = bank()
                        mm_group(b[:, :], k, [(wd(kc, 0, 128), z[:, kc, ts], [kd, "z"]) for kc in range(8)])
                        P.op("dve", lambda E, b=b, m=m, ts=ts: E.tensor_tensor(out=r[:, m, ts], in0=b[:, :], in1=r[:, m, ts], op=ALU.add), reads=[k, ("r", n, m)], writes=[("r", n, m)])

            def down_mm(n):
                ts = slice(n * TT, (n + 1) * TT)
                for m in range(8):
                    b, k = bank()
                    mm_group(b[:, :], k, [(wres[:, kc, m * 128:(m + 1) * 128], z[:, kc, ts], ["wres", "wres7", "z"]) for kc in range(8)])
                    P.op("dve", lambda E, b=b, m=m, ts=ts: E.tensor_tensor(out=r[:, m, ts], in0=b[:, :], in1=r[:, m, ts], op=ALU.add), reads=[k, ("r", n, m)], writes=[("r", n, m)])

            proj_ln(down_mm, 8, write_out=True)
            P.barrier()
        P.finish()
        with nc.Block() as block:
            P.run(block)
    return nc


_NC = [None]


def kernel(**inp):
    f = lambda a: np.ascontiguousarray(np.asarray(a, dtype=np.float32))
    x = f(inp["x"]); mem = f(inp["mem"])
    fm = lambda v: f(v).reshape(8, 128).T
    vecs = np.zeros((128, 12, 8), np.float32)
    for i, (gk, bk) in enumerate([("in_norm_g", "in_norm_b"), ("conv_norm_g", "conv_norm_b"), ("ln1_g", "ln1_b"), ("ln2_g", "ln2_b"), ("ln3_g", "ln3_b")]):
        vecs[:, 2 * i] = fm(np.asarray(inp[gk]).reshape(-1))
        vecs[:, 2 * i + 1] = fm(np.asarray(inp[bk]).reshape(-1))
    vecs[:, 10] = fm(np.asarray(inp["conv_db"]).reshape(-1))
    dwT = f(inp["conv_dw"])[0].T.reshape(8, 128, 31).transpose(1, 0, 2)
    dup = lambda a: np.concatenate([a, a], axis=0)
    lam = np.stack([dup(f(inp["ssm_lambda_re"])[0].T), dup(f(inp["ssm_lambda_im"])[0].T),
                    np.broadcast_to(f(inp["ssm_log_step"])[0][None, :], (128, G))], axis=1)
    Br = dup(f(inp["ssm_b_re"])[0].transpose(1, 0, 2)); Bi = dup(f(inp["ssm_b_im"])[0].transpose(1, 0, 2))
    Cr = dup(f(inp["ssm_c_re"])[0].transpose(2, 0, 1)); Ci = dup(f(inp["ssm_c_im"])[0].transpose(2, 0, 1))
    bc_in = np.stack([Br, Bi, Cr, Ci], axis=1)
    dvec = f(inp["ssm_d"])[0].reshape(4, 128).T
    consts = np.stack([np.full((128, 128), 1.0 / 1024, np.float32), np.ones((128, 128), np.float32), np.eye(128, dtype=np.float32)], axis=1)
    jl = np.arange(128) // 16
    cmask = (jl[None, :] >= jl[:, None]).astype(np.float32)
    selP = np.zeros((128, 64, 128), np.float32)
    selU = np.zeros((128, 64, 128), np.float32)
    for gl in range(8):
        for j in range(8):
            for h in range(16):
                selP[gl * 16 + h, gl * 8 + j, j * 16 + h] = 1.0
                selU[j * 16 + h, j * 8 + gl, gl * 16 + h] = 1.0
    rowc = np.broadcast_to(np.arange(1, 65, dtype=np.float32)[None, :], (128, 64)).copy()
    jmat = np.zeros((128, 128), np.float32)
    for m_ in range(64):
        jmat[m_ + 64, m_] = -1.0
        jmat[m_, m_ + 64] = 1.0
    common = dict(rowc=rowc, jmat=jmat,
        w_in=f(inp["w_in"])[0], w_conv_out=f(inp["w_conv_out"])[0], w_glu=f(inp["w_ssm_glu"])[0], w_mix=f(inp["w_mix_out"])[0],
        wq=f(inp["xa_wq"])[0], wkv=f(inp["xa_wkv"])[0], wo=f(inp["xa_wo"])[0], w_up=f(inp["mlp_w_up"])[0], w_down=f(inp["mlp_w_down"])[0],
        vecs=vecs, dwT=f(dwT), lam=f(lam), bc_in=f(bc_in), dvec=f(dvec), consts=f(consts), cmask=cmask, selP=selP, selU=selU)
    in_maps = []
    for core in range(8):
        b, half = core // 2, core % 2
        xT = np.zeros((D, NA), np.float32)
        xT[:, NT:] = x[b, half * NT:(half + 1) * NT].T
        if half == 1:
            xT[:, :NT] = x[b, 0:NT].T
        m = dict(common)
        m["xT"] = xT
        m["maskv"] = np.full((128, 1), float(half), np.float32)
        m["memT"] = f(mem[b].T)
        in_maps.append(m)
    if _NC[0] is None:
        _NC[0] = build()
    res = run_bass_kernel_spmd(_NC[0], in_maps, core_ids=list(range(8)))
    out = np.zeros((4, 4096, D), np.float32)
    for core in range(8):
        b, half = core // 2, core % 2
        out[b, half * NT:(half + 1) * NT] = res.results[core]["outT"].T
    return out
```
